# Optimizing a Trainium2 kernel written in Bass

```python
import math
import jax, jax.numpy as jnp
from jax import lax
import numpy as np

D_MODEL = 1024
BATCH = 8
SEQ = 8192
DEPTH = 1
DEC_BATCH = 2
DEC_SEQ = 8192
PAST_LEN = 128

D_MIX = D_MODEL
MLA_HEADS = 4
NOPE_DIM = 128
ROPE_DIM = 64
V_DIM = 128
QK_DIM = NOPE_DIM + ROPE_DIM
Q_LORA = 256
KV_LORA = 128
ROPE_THETA = 10000.0
Q_BLOCK = 128
MLA_WIDTH = MLA_HEADS * V_DIM
FOURIER_WIDTH = D_MIX - MLA_WIDTH
F_GROUPS = 8
F_GROUP_DIM = FOURIER_WIDTH // F_GROUPS
D_IN = Q_LORA + KV_LORA + ROPE_DIM + FOURIER_WIDTH
D_FF = int(math.ceil(8 * D_MODEL / 3 / 256) * 256)
N_MOD = 6
EPS = 1e-6

kernel_name = "hybrid_mla_fnet_adaln_encoder"


def rmsnorm(x, g):
    xf = x.astype(jnp.float32)
    y = xf * lax.rsqrt(jnp.mean(xf * xf, axis=-1, keepdims=True) + EPS)
    return (y * g.astype(jnp.float32)).astype(x.dtype)


def rope_tables(seq):
    pos = jnp.arange(seq, dtype=jnp.float32)
    inv = 1.0 / (ROPE_THETA ** (jnp.arange(0, ROPE_DIM, 2, dtype=jnp.float32) / ROPE_DIM))
    ang = pos[:, None] * inv[None, :]
    return jnp.cos(ang), jnp.sin(ang)


def apply_rope(x, cos, sin):
    half = ROPE_DIM // 2
    xf = x.astype(jnp.float32)
    x1, x2 = xf[..., :half], xf[..., half:]
    out = jnp.concatenate([x1 * cos - x2 * sin, x2 * cos + x1 * sin], axis=-1)
    return out.astype(x.dtype)


def mla_group(q_lat, kv_lat, k_pe, g_q_lat, w_uq, g_kv_lat, w_ukv):
    b, s, _ = q_lat.shape
    cos, sin = rope_tables(s)
    q = (rmsnorm(q_lat, g_q_lat) @ w_uq).reshape(b, s, MLA_HEADS, QK_DIM)
    q_nope, q_pe = q[..., :NOPE_DIM], q[..., NOPE_DIM:]
    q_pe = apply_rope(q_pe, cos[:, None, :], sin[:, None, :])
    kv = (rmsnorm(kv_lat, g_kv_lat) @ w_ukv).reshape(b, s, MLA_HEADS, NOPE_DIM + V_DIM)
    k_nope, v = kv[..., :NOPE_DIM], kv[..., NOPE_DIM:]
    k_pe = apply_rope(k_pe, cos, sin)
    scale = 1.0 / math.sqrt(QK_DIM)
    nb = s // Q_BLOCK
    qn = q_nope.reshape(b, nb, Q_BLOCK, MLA_HEADS, NOPE_DIM).transpose(1, 0, 2, 3, 4)
    qp = q_pe.reshape(b, nb, Q_BLOCK, MLA_HEADS, ROPE_DIM).transpose(1, 0, 2, 3, 4)

    def attend(args):
        qn_b, qp_b = args
        sc = (jnp.einsum('bqhd,bkhd->bhqk', qn_b, k_nope)
              + jnp.einsum('bqhr,bkr->bhqk', qp_b, k_pe)).astype(jnp.float32) * scale
        p = jax.nn.softmax(sc, axis=-1).astype(v.dtype)
        return jnp.einsum('bhqk,bkhd->bqhd', p, v)

    o = lax.map(attend, (qn, qp))
    return o.transpose(1, 0, 2, 3, 4).reshape(b, s, MLA_WIDTH)


def fourier_group(f_in, w_four):
    b, s, _ = f_in.shape
    f = f_in.reshape(b, s, F_GROUPS, F_GROUP_DIM).astype(jnp.float32)
    fr = jnp.real(jnp.fft.fft2(f, axes=(1, 3), norm='ortho')).astype(f_in.dtype)
    out = jnp.einsum('bsgc,gcd->bsgd', fr, w_four)
    return out.reshape(b, s, FOURIER_WIDTH)


def layer(x, c, w_ada, b_ada, g_mix, w_in, g_q_lat, w_uq, g_kv_lat, w_ukv, w_four,
          w_out, g_ffn, w_gate, w_up, w_down):
    mod = jax.nn.silu(c) @ w_ada + b_ada
    sh1, sc1, ga1, sh2, sc2, ga2 = [m[:, None, :] for m in jnp.split(mod, N_MOD, axis=-1)]
    h = rmsnorm(x, g_mix) * (1.0 + sc1) + sh1
    z = h @ w_in
    o1 = Q_LORA
    o2 = o1 + KV_LORA
    o3 = o2 + ROPE_DIM
    q_lat, kv_lat, k_pe, f_in = z[..., :o1], z[..., o1:o2], z[..., o2:o3], z[..., o3:]
    y_mla = mla_group(q_lat, kv_lat, k_pe, g_q_lat, w_uq, g_kv_lat, w_ukv)
    y_four = fourier_group(f_in, w_four)
    mix = jnp.concatenate([y_mla, y_four], axis=-1) @ w_out
    x = x + ga1 * mix
    h2 = rmsnorm(x, g_ffn) * (1.0 + sc2) + sh2
    ffn = (jax.nn.silu(h2 @ w_gate) * (h2 @ w_up)) @ w_down
    return x + ga2 * ffn


def setup_inputs(seed: int = 0) -> dict:
    key = jax.random.key(seed)
    ks = jax.random.split(key, 24)
    f32 = jnp.float32

    def nrm(k, shape, fan_in, mult=1.0):
        return jax.random.normal(k, shape, f32) * (mult * fan_in ** -0.5)

    def gain(k, shape):
        return 1.0 + 0.05 * jax.random.normal(k, shape, f32)

    return {
        "x_prompt": jax.random.normal(ks[0], (BATCH, SEQ, D_MODEL), f32),
        "x_sample": jax.random.normal(ks[1], (DEC_BATCH, DEC_SEQ, D_MODEL), f32),
        "c_prompt": jax.random.normal(ks[2], (BATCH, D_MODEL), f32),
        "c_sample": jax.random.normal(ks[3], (DEC_BATCH, D_MODEL), f32),
        "w_ada": nrm(ks[4], (DEPTH, D_MODEL, N_MOD * D_MODEL), D_MODEL, 0.5),
        "b_ada": 0.02 * jax.random.normal(ks[5], (DEPTH, N_MOD * D_MODEL), f32),
        "g_mix": gain(ks[6], (DEPTH, D_MODEL)),
        "w_in": nrm(ks[7], (DEPTH, D_MODEL, D_IN), D_MODEL),
        "g_q_lat": gain(ks[8], (DEPTH, Q_LORA)),
        "w_uq": nrm(ks[9], (DEPTH, Q_LORA, MLA_HEADS * QK_DIM), Q_LORA),
        "g_kv_lat": gain(ks[10], (DEPTH, KV_LORA)),
        "w_ukv": nrm(ks[11], (DEPTH, KV_LORA, MLA_HEADS * (NOPE_DIM + V_DIM)), KV_LORA),
        "w_four": nrm(ks[12], (DEPTH, F_GROUPS, F_GROUP_DIM, F_GROUP_DIM), F_GROUP_DIM),
        "w_out": nrm(ks[13], (DEPTH, D_MIX, D_MODEL), D_MIX),
        "g_ffn": gain(ks[14], (DEPTH, D_MODEL)),
        "w_gate": nrm(ks[15], (DEPTH, D_MODEL, D_FF), D_MODEL),
        "w_up": nrm(ks[16], (DEPTH, D_MODEL, D_FF), D_MODEL),
        "w_down": nrm(ks[17], (DEPTH, D_FF, D_MODEL), D_FF),
        "g_final": gain(ks[18], (D_MODEL,)),
    }


def reference(x_prompt, x_sample, c_prompt, c_sample, w_ada, b_ada, g_mix, w_in, g_q_lat,
              w_uq, g_kv_lat, w_ukv, w_four, w_out, g_ffn, w_gate, w_up, w_down, g_final):
    def trunk(x, c):
        for l in range(DEPTH):
            x = layer(x, c, w_ada[l], b_ada[l], g_mix[l], w_in[l], g_q_lat[l], w_uq[l],
                      g_kv_lat[l], w_ukv[l], w_four[l], w_out[l], g_ffn[l], w_gate[l],
                      w_up[l], w_down[l])
        return rmsnorm(x, g_final)

    y_prompt = trunk(x_prompt, c_prompt)
    y_sample = trunk(x_sample, c_sample)
    return (y_prompt, y_sample)
```

```python
import math
from contextlib import ExitStack

import numpy as np
import concourse.bass as bass
import concourse.mybir as mybir
from concourse.bass_utils import run_bass_kernel_spmd

F32 = mybir.dt.float32
BF16 = mybir.dt.bfloat16
AF = mybir.ActivationFunctionType
ALU = mybir.AluOpType

D = 1024
KC = 8
DIN = 960
DFF = 2816
FC = DFF // 128
EPS = 1e-6
SCALE = 1.0 / math.sqrt(192.0)
N_CORES = 8
import os
_PH = os.environ.get("K_PHASES")
PHASES = None if not _PH else set(int(v) for v in _PH.split(","))
MAXOPS = int(os.environ.get("K_MAXOPS", "0"))


class Res:
    __slots__ = ("name", "writer", "readers")

    def __init__(self, name=""):
        self.name = name
        self.writer = None
        self.readers = []


class Buf:
    __slots__ = ("h", "r")

    def __init__(self, h, name=""):
        self.h = h
        self.r = Res(name)


class Op:
    __slots__ = ("eng", "fn", "deps", "signaled", "cnt", "sem", "is_dma", "ndma", "ph")

    def __init__(self, eng, fn):
        self.eng = eng
        self.fn = fn
        self.deps = []
        self.signaled = False
        self.cnt = 0
        self.sem = None
        self.is_dma = False
        self.ndma = 0
        self.ph = None


ENGS = ("pe", "act", "dve", "pool", "sp")
HANDLES = {"pe": "tensor", "act": "scalar", "dve": "vector", "pool": "gpsimd", "sp": "sync"}


class SemTable:
    def __init__(self, nc, stack):
        self.nc = nc
        self.stack = stack
        self.eng = {e: [stack.enter_context(nc.semaphore("s_" + e)), 0] for e in ENGS if e != "sp"}
        self.dma = {}

    def dma_sem(self, key):
        if key not in self.dma:
            self.dma[key] = [self.stack.enter_context(self.nc.semaphore("d_%d" % len(self.dma))), 0]
        return self.dma[key]


class Sched:
    def __init__(self, nc, sems):
        self.nc = nc
        self.sems = sems
        self.ops = {e: [] for e in ENGS}

    def _deps(self, op, reads, writes):
        deps = []
        for r in reads:
            if r.writer is not None:
                deps.append((0, r.writer))
        for w in writes:
            if w.writer is not None:
                deps.append((1, w.writer))
            for rd in w.readers:
                deps.append((2, rd))
        out = []
        seen = set()
        for kind, d in deps:
            if d is op or id(d) in seen or d.ph is not op.ph:
                continue
            if (not d.is_dma) and (not op.is_dma) and d.eng == op.eng:
                if op.eng == "pe" or kind != 0:
                    continue
            seen.add(id(d))
            d.signaled = True
            out.append(d)
        op.deps = out
        for r in reads:
            if not op.is_dma:
                r.readers = [x for x in r.readers if x.is_dma or x.eng != op.eng or x.ph is not op.ph]
            r.readers.append(op)
        for w in writes:
            w.writer = op
            w.readers = []

    def add(self, eng, fn, reads=(), writes=()):
        self.nrec = getattr(self, "nrec", 0) + 1
        if MAXOPS and self.nrec > MAXOPS:
            return None
        op = Op(eng, fn)
        op.ph = self
        self._deps(op, [b.r for b in reads], [b.r for b in writes])
        self.ops[eng].append(op)
        return op

    def dma(self, eng, fns, key, reads=(), writes=()):
        self.nrec = getattr(self, "nrec", 0) + 1
        if MAXOPS and self.nrec > MAXOPS:
            return None
        op = Op(eng, fns)
        op.is_dma = True
        op.ndma = len(fns)
        op.sem = key
        op.ph = self
        self._deps(op, [b.r for b in reads], [b.r for b in writes])
        self.ops[eng].append(op)
        return op

    def emit(self, phase=None):
        nc = self.nc
        if PHASES is not None and phase not in PHASES:
            return
        for e in ENGS:
            for op in self.ops[e]:
                if op.is_dma:
                    ent = self.sems.dma_sem(op.sem)
                    ent[1] += 16 * op.ndma
                    op.cnt = ent[1]
                    op.sem = ent[0]
                elif op.signaled:
                    ent = self.sems.eng[e]
                    ent[1] += 1
                    op.cnt = ent[1]
                    op.sem = ent[0]

        def run(e, h):
            waited = {}
            for op in self.ops[e]:
                for d in op.deps:
                    key = id(d.sem)
                    if waited.get(key, 0) >= d.cnt:
                        continue
                    waited[key] = d.cnt
                    if os.environ.get("K_TRACE"):
                        print("TRACE", e, "wait", d.sem.name if hasattr(d.sem, "name") else d.sem, d.cnt, "dep_eng", d.eng, "dma" if d.is_dma else "")
                    h.wait_ge(d.sem, d.cnt)
                if os.environ.get("K_TRACE"):
                    print("TRACE", e, "op", "dma" if op.is_dma else "", "sig" if op.signaled else "", op.cnt)
                if op.is_dma:
                    for f in op.fn:
                        f(h).then_inc(op.sem, 16)
                else:
                    ins = op.fn(h)
                    if op.signaled:
                        ins.then_inc(op.sem, 1)
            done = {}
            for op in self.ops[e]:
                if op.is_dma:
                    done[id(op.sem)] = (op.sem, op.cnt)
            for s, v in done.values():
                h.wait_ge(s, v)

        with nc.Block() as block:
            for e in ENGS:
                if self.ops[e]:
                    getattr(block, HANDLES[e])(lambda h, e=e: run(e, h))


class Cfg:
    def __init__(self, S):
        self.S = S
        self.N2 = S // 128
        self.PC = 128 // self.N2
        self.NT = S // 128
        self.QS = S // 4
        self.NQ2 = self.N2 // 4
        assert self.N2 * self.PC == 128 and self.QS % 512 == 0


def build_program(cfg):
    S, N2, PC, NT, QS = cfg.S, cfg.N2, cfg.PC, cfg.NT, cfg.QS
    NG = 128 // PC
    nc = bass.Bass("TRN2", target_bir_lowering=False)

    def din(name, shape, dt=F32):
        return nc.dram_tensor(name, list(shape), dt, kind="ExternalInput").ap()

    xs = [din("xs_p", [S, D]), din("xs_s", [S, D])]
    xm_s = din("xm_s", [QS, D])
    cT = din("cT", [128, KC, 2])
    w_ada = din("w_ada", [D, 6 * D])
    b_ada = din("b_ada", [1, 6 * D])
    g_mix = din("g_mix", [1, D])
    g_ffn = din("g_ffn", [1, D])
    g_final = din("g_final", [1, D])
    w_in = din("w_in", [D, DIN])
    gqT = din("gqT", [128, 2])
    w_uq = din("w_uq", [256, 768])
    gkvT = din("gkvT", [128, 1])
    w_ukv = din("w_ukv", [128, 1024])
    w_four = din("w_four", [8, 64, 64])
    w_out = din("w_out", [D, D])
    w_gate = din("w_gate", [D, DFF])
    w_up = din("w_up", [D, DFF])
    w_down = din("w_down", [DFF, D])
    rope_c = din("rope_c", [64, S])
    rope_s = din("rope_s", [64, S])
    rope_cm = din("rope_cm", [64, QS])
    rope_sm = din("rope_sm", [64, QS])
    cb_d = din("cb", [128, 128])
    sb_d = din("sb", [128, 128])
    t1u_d = din("t1u", [128, 256])
    t1v_d = din("t1v", [128, 256])
    ta_d = din("ta", [128, 512])
    ts_d = din("ts", [128, 256])
    bc_d = [din("bc_p", [128, 128]), din("bc_s", [128, 32])]
    bs_d = [din("bs_p", [128, 128]), din("bs_s", [128, 32])]
    y_p = nc.dram_tensor("y_p", [S, D], F32, kind="ExternalOutput").ap()
    y_s = nc.dram_tensor("y_s", [QS, D], F32, kind="ExternalOutput").ap()
    yf = [nc.dram_tensor("yf_p", [S, 512], BF16).ap(), nc.dram_tensor("yf_s", [QS, 512], BF16).ap()]
    x1scr = nc.dram_tensor("x1scr", [S + QS, D], F32).ap()
    modscr = nc.dram_tensor("modscr", [2, 6, D], F32).ap()
    yf_res = [[Res() for _ in range(S // 128)], [Res() for _ in range(QS // 128)]]
    x1_res = [Res() for _ in range((S + QS) // 128)]

    with ExitStack() as top:
        top.enter_context(nc.allow_low_precision("bf16 matmul operands, fp32 accumulation"))
        sems = SemTable(nc, top)

        uid = [0]

        def sb(st, name, shape, dt):
            uid[0] += 1
            return st.enter_context(nc.sbuf_tensor("%s_%d" % (name, uid[0]), list(shape), dt))

        def bufs(st, name, shape, dt, n):
            return [Buf(sb(st, "%s%d" % (name, i), shape, dt)) for i in range(n)]

        def psum_banks(st, nf=6):
            uid[0] += 1
            banks = [Buf(st.enter_context(nc.psum_tensor("pb%d_%d" % (i, uid[0]), [128, 512], F32))) for i in range(nf)]
            ptr = [Buf(st.enter_context(nc.psum_tensor("ptr%d_%d" % (i, uid[0]), [128, 1024], BF16))) for i in range(8 - nf)]
            if len(ptr) == 1:
                ptr = [ptr[0], ptr[0]]
            return banks, ptr

        class Wrap:
            def __init__(self, r):
                self.r = r

        ident = Buf(sb(top, "ident", [128, 128], BF16))
        ones_bf = Buf(sb(top, "ones_bf", [128, 128], BF16))
        epsb = Buf(sb(top, "epsb", [128, 1], F32))

        with ExitStack() as w1:
            win = Buf(sb(w1, "win", [128, KC, 1024], BF16))
            wuq = Buf(sb(w1, "wuq", [128, 2, 1024], BF16))
            wukv = Buf(sb(w1, "wukv", [128, 1024], BF16))
            wukT = Buf(sb(w1, "wukT", [128, 4, 128], BF16))
            gkv = Buf(sb(w1, "gkv", [128, 1], F32))
            abh = Buf(sb(w1, "abh", [128, 8, 128], BF16))
            t1u = Buf(sb(w1, "t1u", [128, 256], BF16))
            t1v = Buf(sb(w1, "t1v", [128, 256], BF16))
            ta = Buf(sb(w1, "ta", [128, 512], F32))
            tsn = Buf(sb(w1, "tsn", [128, 256], F32))
            bcs = [Buf(sb(w1, "bc0", [128, 128], BF16)), Buf(sb(w1, "bc1", [128, 32], BF16))]
            bss = [Buf(sb(w1, "bs0", [128, 128], BF16)), Buf(sb(w1, "bs1", [128, 32], BF16))]

            with ExitStack() as st:
                Sd = Sched(nc, sems)
                banks, ptr = psum_banks(st)
                ptr = ptr[0]
                identf = Buf(sb(st, "identf", [128, 128], F32))
                Sd.add("pool", lambda h: h.memset(identf.h[:], 0.0), writes=[identf])
                Sd.add("pool", lambda h: h.affine_select(out=identf.h[:], in_=identf.h[:], pattern=[[-1, 128]],
                                                         compare_op=ALU.not_equal, fill=1.0, base=0,
                                                         channel_multiplier=1), reads=[identf], writes=[identf])
                Sd.add("dve", lambda h: h.tensor_copy(out=ident.h[:], in_=identf.h[:]), reads=[identf], writes=[ident])
                Sd.add("pool", lambda h: h.memset(ones_bf.h[:], 1.0), writes=[ones_bf])
                Sd.add("pool", lambda h: h.memset(epsb.h[:], EPS), writes=[epsb])
                Sd.dma("pool", [lambda h: h.dma_start(out=win.h[:, :, 0:DIN], in_=w_in.rearrange("(kc p) n -> p kc n", p=128)),
                                lambda h: h.dma_start(out=win.h[:, :, 960:992], in_=w_in[:, 416:448].rearrange("(kc p) n -> p kc n", p=128)),
                                lambda h: h.dma_start(out=win.h[:, :, 992:1024], in_=w_in[:, 384:416].rearrange("(kc p) n -> p kc n", p=128))],
                       "set_win", writes=[win])
                Sd.add("pool", lambda h: h.tensor_scalar(out=win.h[:, :, 960:992], in0=win.h[:, :, 960:992], scalar1=-1.0,
                                                         scalar2=None, op0=ALU.mult), reads=[win], writes=[win])
                Sd.dma("pool", [lambda h: h.dma_start(out=wukv.h[:], in_=w_ukv),
                                lambda h: h.dma_start(out=t1u.h[:], in_=t1u_d),
                                lambda h: h.dma_start(out=t1v.h[:], in_=t1v_d),
                                lambda h: h.dma_start(out=bcs[0].h[:], in_=bc_d[0]),
                                lambda h: h.dma_start(out=bcs[1].h[:], in_=bc_d[1]),
                                lambda h: h.dma_start(out=bss[0].h[:], in_=bs_d[0]),
                                lambda h: h.dma_start(out=bss[1].h[:], in_=bs_d[1])],
                       "set_misc", writes=[wukv, t1u, t1v, bcs[0], bcs[1], bss[0], bss[1]])
                Sd.dma("sp", [lambda h: h.dma_start(out=ta.h[:], in_=ta_d),
                              lambda h: h.dma_start(out=tsn.h[:], in_=ts_d),
                              lambda h: h.dma_start(out=gkv.h[:], in_=gkvT)], "set_f32", writes=[ta, tsn, gkv])
                wuqf = Buf(sb(st, "wuqf", [128, 2, 768], F32))
                gq = Buf(sb(st, "gq", [128, 2], F32))
                Sd.dma("sp", [lambda h: h.dma_start(out=wuqf.h[:], in_=w_uq.rearrange("(kc p) n -> p kc n", p=128)),
                              lambda h: h.dma_start(out=gq.h[:], in_=gqT)], "set_wuq", writes=[wuqf, gq])
                for kc in range(2):
                    Sd.add("dve", lambda h, kc=kc: h.tensor_scalar(out=wuqf.h[:, kc, :], in0=wuqf.h[:, kc, :], scalar1=gq.h[:, kc:kc + 1],
                                                                   scalar2=None, op0=ALU.mult), reads=[wuqf, gq], writes=[wuqf])
                    Sd.add("dve", lambda h, kc=kc: h.tensor_copy(out=wuq.h[:, kc, 0:768], in_=wuqf.h[:, kc, :]), reads=[wuqf], writes=[wuq])
                    for hd in range(4):
                        b0 = hd * 192 + 128
                        Sd.add("dve", lambda h, kc=kc, hd=hd, b0=b0: h.tensor_scalar(
                            out=wuq.h[:, kc, 768 + hd * 64:768 + hd * 64 + 32], in0=wuqf.h[:, kc, b0 + 32:b0 + 64],
                            scalar1=-1.0, scalar2=None, op0=ALU.mult), reads=[wuqf], writes=[wuq])
                        Sd.add("dve", lambda h, kc=kc, hd=hd, b0=b0: h.tensor_copy(
                            out=wuq.h[:, kc, 768 + hd * 64 + 32:768 + hd * 64 + 64], in_=wuqf.h[:, kc, b0:b0 + 32]),
                            reads=[wuqf], writes=[wuq])
                for hd in range(4):
                    Sd.add("pe", lambda h, hd=hd: h.transpose(out=ptr.h[:, hd * 128:(hd + 1) * 128],
                                                              in_=wukv.h[:, hd * 256:hd * 256 + 128], identity=ident.h[:]),
                           reads=[wukv, ident], writes=[ptr])
                Sd.add("dve", lambda h: h.tensor_copy(out=wukT.h[:].rearrange("p a b -> p (a b)"), in_=ptr.h[:, 0:512]),
                       reads=[ptr], writes=[wukT])
                cbt = Buf(sb(st, "cbt", [128, 128], F32))
                sbt = Buf(sb(st, "sbt", [128, 128], F32))
                wblk = Buf(sb(st, "wblk", [128, 4, 128], F32))
                Sd.add("pool", lambda h: h.memset(wblk.h[:], 0.0), writes=[wblk])
                fl = [lambda h: h.dma_start(out=cbt.h[:], in_=cb_d), lambda h: h.dma_start(out=sbt.h[:], in_=sb_d)]
                for cc in range(4):
                    for gl in range(2):
                        fl.append(lambda h, cc=cc, gl=gl: h.dma_start(out=wblk.h[gl * 64:(gl + 1) * 64, cc, gl * 64:(gl + 1) * 64],
                                                                      in_=w_four[2 * cc + gl]))
                Sd.dma("sp", fl, "set_four", writes=[cbt, sbt, wblk])
                for cc in range(4):
                    bk = banks[cc % 2]
                    Sd.add("pe", lambda h, cc=cc, bk=bk: h.matmul(bk.h[:, 0:128], lhsT=cbt.h[:], rhs=wblk.h[:, cc, :], start=True, stop=True),
                           reads=[cbt, wblk], writes=[bk])
                    Sd.add("pe", lambda h, cc=cc, bk=bk: h.matmul(bk.h[:, 128:256], lhsT=sbt.h[:], rhs=wblk.h[:, cc, :], start=True, stop=True),
                           reads=[sbt, wblk], writes=[bk])
                    Sd.add("dve", lambda h, cc=cc, bk=bk: h.tensor_copy(
                        out=abh.h[:, 2 * cc:2 * cc + 2, :].rearrange("p f (u c) -> p u f c", u=2),
                        in_=bk.h[:, 0:256].rearrange("p (u f c) -> p u f c", u=2, f=2)), reads=[bk], writes=[abh])
                cts = Buf(sb(st, "cts", [128, KC, 2], F32))
                ctb = Buf(sb(st, "ctb", [128, KC, 2], BF16))
                modr = Buf(sb(st, "modr", [2, 6 * D], F32))
                bad = Buf(sb(st, "bad", [2, 6 * D], F32))
                gm2 = Buf(sb(st, "gm2", [2, D], F32))
                gf2 = Buf(sb(st, "gf2", [2, D], F32))
                Sd.dma("sp", [lambda h: h.dma_start(out=cts.h[:], in_=cT),
                              lambda h: h.dma_start(out=bad.h[:], in_=b_ada.partition_broadcast(2)),
                              lambda h: h.dma_start(out=gm2.h[:], in_=g_mix.partition_broadcast(2)),
                              lambda h: h.dma_start(out=gf2.h[:], in_=g_ffn.partition_broadcast(2))],
                       "set_c", writes=[cts, bad, gm2, gf2])
                Sd.add("act", lambda h: h.activation(out=ctb.h[:], in_=cts.h[:], func=AF.Silu), reads=[cts], writes=[ctb])
                wa = bufs(st, "wa", [128, KC, 512], BF16, 2)
                for nt in range(12):
                    wb = wa[nt % 2]
                    bk = banks[2 + nt % 2]
                    Sd.dma("pool", [lambda h, nt=nt, wb=wb: h.dma_start(
                        out=wb.h[:], in_=w_ada[:, nt * 512:(nt + 1) * 512].rearrange("(kc p) n -> p kc n", p=128))],
                        "wa%d" % (nt % 2), writes=[wb])
                    for kc in range(KC):
                        Sd.add("pe", lambda h, kc=kc, wb=wb, bk=bk: h.matmul(bk.h[0:2, :], lhsT=ctb.h[:, kc, :], rhs=wb.h[:, kc, :],
                                                                            start=(kc == 0), stop=(kc == KC - 1)),
                               reads=[ctb, wb], writes=[bk])
                    Sd.add("dve", lambda h, nt=nt, bk=bk: h.tensor_tensor(out=modr.h[:, nt * 512:(nt + 1) * 512], in0=bk.h[0:2, :],
                                                                         in1=bad.h[:, nt * 512:(nt + 1) * 512], op=ALU.add),
                           reads=[bk, bad], writes=[modr])
                Sd.add("dve", lambda h: h.scalar_tensor_tensor(out=modr.h[:, D:2 * D], in0=modr.h[:, D:2 * D], scalar=1.0, in1=gm2.h[:],
                                                               op0=ALU.add, op1=ALU.mult), reads=[modr, gm2], writes=[modr])
                Sd.add("dve", lambda h: h.scalar_tensor_tensor(out=modr.h[:, 4 * D:5 * D], in0=modr.h[:, 4 * D:5 * D], scalar=1.0, in1=gf2.h[:],
                                                               op0=ALU.add, op1=ALU.mult), reads=[modr, gf2], writes=[modr])
                mod_res = Wrap(Res())
                Sd.dma("pool", [lambda h: h.dma_start(out=modscr.rearrange("s j d -> s (j d)"), in_=modr.h[:])], "modst",
                       reads=[modr], writes=[mod_res])
                Sd.emit(0)

            def load_mod(Sd, dst, s, j):
                Sd.dma("sp", [lambda h: h.dma_start(out=dst.h[:], in_=modscr[s, j:j + 1, :].partition_broadcast(128))],
                       "mod_%d" % j, writes=[dst])

            def load_modT(Sd, dst, s, j):
                Sd.dma("sp", [lambda h: h.dma_start(out=dst.h[:], in_=modscr[s, j, :].rearrange("(kc p) -> p kc", p=128),
                                                    allow_slow_non_contiguous=True)], "modT_%d" % j, writes=[dst])

            def front_tile(Sd, xt, ss, xn):
                Sd.add("act", lambda h: h.activation(out=xn.h[:], in_=xt.h[:], func=AF.Square, accum_out=ss.h[:, 0:1]),
                       reads=[xt], writes=[xn, ss])
                Sd.add("act", lambda h: h.activation(out=ss.h[:, 1:2], in_=ss.h[:, 0:1], func=AF.Sqrt, scale=1.0 / D, bias=epsb.h[:, 0:1]),
                       reads=[ss, epsb], writes=[ss])
                Sd.add("dve", lambda h: h.reciprocal(out=ss.h[:, 2:3], in_=ss.h[:, 1:2]), reads=[ss], writes=[ss])
                Sd.add("act", lambda h: h.activation(out=xn.h[:], in_=xt.h[:], func=AF.Identity, scale=ss.h[:, 2:3]),
                       reads=[xt, ss], writes=[xn])

            def front_group(Sd, xns, aT, shT, ptrs, hT, G):
                for kp in range(KC // 2):
                    ptr = ptrs[kp % 2]
                    for k2 in range(2):
                        kc = 2 * kp + k2
                        for t in range(G):
                            Sd.add("pe", lambda h, kc=kc, k2=k2, t=t, ptr=ptr: h.transpose(
                                out=ptr.h[:, (k2 * G + t) * 128:(k2 * G + t + 1) * 128], in_=xns[t].h[:, kc * 128:(kc + 1) * 128],
                                identity=ident.h[:]), reads=[xns[t], ident], writes=[ptr])
                    for k2 in range(2):
                        kc = 2 * kp + k2
                        Sd.add("dve", lambda h, kc=kc, k2=k2, ptr=ptr: h.tensor_scalar(
                            out=hT.h[:, kc, 0:G * 128], in0=ptr.h[:, k2 * G * 128:(k2 + 1) * G * 128],
                            scalar1=aT.h[:, kc:kc + 1], scalar2=shT.h[:, kc:kc + 1], op0=ALU.mult, op1=ALU.add),
                            reads=[ptr, aT, shT], writes=[hT])

            for sq in range(2):
                x_seq = xs[sq]
                x_my = xs[0] if sq == 0 else xm_s
                n_my = S if sq == 0 else QS
                x1_off = 0 if sq == 0 else S
                with ExitStack() as sst:
                    kcT = Buf(sb(sst, "kcT", [128, S], BF16))
                    kpeT = Buf(sb(sst, "kpeT", [64, S], BF16))
                    vc = Buf(sb(sst, "vc", [128, NT, 128], BF16))
                    with ExitStack() as fst:
                        finT = Buf(sb(fst, "finT", [128, 4, S], BF16))
                        with ExitStack() as st:
                            Sd = Sched(nc, sems)
                            banks, ptrs = psum_banks(st)
                            ptr = ptrs[0]
                            a1T = Buf(sb(st, "a1T", [128, KC], F32))
                            sh1T = Buf(sb(st, "sh1T", [128, KC], F32))
                            load_modT(Sd, sh1T, sq, 0)
                            load_modT(Sd, a1T, sq, 1)
                            xts = bufs(st, "xt", [128, D], F32, 3)
                            sss = bufs(st, "ss", [128, 4], F32, 2)
                            xnb = bufs(st, "xn", [128, D], BF16, 8)
                            hTs = bufs(st, "hT", [128, KC, 512], BF16, 2)
                            rcs = bufs(st, "rc", [64, 512], F32, 2)
                            rss = bufs(st, "rs", [64, 512], F32, 2)
                            sqb = Buf(sb(st, "sqb", [128, 512], BF16))
                            rk = Buf(sb(st, "rk", [128, 512], F32))
                            t1 = Buf(sb(st, "t1", [64, 512], F32))
                            t2 = Buf(sb(st, "t2", [64, 512], F32))
                            def pa_tiles(g):
                                for t in range(4):
                                    ti = 4 * g + t
                                    xt = xts[ti % 3]
                                    r0 = g * 512 + t * 128
                                    Sd.dma("sp", [lambda h, xt=xt, r0=r0: h.dma_start(out=xt.h[:], in_=x_seq[r0:r0 + 128, :])],
                                           "xt%d" % (ti % 3), writes=[xt])
                                    front_tile(Sd, xt, sss[ti % 2], xnb[(g % 2) * 4 + t])

                            pa_tiles(0)
                            for g in range(S // 512):
                                hT = hTs[g % 2]
                                rc, rs = rcs[g % 2], rss[g % 2]
                                Sd.dma("sp", [lambda h, g=g, rc=rc: h.dma_start(out=rc.h[:], in_=rope_c[:, g * 512:(g + 1) * 512]),
                                              lambda h, g=g, rs=rs: h.dma_start(out=rs.h[:], in_=rope_s[:, g * 512:(g + 1) * 512])],
                                       "rope%d" % (g % 2), writes=[rc, rs])
                                front_group(Sd, xnb[(g % 2) * 4:(g % 2) * 4 + 4], a1T, sh1T, ptrs, hT, 4)
                                if g + 1 < S // 512:
                                    pa_tiles(g + 1)
                                cols = slice(g * 512, (g + 1) * 512)

                                def zmm(bk, c0, m, hT=hT):
                                    for kc in range(KC):
                                        Sd.add("pe", lambda h, kc=kc: h.matmul(bk.h[0:m, :], lhsT=win.h[:, kc, c0:c0 + m], rhs=hT.h[:, kc, :],
                                                                              start=(kc == 0), stop=(kc == KC - 1)),
                                               reads=[win, hT], writes=[bk])
                                bkv, bsum = banks[0], banks[1]
                                zmm(bkv, 256, 128)
                                Sd.add("act", lambda h: h.activation(out=sqb.h[:], in_=bkv.h[:], func=AF.Square), reads=[bkv], writes=[sqb])
                                Sd.add("pe", lambda h: h.matmul(bsum.h[:], lhsT=ones_bf.h[:], rhs=sqb.h[:], start=True, stop=True),
                                       reads=[ones_bf, sqb], writes=[bsum])
                                Sd.add("act", lambda h: h.activation(out=rk.h[:], in_=bsum.h[:], func=AF.Sqrt, scale=1.0 / 128, bias=epsb.h[:, 0:1]),
                                       reads=[bsum, epsb], writes=[rk])
                                Sd.add("dve", lambda h: h.reciprocal(out=rk.h[:], in_=rk.h[:]), reads=[rk], writes=[rk])
                                Sd.add("dve", lambda h: h.tensor_scalar(out=rk.h[:], in0=rk.h[:], scalar1=gkv.h[:, 0:1], scalar2=None, op0=ALU.mult),
                                       reads=[rk, gkv], writes=[rk])
                                Sd.add("dve", lambda h, cols=cols: h.tensor_tensor(out=kcT.h[:, cols], in0=bkv.h[:], in1=rk.h[:], op=ALU.mult),
                                       reads=[bkv, rk], writes=[kcT])
                                for t in range(4):
                                    Sd.add("pe", lambda h, t=t, g=g: h.transpose(out=ptr.h[:, t * 128:(t + 1) * 128],
                                                                                in_=kcT.h[:, g * 512 + t * 128:g * 512 + (t + 1) * 128],
                                                                                identity=ident.h[:]), reads=[kcT, ident], writes=[ptr])
                                Sd.add("dve", lambda h, g=g: h.tensor_copy(out=vc.h[:, 4 * g:4 * g + 4, :].rearrange("p a b -> p (a b)"),
                                                                          in_=ptr.h[:, 0:512]), reads=[ptr], writes=[vc])
                                bpe, brot = banks[2], banks[3]
                                zmm(bpe, 384, 64)
                                zmm(brot, 960, 64)
                                Sd.add("dve", lambda h, rc=rc: h.tensor_tensor(out=t1.h[:], in0=bpe.h[0:64, :], in1=rc.h[:], op=ALU.mult),
                                       reads=[bpe, rc], writes=[t1])
                                Sd.add("dve", lambda h, rs=rs: h.tensor_tensor(out=t2.h[:], in0=brot.h[0:64, :], in1=rs.h[:], op=ALU.mult),
                                       reads=[brot, rs], writes=[t2])
                                Sd.add("pool", lambda h, cols=cols: h.tensor_tensor(out=kpeT.h[:, cols], in0=t1.h[:], in1=t2.h[:], op=ALU.add),
                                       reads=[t1, t2], writes=[kpeT])
                                for cc in range(4):
                                    bk = banks[4 + cc % 2]
                                    zmm(bk, 448 + cc * 128, 128)
                                    Sd.add("act", lambda h, cc=cc, bk=bk, cols=cols: h.activation(out=finT.h[:, cc, cols], in_=bk.h[:], func=AF.Copy),
                                           reads=[bk], writes=[finT])
                            Sd.emit(1 + 3 * sq)
                        with ExitStack() as st:
                            Sd = Sched(nc, sems)
                            banks, ptrs = psum_banks(st)
                            nk2 = N2 if sq == 0 else cfg.NQ2
                            ncol2 = PC * nk2
                            xall = Buf(sb(st, "xall", [128, 128, N2], BF16))
                            ytok = Buf(sb(st, "ytok", [128, nk2, 64], BF16))
                            tms = bufs(st, "tm", [128, 512], F32, 3)
                            tus = bufs(st, "tu", [128, 512], F32, 3)
                            y2s = bufs(st, "y2", [128, 512], BF16, 3)
                            bcx, bsx = bcs[sq], bss[sq]
                            it = 0
                            for hc in range(8):
                                cc = hc // 2
                                for s2 in range(0, N2, 4):
                                    bk = banks[(s2 // 4) % 2]
                                    for j in range(4):
                                        Sd.add("pe", lambda h, s2=s2, j=j, bk=bk, cc=cc, hc=hc: h.matmul(
                                            bk.h[:, j * 128:(j + 1) * 128],
                                            lhsT=finT.h[:, cc, :].rearrange("p (a b) -> p b a", b=N2)[:, s2 + j, :],
                                            rhs=abh.h[:, hc, :], start=True, stop=True), reads=[finT, abh], writes=[bk])
                                    Sd.add("dve", lambda h, s2=s2, bk=bk: h.tensor_copy(
                                        out=xall.h[:, :, s2:s2 + 4].rearrange("p c s -> p s c"),
                                        in_=bk.h[:].rearrange("p (s c) -> p s c", s=4)),
                                        reads=[bk], writes=[xall])
                                ngh = 64 // PC
                                def s1(gp, it):
                                    b1 = banks[2 + it % 2]
                                    tm, tu, y2 = tms[it % 3], tus[it % 3], y2s[it % 3]
                                    for j in range(2):
                                        c0 = (gp + j) * PC
                                        lu = xall.h[:, c0:c0 + PC, :].rearrange("p c s -> p (c s)")
                                        lv = xall.h[:, 64 + c0:64 + c0 + PC, :].rearrange("p c s -> p (c s)")
                                        Sd.add("pe", lambda h, j=j, lu=lu, b1=b1: h.matmul(b1.h[:, j * 256:(j + 1) * 256], lhsT=lu, rhs=t1u.h[:],
                                                                                          start=True, stop=False), reads=[xall, t1u], writes=[b1])
                                        Sd.add("pe", lambda h, j=j, lv=lv, b1=b1: h.matmul(b1.h[:, j * 256:(j + 1) * 256], lhsT=lv, rhs=t1v.h[:],
                                                                                          start=False, stop=True), reads=[xall, t1v], writes=[b1])
                                    Sd.add("dve", lambda h, b1=b1, tm=tm: h.tensor_tensor(out=tm.h[:], in0=b1.h[:], in1=ta.h[:], op=ALU.mult),
                                           reads=[b1, ta], writes=[tm])
                                    y1v = b1.h[:].rearrange("p (g r k) -> p g r k", g=2, r=2)
                                    tuv = tu.h[:].rearrange("p (g r k) -> p g r k", g=2, r=2)
                                    tsb = tsn.h[:].rearrange("p (g k) -> p g k", g=2)
                                    Sd.add("dve", lambda h, y1v=y1v, tuv=tuv, tsb=tsb: h.tensor_tensor(out=tuv[:, :, 0, :], in0=y1v[:, :, 1, :], in1=tsb,
                                                                                                  op=ALU.mult), reads=[b1, tsn], writes=[tu])
                                    Sd.add("dve", lambda h, y1v=y1v, tuv=tuv, tsb=tsb: h.tensor_tensor(out=tuv[:, :, 1, :], in0=y1v[:, :, 0, :], in1=tsb,
                                                                                                  op=ALU.mult), reads=[b1, tsn], writes=[tu])
                                    tmv = tm.h[:].rearrange("p (g r k) -> p g r k", g=2, r=2)
                                    y2v = y2.h[:].rearrange("p (g r k) -> p g r k", g=2, r=2)
                                    Sd.add("pool", lambda h, tmv=tmv, tuv=tuv, y2v=y2v: h.tensor_tensor(out=y2v[:, :, 0, :], in0=tmv[:, :, 0, :],
                                                                                                    in1=tuv[:, :, 0, :], op=ALU.add),
                                           reads=[tm, tu], writes=[y2])
                                    Sd.add("pool", lambda h, tmv=tmv, tuv=tuv, y2v=y2v: h.tensor_tensor(out=y2v[:, :, 1, :], in0=tmv[:, :, 1, :],
                                                                                                    in1=tuv[:, :, 1, :], op=ALU.subtract),
                                           reads=[tm, tu], writes=[y2])

                                def s2(gp, it):
                                    b2 = banks[4 + it % 2]
                                    y2 = y2s[it % 3]
                                    for j in range(2):
                                        Sd.add("pe", lambda h, j=j, y2=y2, b2=b2: h.matmul(b2.h[:, j * ncol2:(j + 1) * ncol2], lhsT=y2.h[:, j * 256:j * 256 + 128],
                                                                                          rhs=bcx.h[:], start=True, stop=False), reads=[y2, bcx], writes=[b2])
                                        Sd.add("pe", lambda h, j=j, y2=y2, b2=b2: h.matmul(b2.h[:, j * ncol2:(j + 1) * ncol2], lhsT=y2.h[:, j * 256 + 128:j * 256 + 256],
                                                                                          rhs=bsx.h[:], start=False, stop=True), reads=[y2, bsx], writes=[b2])
                                    c0 = gp * PC
                                    Sd.add("act", lambda h, b2=b2, c0=c0: h.activation(
                                        out=ytok.h[:, :, c0:c0 + 2 * PC].rearrange("p k c -> p c k"),
                                        in_=b2.h[:, 0:2 * ncol2].rearrange("p (c k) -> p c k", k=nk2), func=AF.Copy),
                                        reads=[b2], writes=[ytok])

                                gps = list(range(0, ngh, 2))
                                for idx in range(len(gps) + 1):
                                    if idx < len(gps):
                                        s1(gps[idx], it + idx)
                                    if idx >= 1:
                                        s2(gps[idx - 1], it + idx - 1)
                                it += len(gps)
                                kstep = min(8, nk2)
                                Sd.dma("pool", [lambda h, hc=hc, k0=k0: h.dma_start(
                                    out=yf[sq].rearrange("(k2 k1) c -> k1 k2 c", k1=128)[:, k0:k0 + kstep, hc * 64:(hc + 1) * 64],
                                    in_=ytok.h[:, k0:k0 + kstep, :]) for k0 in range(0, nk2, kstep)],
                                    "yfst", reads=[ytok], writes=[Wrap(r) for r in yf_res[sq]])
                            Sd.emit(2 + 3 * sq)
                    with ExitStack() as st:
                        Sd = Sched(nc, sems)
                        banks, ptrs = psum_banks(st, 7)
                        ptr = ptrs[0]
                        wout = Buf(sb(st, "wout", [128, KC, D], BF16))
                        Sd.dma("pool", [lambda h: h.dma_start(out=wout.h[:], in_=w_out.rearrange("(kc p) n -> p kc n", p=128))],
                               "set_wout", writes=[wout])
                        a1T = Buf(sb(st, "a1T", [128, KC], F32))
                        sh1T = Buf(sb(st, "sh1T", [128, KC], F32))
                        ga1b = Buf(sb(st, "ga1b", [128, D], F32))
                        load_modT(Sd, sh1T, sq, 0)
                        load_modT(Sd, a1T, sq, 1)
                        load_mod(Sd, ga1b, sq, 2)
                        for kc in range(KC):
                            Sd.add("pool" if kc % 2 else "dve", lambda h, kc=kc: h.tensor_tensor(out=wout.h[:, kc, :], in0=wout.h[:, kc, :], in1=ga1b.h[:], op=ALU.mult),
                                   reads=[wout, ga1b], writes=[wout])
                        xts = bufs(st, "xt", [128, D], F32, 8)
                        sss = bufs(st, "ss", [128, 4], F32, 2)
                        xnb = bufs(st, "xn", [128, D], BF16, 4)
                        hTs = bufs(st, "hT", [128, KC, 512], BF16, 1)
                        rcs = bufs(st, "rc", [64, 512], F32, 1)
                        rss = bufs(st, "rs", [64, 512], F32, 1)
                        sqb2 = Buf(sb(st, "sqb2", [128, 2, 512], BF16))
                        rq = Buf(sb(st, "rq", [128, 512], F32))
                        qlT = Buf(sb(st, "qlT", [128, 2, 512], BF16))
                        qn = bufs(st, "qn", [128, 512], BF16, 2)
                        qcs = bufs(st, "qc", [128, 4, 512], BF16, 2)
                        qps = bufs(st, "qp", [64, 4, 512], BF16, 2)
                        cq = Buf(sb(st, "cq", [64, 512], F32))
                        sqr = Buf(sb(st, "sqr", [64, 512], F32))
                        t1 = Buf(sb(st, "t1", [64, 512], F32))
                        t2 = Buf(sb(st, "t2", [64, 512], F32))
                        pts = bufs(st, "pt", [128, 512], BF16, 6)
                        rec = Buf(ga1b.h[:, 0:512])
                        rec.r = ga1b.r
                        accD = Buf(sb(st, "accD", [128, 512], F32))
                        accP = Buf(sb(st, "accP", [128, 512], F32))
                        acct = Buf(ga1b.h[:, 512:1024])
                        acct.r = ga1b.r
                        ones_f = Buf(sb(st, "ones_f", [128, 128], F32))
                        Sd.add("pool", lambda h: h.memset(ones_f.h[:], 1.0), writes=[ones_f])
                        onT = Buf(sb(st, "onT", [128, 4, 512], BF16))
                        ymT = Buf(sb(st, "ymT", [128, KC, 512], BF16))
                        yfts = bufs(st, "yft", [128, 512], BF16, 2)
                        ng = n_my // 512
                        bgb = [banks[3], banks[6]]
                        bgi = [0]

                        def bgbank():
                            bgi[0] += 1
                            return bgb[bgi[0] % 2]

                        def prep(g):
                            hT = hTs[0]
                            rc, rs = rcs[0], rss[0]
                            qc, qp = qcs[g % 2], qps[g % 2]
                            pos0 = g * 512
                            rcd, rsd = (rope_c, rope_s) if sq == 0 else (rope_cm, rope_sm)
                            Sd.dma("sp", [lambda h: h.dma_start(out=rc.h[:], in_=rcd[:, pos0:pos0 + 512]),
                                          lambda h: h.dma_start(out=rs.h[:], in_=rsd[:, pos0:pos0 + 512])], "ropec", writes=[rc, rs])
                            for t in range(4):
                                xt = xts[(4 * g + t) % 8]
                                r0 = g * 512 + t * 128
                                Sd.dma("sp", [lambda h, xt=xt, r0=r0: h.dma_start(out=xt.h[:], in_=x_my[r0:r0 + 128, :])],
                                       "xc%d" % ((4 * g + t) % 8), writes=[xt])
                                front_tile(Sd, xt, sss[(4 * g + t) % 2], xnb[t])
                                yield
                            for kp in range(KC // 2):
                                ptr = ptrs[0]
                                for k2 in range(2):
                                    kc = 2 * kp + k2
                                    for t in range(4):
                                        Sd.add("pe", lambda h, kc=kc, k2=k2, t=t, ptr=ptr: h.transpose(
                                            out=ptr.h[:, (k2 * 4 + t) * 128:(k2 * 4 + t + 1) * 128], in_=xnb[t].h[:, kc * 128:(kc + 1) * 128],
                                            identity=ident.h[:]), reads=[xnb[t], ident], writes=[ptr])
                                yield
                                for k2 in range(2):
                                    kc = 2 * kp + k2
                                    Sd.add("dve", lambda h, kc=kc, k2=k2, ptr=ptr: h.tensor_scalar(
                                        out=hT.h[:, kc, 0:512], in0=ptr.h[:, k2 * 512:(k2 + 1) * 512],
                                        scalar1=a1T.h[:, kc:kc + 1], scalar2=sh1T.h[:, kc:kc + 1], op0=ALU.mult, op1=ALU.add),
                                        reads=[ptr, a1T, sh1T], writes=[hT])
                                yield
                            for c2 in range(2):
                                bk = bgbank()
                                for kc in range(KC):
                                    Sd.add("pe", lambda h, kc=kc, c2=c2, bk=bk: h.matmul(bk.h[:], lhsT=win.h[:, kc, c2 * 128:(c2 + 1) * 128], rhs=hT.h[:, kc, :],
                                                                                      start=(kc == 0), stop=(kc == KC - 1)), reads=[win, hT], writes=[bk])
                                yield
                                Sd.add("act", lambda h, c2=c2, bk=bk: h.activation(out=sqb2.h[:, c2, :], in_=bk.h[:], func=AF.Square), reads=[bk], writes=[sqb2])
                                Sd.add("act", lambda h, c2=c2, bk=bk: h.activation(out=qlT.h[:, c2, :], in_=bk.h[:], func=AF.Copy), reads=[bk], writes=[qlT])
                                yield
                            yield
                            bk = bgbank()
                            for c2 in range(2):
                                Sd.add("pe", lambda h, c2=c2, bk=bk: h.matmul(bk.h[:], lhsT=ones_bf.h[:], rhs=sqb2.h[:, c2, :], start=(c2 == 0), stop=(c2 == 1)),
                                       reads=[ones_bf, sqb2], writes=[bk])
                            yield
                            Sd.add("act", lambda h, bk=bk: h.activation(out=rq.h[:], in_=bk.h[:], func=AF.Sqrt, scale=1.0 / 256, bias=epsb.h[:, 0:1]),
                                   reads=[bk, epsb], writes=[rq])
                            yield
                            Sd.add("dve", lambda h: h.reciprocal(out=rq.h[:], in_=rq.h[:]), reads=[rq], writes=[rq])
                            Sd.add("dve", lambda h: h.tensor_tensor(out=cq.h[:], in0=rc.h[:], in1=rq.h[0:64, :], op=ALU.mult), reads=[rc, rq], writes=[cq])
                            Sd.add("dve", lambda h: h.tensor_tensor(out=sqr.h[:], in0=rs.h[:], in1=rq.h[0:64, :], op=ALU.mult), reads=[rs, rq], writes=[sqr])
                            yield
                            yield
                            for hd in range(4):
                                qnb = qn[hd % 2]
                                bn = bgbank()
                                for kc in range(2):
                                    Sd.add("pe", lambda h, kc=kc, hd=hd, bn=bn: h.matmul(bn.h[:], lhsT=wuq.h[:, kc, hd * 192:hd * 192 + 128], rhs=qlT.h[:, kc, :],
                                                                                        start=(kc == 0), stop=(kc == 1)), reads=[wuq, qlT], writes=[bn])
                                yield
                                Sd.add("dve", lambda h, qnb=qnb, bn=bn: h.tensor_tensor(out=qnb.h[:], in0=bn.h[:], in1=rq.h[:], op=ALU.mult), reads=[bn, rq], writes=[qnb])
                                yield
                                yield
                                bc_ = bgbank()
                                Sd.add("pe", lambda h, hd=hd, qnb=qnb, bc_=bc_: h.matmul(bc_.h[:], lhsT=wukT.h[:, hd, :], rhs=qnb.h[:], start=True, stop=True),
                                       reads=[wukT, qnb], writes=[bc_])
                                yield
                                Sd.add("act", lambda h, hd=hd, bc_=bc_: h.activation(out=qc.h[:, hd, :], in_=bc_.h[:], func=AF.Copy), reads=[bc_], writes=[qc])
                                yield
                                bp = bgbank()
                                for kc in range(2):
                                    Sd.add("pe", lambda h, kc=kc, hd=hd, bp=bp: h.matmul(bp.h[0:64, :], lhsT=wuq.h[:, kc, hd * 192 + 128:hd * 192 + 192], rhs=qlT.h[:, kc, :],
                                                                                        start=(kc == 0), stop=(kc == 1)), reads=[wuq, qlT], writes=[bp])
                                yield
                                Sd.add("dve", lambda h, bp=bp: h.tensor_tensor(out=t1.h[:], in0=bp.h[0:64, :], in1=cq.h[:], op=ALU.mult), reads=[bp, cq], writes=[t1])
                                yield
                                br = bgbank()
                                for kc in range(2):
                                    Sd.add("pe", lambda h, kc=kc, hd=hd, br=br: h.matmul(br.h[0:64, :], lhsT=wuq.h[:, kc, 768 + hd * 64:768 + hd * 64 + 64], rhs=qlT.h[:, kc, :],
                                                                                        start=(kc == 0), stop=(kc == 1)), reads=[wuq, qlT], writes=[br])
                                yield
                                Sd.add("dve", lambda h, br=br: h.tensor_tensor(out=t2.h[:], in0=br.h[0:64, :], in1=sqr.h[:], op=ALU.mult), reads=[br, sqr], writes=[t2])
                                yield
                                Sd.add("pool", lambda h, hd=hd, qp=qp: h.tensor_tensor(out=qp.h[:, hd, :], in0=t1.h[:], in1=t2.h[:], op=ALU.add), reads=[t1, t2], writes=[qp])
                                yield

                        def post(g):
                            gx = [xts[(4 * g + t) % 8] for t in range(4)]
                            for t in range(4):
                                yft = yfts[t % 2]
                                r0 = g * 512 + t * 128
                                Sd.dma("sp", [lambda h, yft=yft, r0=r0: h.dma_start(out=yft.h[:], in_=yf[sq][r0:r0 + 128, :])], "yft%d" % (t % 2),
                                       reads=[Wrap(yf_res[sq][r0 // 128])], writes=[yft])
                                yield
                                ptr = ptrs[0]
                                for cc in range(4):
                                    Sd.add("pe", lambda h, cc=cc, yft=yft, ptr=ptr: h.transpose(out=ptr.h[:, cc * 128:(cc + 1) * 128], in_=yft.h[:, cc * 128:(cc + 1) * 128],
                                                                                              identity=ident.h[:]), reads=[yft, ident], writes=[ptr])
                                yield
                                Sd.add("act", lambda h, t=t, ptr=ptr: h.activation(out=ymT.h[:, 4:8, t * 128:(t + 1) * 128],
                                                                                   in_=ptr.h[:, 0:512].rearrange("p (k c) -> p k c", k=4), func=AF.Copy),
                                       reads=[ptr], writes=[ymT])
                                yield
                            yield
                            for t in range(4):
                                for n2 in range(2):
                                    bk = bgbank()
                                    for kc in range(KC):
                                        Sd.add("pe", lambda h, kc=kc, t=t, n2=n2, bk=bk: h.matmul(bk.h[:], lhsT=ymT.h[:, kc, t * 128:(t + 1) * 128],
                                                                                                rhs=wout.h[:, kc, n2 * 512:(n2 + 1) * 512],
                                                                                                start=(kc == 0), stop=(kc == KC - 1)), reads=[ymT, wout], writes=[bk])
                                    yield
                                    Sd.add("dve", lambda h, n2=n2, bk=bk, xg=gx[t]: h.tensor_tensor(out=xg.h[:, n2 * 512:(n2 + 1) * 512], in0=bk.h[:],
                                                                                               in1=xg.h[:, n2 * 512:(n2 + 1) * 512], op=ALU.add),
                                           reads=[bk, gx[t]], writes=[gx[t]])
                                    yield
                                r0 = x1_off + g * 512 + t * 128
                                Sd.dma("pool", [lambda h, xg=gx[t], r0=r0: h.dma_start(out=x1scr[r0:r0 + 128, :], in_=xg.h[:])], "x1st%d" % ((4 * g + t) % 8),
                                       reads=[gx[t]], writes=[Wrap(x1_res[r0 // 128])])

                        def finish_head(hd):
                            bo, bm = banks[4 + hd % 2], bgbank()
                            Sd.add("pe", lambda h, bm=bm: h.matmul(bm.h[:], lhsT=ones_f.h[:], rhs=acct.h[:], start=True, stop=True),
                                   reads=[ones_f, acct], writes=[bm])
                            Sd.add("dve", lambda h, bm=bm: h.reciprocal(out=rec.h[:], in_=bm.h[:]), reads=[bm], writes=[rec])
                            Sd.add("dve", lambda h, hd=hd, bo=bo: h.tensor_tensor(out=onT.h[:, hd, :], in0=bo.h[:], in1=rec.h[:], op=ALU.mult),
                                   reads=[bo, rec], writes=[onT])

                        def vup_head(hd):
                            bk = bgbank()
                            Sd.add("pe", lambda h, hd=hd, bk=bk: h.matmul(bk.h[:], lhsT=wukv.h[:, hd * 256 + 128:hd * 256 + 256], rhs=onT.h[:, hd, :], start=True, stop=True),
                                   reads=[wukv, onT], writes=[bk])
                            Sd.add("act", lambda h, hd=hd, bk=bk: h.activation(out=ymT.h[:, hd, :], in_=bk.h[:], func=AF.Copy), reads=[bk], writes=[ymT])

                        for _ in prep(0):
                            pass
                        blocks = [(g, hd, j) for g in range(ng) for hd in range(4) for j in range(NT)]
                        LAG = 3
                        NS = 3
                        pending = []
                        bg = []
                        for i in range(len(blocks) + LAG + 16):
                            if i < len(blocks):
                                g, hd, j = blocks[i]
                                if hd == 0 and j == 0:
                                    for kind, gg, gen in bg:
                                        for _ in gen:
                                            pass
                                    bg = []
                                    if g == 0 and ng > 1:
                                        bg.append(("prep", 1, prep(1)))
                                qc, qp = qcs[g % 2], qps[g % 2]
                                bs_ = banks[i % NS]
                                pt = pts[i % 6]
                                Sd.add("pe", lambda h, j=j, hd=hd, bs_=bs_, qc=qc: h.matmul(bs_.h[:], lhsT=kcT.h[:, j * 128:(j + 1) * 128], rhs=qc.h[:, hd, :],
                                                                                    start=True, stop=False), reads=[kcT, qc], writes=[bs_])
                                Sd.add("pe", lambda h, j=j, hd=hd, bs_=bs_, qp=qp: h.matmul(bs_.h[:], lhsT=kpeT.h[:, j * 128:(j + 1) * 128], rhs=qp.h[:, hd, :],
                                                                                    start=False, stop=True), reads=[kpeT, qp], writes=[bs_])
                                Sd.add("act", lambda h, bs_=bs_, pt=pt: h.activation(out=pt.h[:], in_=bs_.h[:], func=AF.Exp, scale=SCALE), reads=[bs_], writes=[pt])
                                eng, acc = ("dve", accD) if j % 2 == 0 else ("pool", accP)
                                if j < 2:
                                    Sd.add(eng, lambda h, pt=pt, acc=acc: h.tensor_copy(out=acc.h[:], in_=pt.h[:]), reads=[pt], writes=[acc])
                                else:
                                    Sd.add(eng, lambda h, pt=pt, acc=acc: h.tensor_tensor(out=acc.h[:], in0=acc.h[:], in1=pt.h[:], op=ALU.add),
                                           reads=[pt, acc], writes=[acc])
                                if j == NT - 1:
                                    Sd.add("pool", lambda h: h.tensor_tensor(out=acct.h[:], in0=accD.h[:], in1=accP.h[:], op=ALU.add),
                                           reads=[accD, accP], writes=[acct])
                            if LAG <= i < len(blocks) + LAG:
                                g2, hd2, j2 = blocks[i - LAG]
                                bo = banks[4 + hd2 % 2]
                                pt2 = pts[(i - LAG) % 6]
                                Sd.add("pe", lambda h, j2=j2, pt2=pt2, bo=bo: h.matmul(bo.h[:], lhsT=vc.h[:, j2, :], rhs=pt2.h[:], start=(j2 == 0), stop=(j2 == NT - 1)),
                                       reads=[vc, pt2], writes=[bo])
                                if j2 == NT - 1:
                                    pending.append([i + 4, hd2, 0, g2])
                            if pending and pending[0][0] <= i:
                                if pending[0][2] == 0:
                                    finish_head(pending[0][1])
                                    pending[0][2] = 1
                                    pending[0][0] = i + 6
                                else:
                                    pg = pending[0][3]
                                    while bg and bg[0][0] == "post" and bg[0][1] < pg:
                                        for _ in bg[0][2]:
                                            pass
                                        bg.pop(0)
                                    ph = pending.pop(0)[1]
                                    vup_head(ph)
                                    if ph == 3:
                                        bg.append(("post", pg, post(pg)))
                                        if pg + 2 < ng:
                                            bg.append(("prep", pg + 2, prep(pg + 2)))
                            elif bg and i % 2 == 1:
                                try:
                                    next(bg[0][2])
                                except StopIteration:
                                    bg.pop(0)
                        assert not pending
                        for kind, gg, gen in bg:
                            for _ in gen:
                                pass
                        Sd.emit(3 + 3 * sq)

        with ExitStack() as st:
            Sd = Sched(nc, sems)
            banks, ptrs = psum_banks(st)
            wg = Buf(sb(st, "wg", [128, KC, DFF], BF16))
            wu = Buf(sb(st, "wu", [128, KC, DFF], BF16))
            wd = Buf(sb(st, "wd", [128, FC, D], BF16))
            for kc in range(KC):
                Sd.dma("pool", [lambda h, kc=kc: h.dma_start(out=wg.h[:, kc, :], in_=w_gate[kc * 128:(kc + 1) * 128, :]),
                                lambda h, kc=kc: h.dma_start(out=wu.h[:, kc, :], in_=w_up[kc * 128:(kc + 1) * 128, :])], "set_wgu", writes=[wg, wu])
            for f0 in range(0, FC, 2):
                Sd.dma("pool", [lambda h, f0=f0: h.dma_start(out=wd.h[:, f0:f0 + 2, :], in_=w_down[f0 * 128:(f0 + 2) * 128, :].rearrange("(f p) n -> p f n", p=128))],
                       "set_wd", writes=[wd])
            gfb = Buf(sb(st, "gfb", [128, D], F32))
            Sd.dma("sp", [lambda h: h.dma_start(out=gfb.h[:], in_=g_final.partition_broadcast(128))], "set_gf", writes=[gfb])
            a2T = Buf(sb(st, "a2T", [128, KC], F32))
            sh2T = Buf(sb(st, "sh2T", [128, KC], F32))
            ga2b = Buf(sb(st, "ga2b", [128, D], F32))
            xts = bufs(st, "xt", [128, D], F32, 3)
            junk = Buf(sb(st, "junk", [128, D], BF16))
            sss = bufs(st, "ss", [128, 4], F32, 2)
            tmps = bufs(st, "tmp", [128, D], F32, 2)
            hbs = bufs(st, "hb", [128, D], BF16, 4)
            hTs = bufs(st, "hT", [128, KC, 256], BF16, 2)
            sgs = bufs(st, "sg", [128, 512], F32, 2)
            atoks = bufs(st, "atok", [128, 512], BF16, 2)
            gcnt = [0]
            actT = Buf(sb(st, "actT", [128, FC, 256], BF16))
            mts = bufs(st, "mt", [128, 512], F32, 1)
            yts = bufs(st, "yt", [128, D], F32, 2)
            ots = tmps
            ss3 = bufs(st, "ss3", [128, 4], F32, 2)
            ti = 0
            ngrp = (S + QS) // 256
            for g in range(ngrp):
                r_base = g * 256
                sq = 0 if r_base < S else 1
                if g == 0 or r_base == S:
                    load_modT(Sd, sh2T, sq, 3)
                    load_modT(Sd, a2T, sq, 4)
                    load_mod(Sd, ga2b, sq, 5)
                hT = hTs[g % 2]
                gx = [xts[(2 * g + t) % 3] for t in range(2)]
                for t in range(2):
                    xt = gx[t]
                    r0 = r_base + t * 128
                    Sd.dma("sp", [lambda h, xt=xt, r0=r0: h.dma_start(out=xt.h[:], in_=x1scr[r0:r0 + 128, :])], "xd%d" % ((2 * g + t) % 3),
                           reads=[Wrap(x1_res[r0 // 128])], writes=[xt])
                    front_tile(Sd, xt, sss[ti % 2], hbs[(g % 2) * 2 + t])
                    ti += 1
                front_group(Sd, hbs[(g % 2) * 2:(g % 2) * 2 + 2], a2T, sh2T, ptrs, hT, 2)
                for t in range(2):
                    for r in range((DFF + 511) // 512):
                        c0 = r * 512
                        w = min(512, DFF - c0)
                        bA, bB = banks[2 * (gcnt[0] % 2)], banks[2 * (gcnt[0] % 2) + 1]
                        sg = sgs[gcnt[0] % 2]
                        atok = atoks[gcnt[0] % 2]
                        ptr = ptrs[gcnt[0] % 2]
                        for kc in range(KC):
                            Sd.add("pe", lambda h, kc=kc, t=t, c0=c0, w=w, bA=bA, hT=hT: h.matmul(bA.h[:, 0:w], lhsT=hT.h[:, kc, t * 128:(t + 1) * 128], rhs=wg.h[:, kc, c0:c0 + w],
                                                                                           start=(kc == 0), stop=(kc == KC - 1)), reads=[wg, hT], writes=[bA])
                        for kc in range(KC):
                            Sd.add("pe", lambda h, kc=kc, t=t, c0=c0, w=w, bB=bB, hT=hT: h.matmul(bB.h[:, 0:w], lhsT=hT.h[:, kc, t * 128:(t + 1) * 128], rhs=wu.h[:, kc, c0:c0 + w],
                                                                                           start=(kc == 0), stop=(kc == KC - 1)), reads=[wu, hT], writes=[bB])
                        Sd.add("act", lambda h, w=w, bA=bA, sg=sg: h.activation(out=sg.h[:, 0:w], in_=bA.h[:, 0:w], func=AF.Silu), reads=[bA], writes=[sg])
                        Sd.add("dve", lambda h, w=w, bB=bB, sg=sg, atok=atok: h.tensor_tensor(out=atok.h[:, 0:w], in0=bB.h[:, 0:w], in1=sg.h[:, 0:w], op=ALU.mult),
                               reads=[bB, sg], writes=[atok])
                        nq = w // 128
                        for q in range(nq):
                            Sd.add("pe", lambda h, q=q, atok=atok, ptr=ptr: h.transpose(out=ptr.h[:, q * 128:(q + 1) * 128], in_=atok.h[:, q * 128:(q + 1) * 128],
                                                                                      identity=ident.h[:]), reads=[atok, ident], writes=[ptr])
                        if gcnt[0] % 2 == 0:
                            Sd.add("act", lambda h, r=r, t=t, nq=nq, w=w, ptr=ptr: h.activation(out=actT.h[:, 4 * r:4 * r + nq, t * 128:(t + 1) * 128],
                                                                                          in_=ptr.h[:, 0:w].rearrange("p (k c) -> p k c", k=nq), func=AF.Copy),
                                   reads=[ptr], writes=[actT])
                        else:
                            Sd.add("dve", lambda h, r=r, t=t, nq=nq, w=w, ptr=ptr: h.tensor_copy(out=actT.h[:, 4 * r:4 * r + nq, t * 128:(t + 1) * 128],
                                                                                           in_=ptr.h[:, 0:w].rearrange("p (k c) -> p k c", k=nq)),
                                   reads=[ptr], writes=[actT])
                        gcnt[0] += 1
                for t in range(2):
                    yt = yts[t]
                    ot = ots[t]
                    s3 = ss3[t]
                    for n2 in range(2):
                        bk = banks[4 + (2 * t + n2) % 2]
                        mt = mts[0]
                        for fc in range(FC):
                            Sd.add("pe", lambda h, fc=fc, t=t, n2=n2, bk=bk: h.matmul(bk.h[:], lhsT=actT.h[:, fc, t * 128:(t + 1) * 128],
                                                                                    rhs=wd.h[:, fc, n2 * 512:(n2 + 1) * 512],
                                                                                    start=(fc == 0), stop=(fc == FC - 1)), reads=[actT, wd], writes=[bk])
                        Sd.add("dve", lambda h, n2=n2, bk=bk, mt=mt: h.tensor_tensor(out=mt.h[:], in0=bk.h[:], in1=ga2b.h[:, n2 * 512:(n2 + 1) * 512], op=ALU.mult),
                               reads=[bk, ga2b], writes=[mt])
                        Sd.add("pool", lambda h, n2=n2, mt=mt, xg=gx[t], yt=yt: h.tensor_tensor(out=yt.h[:, n2 * 512:(n2 + 1) * 512], in0=mt.h[:],
                                                                                       in1=xg.h[:, n2 * 512:(n2 + 1) * 512], op=ALU.add),
                               reads=[mt, gx[t]], writes=[yt])
                    Sd.add("act", lambda h, yt=yt, s3=s3: h.activation(out=junk.h[:], in_=yt.h[:], func=AF.Square, accum_out=s3.h[:, 0:1]),
                           reads=[yt], writes=[junk, s3])
                    Sd.add("act", lambda h, s3=s3: h.activation(out=s3.h[:, 1:2], in_=s3.h[:, 0:1], func=AF.Sqrt, scale=1.0 / D, bias=epsb.h[:, 0:1]),
                           reads=[s3, epsb], writes=[s3])
                    Sd.add("dve", lambda h, s3=s3: h.reciprocal(out=s3.h[:, 2:3], in_=s3.h[:, 1:2]), reads=[s3], writes=[s3])
                    Sd.add("act", lambda h, yt=yt, ot=ot, s3=s3: h.activation(out=ot.h[:], in_=yt.h[:], func=AF.Identity, scale=s3.h[:, 2:3]),
                           reads=[yt, s3], writes=[ot])
                    Sd.add("pool", lambda h, ot=ot: h.tensor_tensor(out=ot.h[:], in0=ot.h[:], in1=gfb.h[:], op=ALU.mult), reads=[ot, gfb], writes=[ot])
                    r0 = r_base + t * 128
                    if r0 < S:
                        dst = y_p[r0:r0 + 128, :]
                    else:
                        dst = y_s[r0 - S:r0 - S + 128, :]
                    Sd.dma("pool", [lambda h, ot=ot, dst=dst: h.dma_start(out=dst, in_=ot.h[:])], "yst%d" % t, reads=[ot])
            Sd.emit(7)
    return nc


def _tables(cfg):
    S, N2, PC = cfg.S, cfg.N2, cfg.PC
    f64 = np.float64
    pos = np.arange(S, dtype=np.float32)
    inv = (1.0 / (np.float32(10000.0) ** (np.arange(0, 64, 2, dtype=np.float32) / np.float32(64)))).astype(np.float32)
    ang = (pos[None, :] * inv[:, None]).astype(np.float32)
    rope_c = np.concatenate([np.cos(ang), np.cos(ang)], 0).astype(np.float32)
    rope_s = np.concatenate([np.sin(ang), np.sin(ang)], 0).astype(np.float32)
    norm = 1.0 / math.sqrt(S * 64)
    m = np.arange(64, dtype=f64)
    C64 = np.cos(2 * np.pi * np.outer(m, m) / 64) * norm
    S64 = np.sin(2 * np.pi * np.outer(m, m) / 64) * norm
    cb = np.zeros((128, 128), f64)
    sbm = np.zeros((128, 128), f64)
    for gl in range(2):
        cb[gl * 64:(gl + 1) * 64, gl * 64:(gl + 1) * 64] = C64
        sbm[gl * 64:(gl + 1) * 64, gl * 64:(gl + 1) * 64] = -S64
    s1 = np.arange(128, dtype=f64)
    C1 = np.cos(2 * np.pi * np.outer(s1, s1) / 128)
    S1 = np.sin(2 * np.pi * np.outer(s1, s1) / 128)
    t1u = np.concatenate([C1, -S1], 1)
    t1v = np.concatenate([S1, C1], 1)
    s2 = np.arange(N2, dtype=f64)
    Tc = np.cos(2 * np.pi * np.outer(s2, s1) / S)
    Ts = np.sin(2 * np.pi * np.outer(s2, s1) / S)
    Tcp = np.tile(Tc, (PC, 1))
    Tsp = np.tile(Ts, (PC, 1))
    ta = np.tile(Tcp, (1, 4))
    C2 = np.cos(2 * np.pi * np.outer(s2, s2) / N2)
    S2 = np.sin(2 * np.pi * np.outer(s2, s2) / N2)

    def blk(M, k2s):
        n = len(k2s)
        out = np.zeros((128, PC * n), f64)
        for c in range(PC):
            out[c * N2:(c + 1) * N2, c * n:(c + 1) * n] = M[:, k2s]
        return out

    full = np.arange(N2)
    tb = dict(rope_c=rope_c, rope_s=rope_s, cb=cb, sb=sbm, t1u=t1u, t1v=t1v, ta=ta, ts=Tsp,
              bc_p=blk(C2, full), bs_p=blk(S2, full))
    tb["ts"] = np.tile(Tsp, (1, 2))
    tb = {k: np.ascontiguousarray(v, dtype=np.float32) for k, v in tb.items()}
    bq = []
    for r in range(4):
        k2s = np.arange(cfg.NQ2 * r, cfg.NQ2 * (r + 1))
        bq.append((np.ascontiguousarray(blk(C2, k2s), dtype=np.float32), np.ascontiguousarray(blk(S2, k2s), dtype=np.float32)))
    return tb, bq


_CACHE = {}


def _run(cfg, inputs):
    S, QS = cfg.S, cfg.QS
    f = lambda a: np.ascontiguousarray(np.asarray(a), dtype=np.float32)
    x_prompt, x_sample = f(inputs["x_prompt"]), f(inputs["x_sample"])
    c_prompt, c_sample = f(inputs["c_prompt"]), f(inputs["c_sample"])
    tb, bq = _tables(cfg)
    shared = dict(
        w_ada=f(inputs["w_ada"][0]), b_ada=f(inputs["b_ada"][0]).reshape(1, -1), g_mix=f(inputs["g_mix"][0]).reshape(1, -1),
        g_ffn=f(inputs["g_ffn"][0]).reshape(1, -1), g_final=f(inputs["g_final"]).reshape(1, -1), w_in=f(inputs["w_in"][0]),
        gqT=f(f(inputs["g_q_lat"][0]).reshape(2, 128).T), w_uq=f(inputs["w_uq"][0]),
        gkvT=f(f(inputs["g_kv_lat"][0]).reshape(128, 1)), w_ukv=f(inputs["w_ukv"][0]), w_four=f(inputs["w_four"][0]),
        w_out=f(inputs["w_out"][0]), w_gate=f(inputs["w_gate"][0]), w_up=f(inputs["w_up"][0]), w_down=f(inputs["w_down"][0]),
    )
    shared.update(tb)
    in_maps = []
    for i in range(N_CORES):
        sidx, r = i // 4, i % 4
        cc = np.stack([c_prompt[i], c_sample[sidx]], 0)
        cT = f(cc.reshape(2, KC, 128).transpose(2, 1, 0))
        m = dict(shared)
        m.update(xs_p=x_prompt[i], xs_s=x_sample[sidx], xm_s=f(x_sample[sidx, r * QS:(r + 1) * QS]), cT=cT,
                 bc_s=bq[r][0], bs_s=bq[r][1],
                 rope_cm=f(tb["rope_c"][:, r * QS:(r + 1) * QS]), rope_sm=f(tb["rope_s"][:, r * QS:(r + 1) * QS]))
        in_maps.append(m)
    nc = build_program(cfg)
    res = run_bass_kernel_spmd(nc, in_maps, core_ids=list(range(N_CORES)))
    y_prompt = np.stack([np.asarray(res.results[i]["y_p"], dtype=np.float32) for i in range(N_CORES)], 0)
    y_sample = np.zeros((2, S, D), np.float32)
    for i in range(N_CORES):
        y_sample[i // 4, (i % 4) * QS:(i % 4 + 1) * QS] = np.asarray(res.results[i]["y_s"], dtype=np.float32)
    return y_prompt, y_sample


def kernel(**inputs):
    S = int(np.asarray(inputs["x_prompt"]).shape[1])
    return _run(Cfg(S), inputs)
```

```python
import math
from contextlib import ExitStack

import numpy as np
import concourse.bass as bass
import concourse.mybir as mybir
from concourse.bass_utils import run_bass_kernel_spmd

F32 = mybir.dt.float32
BF16 = mybir.dt.bfloat16
AF = mybir.ActivationFunctionType
ALU = mybir.AluOpType

D = 1024
KC = 8
DIN = 960
DFF = 2816
FC = DFF // 128
EPS = 1e-6
SCALE = 1.0 / math.sqrt(192.0)
N_CORES = 8
import os
_PH = os.environ.get("K_PHASES")
PHASES = None if not _PH else set(int(v) for v in _PH.split(","))
MAXOPS = int(os.environ.get("K_MAXOPS", "0"))


class Res:
    __slots__ = ("name", "writer", "readers")

    def __init__(self, name=""):
        self.name = name
        self.writer = None
        self.readers = []


class Buf:
    __slots__ = ("h", "r")

    def __init__(self, h, name=""):
        self.h = h
        self.r = Res(name)


class Op:
    __slots__ = ("eng", "fn", "deps", "signaled", "cnt", "sem", "is_dma", "ndma", "ph")

    def __init__(self, eng, fn):
        self.eng = eng
        self.fn = fn
        self.deps = []
        self.signaled = False
        self.cnt = 0
        self.sem = None
        self.is_dma = False
        self.ndma = 0
        self.ph = None


ENGS = ("pe", "act", "dve", "pool", "sp")
HANDLES = {"pe": "tensor", "act": "scalar", "dve": "vector", "pool": "gpsimd", "sp": "sync"}


class SemTable:
    def __init__(self, nc, stack):
        self.nc = nc
        self.stack = stack
        self.eng = {e: [stack.enter_context(nc.semaphore("s_" + e)), 0] for e in ENGS if e != "sp"}
        self.dma = {}

    def dma_sem(self, key):
        if key not in self.dma:
            self.dma[key] = [self.stack.enter_context(self.nc.semaphore("d_%d" % len(self.dma))), 0]
        return self.dma[key]


class Sched:
    def __init__(self, nc, sems):
        self.nc = nc
        self.sems = sems
        self.ops = {e: [] for e in ENGS}

    def _deps(self, op, reads, writes):
        deps = []
        for r in reads:
            if r.writer is not None:
                deps.append((0, r.writer))
        for w in writes:
            if w.writer is not None:
                deps.append((1, w.writer))
            for rd in w.readers:
                deps.append((2, rd))
        out = []
        seen = set()
        for kind, d in deps:
            if d is op or id(d) in seen or d.ph is not op.ph:
                continue
            if (not d.is_dma) and (not op.is_dma) and d.eng == op.eng:
                if op.eng == "pe" or kind != 0:
                    continue
            seen.add(id(d))
            d.signaled = True
            out.append(d)
        op.deps = out
        for r in reads:
            if not op.is_dma:
                r.readers = [x for x in r.readers if x.is_dma or x.eng != op.eng or x.ph is not op.ph]
            r.readers.append(op)
        for w in writes:
            w.writer = op
            w.readers = []

    def add(self, eng, fn, reads=(), writes=()):
        self.nrec = getattr(self, "nrec", 0) + 1
        if MAXOPS and self.nrec > MAXOPS:
            return None
        op = Op(eng, fn)
        op.ph = self
        self._deps(op, [b.r for b in reads], [b.r for b in writes])
        self.ops[eng].append(op)
        return op

    def dma(self, eng, fns, key, reads=(), writes=()):
        self.nrec = getattr(self, "nrec", 0) + 1
        if MAXOPS and self.nrec > MAXOPS:
            return None
        op = Op(eng, fns)
        op.is_dma = True
        op.ndma = len(fns)
        op.sem = key
        op.ph = self
        self._deps(op, [b.r for b in reads], [b.r for b in writes])
        self.ops[eng].append(op)
        return op

    def emit(self, phase=None):
        nc = self.nc
        if PHASES is not None and phase not in PHASES:
            return
        for e in ENGS:
            for op in self.ops[e]:
                if op.is_dma:
                    ent = self.sems.dma_sem(op.sem)
                    ent[1] += 16 * op.ndma
                    op.cnt = ent[1]
                    op.sem = ent[0]
                elif op.signaled:
                    ent = self.sems.eng[e]
                    ent[1] += 1
                    op.cnt = ent[1]
                    op.sem = ent[0]

        def run(e, h):
            waited = {}
            for op in self.ops[e]:
                for d in op.deps:
                    key = id(d.sem)
                    if waited.get(key, 0) >= d.cnt:
                        continue
                    waited[key] = d.cnt
                    if os.environ.get("K_TRACE"):
                        print("TRACE", e, "wait", d.sem.name if hasattr(d.sem, "name") else d.sem, d.cnt, "dep_eng", d.eng, "dma" if d.is_dma else "")
                    h.wait_ge(d.sem, d.cnt)
                if os.environ.get("K_TRACE"):
                    print("TRACE", e, "op", "dma" if op.is_dma else "", "sig" if op.signaled else "", op.cnt)
                if op.is_dma:
                    for f in op.fn:
                        f(h).then_inc(op.sem, 16)
                else:
                    ins = op.fn(h)
                    if op.signaled:
                        ins.then_inc(op.sem, 1)
            done = {}
            for op in self.ops[e]:
                if op.is_dma:
                    done[id(op.sem)] = (op.sem, op.cnt)
            for s, v in done.values():
                h.wait_ge(s, v)

        with nc.Block() as block:
            for e in ENGS:
                if self.ops[e]:
                    getattr(block, HANDLES[e])(lambda h, e=e: run(e, h))


class Cfg:
    def __init__(self, S):
        self.S = S
        self.N2 = S // 128
        self.PC = 128 // self.N2
        self.NT = S // 128
        self.QS = S // 4
        self.NQ2 = self.N2 // 4
        assert self.N2 * self.PC == 128 and self.QS % 512 == 0


def build_program(cfg):
    S, N2, PC, NT, QS = cfg.S, cfg.N2, cfg.PC, cfg.NT, cfg.QS
    NG = 128 // PC
    nc = bass.Bass("TRN2", target_bir_lowering=False)

    def din(name, shape, dt=F32):
        return nc.dram_tensor(name, list(shape), dt, kind="ExternalInput").ap()

    xs = [din("xs_p", [S, D]), din("xs_s", [S, D])]
    xm_s = din("xm_s", [QS, D])
    cT = din("cT", [128, KC, 2])
    w_ada = din("w_ada", [D, 6 * D])
    b_ada = din("b_ada", [1, 6 * D])
    g_mix = din("g_mix", [1, D])
    g_ffn = din("g_ffn", [1, D])
    g_final = din("g_final", [1, D])
    w_in = din("w_in", [D, DIN])
    gqT = din("gqT", [128, 2])
    w_uq = din("w_uq", [256, 768])
    gkvT = din("gkvT", [128, 1])
    w_ukv = din("w_ukv", [128, 1024])
    w_four = din("w_four", [8, 64, 64])
    w_out = din("w_out", [D, D])
    w_gate = din("w_gate", [D, DFF])
    w_up = din("w_up", [D, DFF])
    w_down = din("w_down", [DFF, D])
    rope_c = din("rope_c", [64, S])
    rope_s = din("rope_s", [64, S])
    rope_cm = din("rope_cm", [64, QS])
    rope_sm = din("rope_sm", [64, QS])
    cb_d = din("cb", [128, 128])
    sb_d = din("sb", [128, 128])
    t1u_d = din("t1u", [128, 256])
    t1v_d = din("t1v", [128, 256])
    ta_d = din("ta", [128, 512])
    ts_d = din("ts", [128, 256])
    bc_d = [din("bc_p", [128, 128]), din("bc_s", [128, 32])]
    bs_d = [din("bs_p", [128, 128]), din("bs_s", [128, 32])]
    y_p = nc.dram_tensor("y_p", [S, D], F32, kind="ExternalOutput").ap()
    y_s = nc.dram_tensor("y_s", [QS, D], F32, kind="ExternalOutput").ap()
    yf = [nc.dram_tensor("yf_p", [S, 512], BF16).ap(), nc.dram_tensor("yf_s", [QS, 512], BF16).ap()]
    x1scr = nc.dram_tensor("x1scr", [S + QS, D], F32).ap()
    modscr = nc.dram_tensor("modscr", [2, 6, D], F32).ap()
    yf_res = [[Res() for _ in range(S // 128)], [Res() for _ in range(QS // 128)]]
    x1_res = [Res() for _ in range((S + QS) // 128)]

    with ExitStack() as top:
        top.enter_context(nc.allow_low_precision("bf16 matmul operands, fp32 accumulation"))
        sems = SemTable(nc, top)

        uid = [0]

        def sb(st, name, shape, dt):
            uid[0] += 1
            return st.enter_context(nc.sbuf_tensor("%s_%d" % (name, uid[0]), list(shape), dt))

        def bufs(st, name, shape, dt, n):
            return [Buf(sb(st, "%s%d" % (name, i), shape, dt)) for i in range(n)]

        def psum_banks(st, nf=6):
            uid[0] += 1
            banks = [Buf(st.enter_context(nc.psum_tensor("pb%d_%d" % (i, uid[0]), [128, 512], F32))) for i in range(nf)]
            ptr = [Buf(st.enter_context(nc.psum_tensor("ptr%d_%d" % (i, uid[0]), [128, 1024], BF16))) for i in range(8 - nf)]
            if len(ptr) == 1:
                ptr = [ptr[0], ptr[0]]
            return banks, ptr

        class Wrap:
            def __init__(self, r):
                self.r = r

        ident = Buf(sb(top, "ident", [128, 128], BF16))
        ones_bf = Buf(sb(top, "ones_bf", [128, 128], BF16))
        epsb = Buf(sb(top, "epsb", [128, 1], F32))

        with ExitStack() as w1:
            win = Buf(sb(w1, "win", [128, KC, 1024], BF16))
            wuq = Buf(sb(w1, "wuq", [128, 2, 1024], BF16))
            wukv = Buf(sb(w1, "wukv", [128, 1024], BF16))
            wukT = Buf(sb(w1, "wukT", [128, 4, 128], BF16))
            gkv = Buf(sb(w1, "gkv", [128, 1], F32))
            abh = Buf(sb(w1, "abh", [128, 8, 128], BF16))
            t1u = Buf(sb(w1, "t1u", [128, 256], BF16))
            t1v = Buf(sb(w1, "t1v", [128, 256], BF16))
            ta = Buf(sb(w1, "ta", [128, 512], F32))
            tsn = Buf(sb(w1, "tsn", [128, 256], F32))
            bcs = [Buf(sb(w1, "bc0", [128, 128], BF16)), Buf(sb(w1, "bc1", [128, 32], BF16))]
            bss = [Buf(sb(w1, "bs0", [128, 128], BF16)), Buf(sb(w1, "bs1", [128, 32], BF16))]

            with ExitStack() as st:
                Sd = Sched(nc, sems)
                banks, ptr = psum_banks(st)
                ptr = ptr[0]
                identf = Buf(sb(st, "identf", [128, 128], F32))
                Sd.add("pool", lambda h: h.memset(identf.h[:], 0.0), writes=[identf])
                Sd.add("pool", lambda h: h.affine_select(out=identf.h[:], in_=identf.h[:], pattern=[[-1, 128]],
                                                         compare_op=ALU.not_equal, fill=1.0, base=0,
                                                         channel_multiplier=1), reads=[identf], writes=[identf])
                Sd.add("dve", lambda h: h.tensor_copy(out=ident.h[:], in_=identf.h[:]), reads=[identf], writes=[ident])
                Sd.add("pool", lambda h: h.memset(ones_bf.h[:], 1.0), writes=[ones_bf])
                Sd.add("pool", lambda h: h.memset(epsb.h[:], EPS), writes=[epsb])
                Sd.dma("pool", [lambda h: h.dma_start(out=win.h[:, :, 0:DIN], in_=w_in.rearrange("(kc p) n -> p kc n", p=128)),
                                lambda h: h.dma_start(out=win.h[:, :, 960:992], in_=w_in[:, 416:448].rearrange("(kc p) n -> p kc n", p=128)),
                                lambda h: h.dma_start(out=win.h[:, :, 992:1024], in_=w_in[:, 384:416].rearrange("(kc p) n -> p kc n", p=128))],
                       "set_win", writes=[win])
                Sd.add("pool", lambda h: h.tensor_scalar(out=win.h[:, :, 960:992], in0=win.h[:, :, 960:992], scalar1=-1.0,
                                                         scalar2=None, op0=ALU.mult), reads=[win], writes=[win])
                Sd.dma("pool", [lambda h: h.dma_start(out=wukv.h[:], in_=w_ukv),
                                lambda h: h.dma_start(out=t1u.h[:], in_=t1u_d),
                                lambda h: h.dma_start(out=t1v.h[:], in_=t1v_d),
                                lambda h: h.dma_start(out=bcs[0].h[:], in_=bc_d[0]),
                                lambda h: h.dma_start(out=bcs[1].h[:], in_=bc_d[1]),
                                lambda h: h.dma_start(out=bss[0].h[:], in_=bs_d[0]),
                                lambda h: h.dma_start(out=bss[1].h[:], in_=bs_d[1])],
                       "set_misc", writes=[wukv, t1u, t1v, bcs[0], bcs[1], bss[0], bss[1]])
                Sd.dma("sp", [lambda h: h.dma_start(out=ta.h[:], in_=ta_d),
                              lambda h: h.dma_start(out=tsn.h[:], in_=ts_d),
                              lambda h: h.dma_start(out=gkv.h[:], in_=gkvT)], "set_f32", writes=[ta, tsn, gkv])
                wuqf = Buf(sb(st, "wuqf", [128, 2, 768], F32))
                gq = Buf(sb(st, "gq", [128, 2], F32))
                Sd.dma("sp", [lambda h: h.dma_start(out=wuqf.h[:], in_=w_uq.rearrange("(kc p) n -> p kc n", p=128)),
                              lambda h: h.dma_start(out=gq.h[:], in_=gqT)], "set_wuq", writes=[wuqf, gq])
                for kc in range(2):
                    Sd.add("dve", lambda h, kc=kc: h.tensor_scalar(out=wuqf.h[:, kc, :], in0=wuqf.h[:, kc, :], scalar1=gq.h[:, kc:kc + 1],
                                                                   scalar2=None, op0=ALU.mult), reads=[wuqf, gq], writes=[wuqf])
                    Sd.add("dve", lambda h, kc=kc: h.tensor_copy(out=wuq.h[:, kc, 0:768], in_=wuqf.h[:, kc, :]), reads=[wuqf], writes=[wuq])
                    for hd in range(4):
                        b0 = hd * 192 + 128
                        Sd.add("dve", lambda h, kc=kc, hd=hd, b0=b0: h.tensor_scalar(
                            out=wuq.h[:, kc, 768 + hd * 64:768 + hd * 64 + 32], in0=wuqf.h[:, kc, b0 + 32:b0 + 64],
                            scalar1=-1.0, scalar2=None, op0=ALU.mult), reads=[wuqf], writes=[wuq])
                        Sd.add("dve", lambda h, kc=kc, hd=hd, b0=b0: h.tensor_copy(
                            out=wuq.h[:, kc, 768 + hd * 64 + 32:768 + hd * 64 + 64], in_=wuqf.h[:, kc, b0:b0 + 32]),
                            reads=[wuqf], writes=[wuq])
                for hd in range(4):
                    Sd.add("pe", lambda h, hd=hd: h.transpose(out=ptr.h[:, hd * 128:(hd + 1) * 128],
                                                              in_=wukv.h[:, hd * 256:hd * 256 + 128], identity=ident.h[:]),
                           reads=[wukv, ident], writes=[ptr])
                Sd.add("dve", lambda h: h.tensor_copy(out=wukT.h[:].rearrange("p a b -> p (a b)"), in_=ptr.h[:, 0:512]),
                       reads=[ptr], writes=[wukT])
                cbt = Buf(sb(st, "cbt", [128, 128], F32))
                sbt = Buf(sb(st, "sbt", [128, 128], F32))
                wblk = Buf(sb(st, "wblk", [128, 4, 128], F32))
                Sd.add("pool", lambda h: h.memset(wblk.h[:], 0.0), writes=[wblk])
                fl = [lambda h: h.dma_start(out=cbt.h[:], in_=cb_d), lambda h: h.dma_start(out=sbt.h[:], in_=sb_d)]
                for cc in range(4):
                    for gl in range(2):
                        fl.append(lambda h, cc=cc, gl=gl: h.dma_start(out=wblk.h[gl * 64:(gl + 1) * 64, cc, gl * 64:(gl + 1) * 64],
                                                                      in_=w_four[2 * cc + gl]))
                Sd.dma("sp", fl, "set_four", writes=[cbt, sbt, wblk])
                for cc in range(4):
                    bk = banks[cc % 2]
                    Sd.add("pe", lambda h, cc=cc, bk=bk: h.matmul(bk.h[:, 0:128], lhsT=cbt.h[:], rhs=wblk.h[:, cc, :], start=True, stop=True),
                           reads=[cbt, wblk], writes=[bk])
                    Sd.add("pe", lambda h, cc=cc, bk=bk: h.matmul(bk.h[:, 128:256], lhsT=sbt.h[:], rhs=wblk.h[:, cc, :], start=True, stop=True),
                           reads=[sbt, wblk], writes=[bk])
                    Sd.add("dve", lambda h, cc=cc, bk=bk: h.tensor_copy(
                        out=abh.h[:, 2 * cc:2 * cc + 2, :].rearrange("p f (u c) -> p u f c", u=2),
                        in_=bk.h[:, 0:256].rearrange("p (u f c) -> p u f c", u=2, f=2)), reads=[bk], writes=[abh])
                cts = Buf(sb(st, "cts", [128, KC, 2], F32))
                ctb = Buf(sb(st, "ctb", [128, KC, 2], BF16))
                modr = Buf(sb(st, "modr", [2, 6 * D], F32))
                bad = Buf(sb(st, "bad", [2, 6 * D], F32))
                gm2 = Buf(sb(st, "gm2", [2, D], F32))
                gf2 = Buf(sb(st, "gf2", [2, D], F32))
                Sd.dma("sp", [lambda h: h.dma_start(out=cts.h[:], in_=cT),
                              lambda h: h.dma_start(out=bad.h[:], in_=b_ada.partition_broadcast(2)),
                              lambda h: h.dma_start(out=gm2.h[:], in_=g_mix.partition_broadcast(2)),
                              lambda h: h.dma_start(out=gf2.h[:], in_=g_ffn.partition_broadcast(2))],
                       "set_c", writes=[cts, bad, gm2, gf2])
                Sd.add("act", lambda h: h.activation(out=ctb.h[:], in_=cts.h[:], func=AF.Silu), reads=[cts], writes=[ctb])
                wa = bufs(st, "wa", [128, KC, 512], BF16, 2)
                for nt in range(12):
                    wb = wa[nt % 2]
                    bk = banks[2 + nt % 2]
                    Sd.dma("pool", [lambda h, nt=nt, wb=wb: h.dma_start(
                        out=wb.h[:], in_=w_ada[:, nt * 512:(nt + 1) * 512].rearrange("(kc p) n -> p kc n", p=128))],
                        "wa%d" % (nt % 2), writes=[wb])
                    for kc in range(KC):
                        Sd.add("pe", lambda h, kc=kc, wb=wb, bk=bk: h.matmul(bk.h[0:2, :], lhsT=ctb.h[:, kc, :], rhs=wb.h[:, kc, :],
                                                                            start=(kc == 0), stop=(kc == KC - 1)),
                               reads=[ctb, wb], writes=[bk])
                    Sd.add("dve", lambda h, nt=nt, bk=bk: h.tensor_tensor(out=modr.h[:, nt * 512:(nt + 1) * 512], in0=bk.h[0:2, :],
                                                                         in1=bad.h[:, nt * 512:(nt + 1) * 512], op=ALU.add),
                           reads=[bk, bad], writes=[modr])
                Sd.add("dve", lambda h: h.scalar_tensor_tensor(out=modr.h[:, D:2 * D], in0=modr.h[:, D:2 * D], scalar=1.0, in1=gm2.h[:],
                                                               op0=ALU.add, op1=ALU.mult), reads=[modr, gm2], writes=[modr])
                Sd.add("dve", lambda h: h.scalar_tensor_tensor(out=modr.h[:, 4 * D:5 * D], in0=modr.h[:, 4 * D:5 * D], scalar=1.0, in1=gf2.h[:],
                                                               op0=ALU.add, op1=ALU.mult), reads=[modr, gf2], writes=[modr])
                mod_res = Wrap(Res())
                Sd.dma("pool", [lambda h: h.dma_start(out=modscr.rearrange("s j d -> s (j d)"), in_=modr.h[:])], "modst",
                       reads=[modr], writes=[mod_res])
                Sd.emit(0)

            def load_mod(Sd, dst, s, j):
                Sd.dma("sp", [lambda h: h.dma_start(out=dst.h[:], in_=modscr[s, j:j + 1, :].partition_broadcast(128))],
                       "mod_%d" % j, writes=[dst])

            def load_modT(Sd, dst, s, j):
                Sd.dma("sp", [lambda h: h.dma_start(out=dst.h[:], in_=modscr[s, j, :].rearrange("(kc p) -> p kc", p=128),
                                                    allow_slow_non_contiguous=True)], "modT_%d" % j, writes=[dst])

            def front_tile(Sd, xt, ss, xn):
                Sd.add("act", lambda h: h.activation(out=xn.h[:], in_=xt.h[:], func=AF.Square, accum_out=ss.h[:, 0:1]),
                       reads=[xt], writes=[xn, ss])
                Sd.add("act", lambda h: h.activation(out=ss.h[:, 1:2], in_=ss.h[:, 0:1], func=AF.Sqrt, scale=1.0 / D, bias=epsb.h[:, 0:1]),
                       reads=[ss, epsb], writes=[ss])
                Sd.add("dve", lambda h: h.reciprocal(out=ss.h[:, 2:3], in_=ss.h[:, 1:2]), reads=[ss], writes=[ss])
                Sd.add("act", lambda h: h.activation(out=xn.h[:], in_=xt.h[:], func=AF.Identity, scale=ss.h[:, 2:3]),
                       reads=[xt, ss], writes=[xn])

            def front_group(Sd, xns, aT, shT, ptrs, hT, G):
                for kp in range(KC // 2):
                    ptr = ptrs[kp % 2]
                    for k2 in range(2):
                        kc = 2 * kp + k2
                        for t in range(G):
                            Sd.add("pe", lambda h, kc=kc, k2=k2, t=t, ptr=ptr: h.transpose(
                                out=ptr.h[:, (k2 * G + t) * 128:(k2 * G + t + 1) * 128], in_=xns[t].h[:, kc * 128:(kc + 1) * 128],
                                identity=ident.h[:]), reads=[xns[t], ident], writes=[ptr])
                    for k2 in range(2):
                        kc = 2 * kp + k2
                        Sd.add("dve", lambda h, kc=kc, k2=k2, ptr=ptr: h.tensor_scalar(
                            out=hT.h[:, kc, 0:G * 128], in0=ptr.h[:, k2 * G * 128:(k2 + 1) * G * 128],
                            scalar1=aT.h[:, kc:kc + 1], scalar2=shT.h[:, kc:kc + 1], op0=ALU.mult, op1=ALU.add),
                            reads=[ptr, aT, shT], writes=[hT])

            for sq in range(2):
                x_seq = xs[sq]
                x_my = xs[0] if sq == 0 else xm_s
                n_my = S if sq == 0 else QS
                x1_off = 0 if sq == 0 else S
                with ExitStack() as sst:
                    kcT = Buf(sb(sst, "kcT", [128, S], BF16))
                    kpeT = Buf(sb(sst, "kpeT", [128, S], BF16))
                    vc = Buf(sb(sst, "vc", [128, NT, 128], BF16))
                    with ExitStack() as fst:
                        finT = Buf(sb(fst, "finT", [128, 4, S], BF16))
                        with ExitStack() as st:
                            Sd = Sched(nc, sems)
                            banks, ptrs = psum_banks(st)
                            ptr = ptrs[0]
                            a1T = Buf(sb(st, "a1T", [128, KC], F32))
                            sh1T = Buf(sb(st, "sh1T", [128, KC], F32))
                            load_modT(Sd, sh1T, sq, 0)
                            load_modT(Sd, a1T, sq, 1)
                            xts = bufs(st, "xt", [128, D], F32, 3)
                            sss = bufs(st, "ss", [128, 4], F32, 2)
                            xnb = bufs(st, "xn", [128, D], BF16, 8)
                            hTs = bufs(st, "hT", [128, KC, 512], BF16, 2)
                            rcs = bufs(st, "rc", [64, 512], F32, 2)
                            rss = bufs(st, "rs", [64, 512], F32, 2)
                            sqb = Buf(sb(st, "sqb", [128, 512], BF16))
                            rk = Buf(sb(st, "rk", [128, 512], F32))
                            t1 = Buf(sb(st, "t1", [64, 512], F32))
                            t2 = Buf(sb(st, "t2", [64, 512], F32))
                            def pa_tiles(g):
                                for t in range(4):
                                    ti = 4 * g + t
                                    xt = xts[ti % 3]
                                    r0 = g * 512 + t * 128
                                    Sd.dma("sp", [lambda h, xt=xt, r0=r0: h.dma_start(out=xt.h[:], in_=x_seq[r0:r0 + 128, :])],
                                           "xt%d" % (ti % 3), writes=[xt])
                                    front_tile(Sd, xt, sss[ti % 2], xnb[(g % 2) * 4 + t])

                            Sd.add("pool", lambda h: h.memset(kpeT.h[64:128, :], 0.0), writes=[kpeT])
                            pa_tiles(0)
                            for g in range(S // 512):
                                hT = hTs[g % 2]
                                rc, rs = rcs[g % 2], rss[g % 2]
                                Sd.dma("sp", [lambda h, g=g, rc=rc: h.dma_start(out=rc.h[:], in_=rope_c[:, g * 512:(g + 1) * 512]),
                                              lambda h, g=g, rs=rs: h.dma_start(out=rs.h[:], in_=rope_s[:, g * 512:(g + 1) * 512])],
                                       "rope%d" % (g % 2), writes=[rc, rs])
                                front_group(Sd, xnb[(g % 2) * 4:(g % 2) * 4 + 4], a1T, sh1T, ptrs, hT, 4)
                                if g + 1 < S // 512:
                                    pa_tiles(g + 1)
                                cols = slice(g * 512, (g + 1) * 512)

                                def zmm(bk, c0, m, hT=hT):
                                    for kc in range(KC):
                                        Sd.add("pe", lambda h, kc=kc: h.matmul(bk.h[0:m, :], lhsT=win.h[:, kc, c0:c0 + m], rhs=hT.h[:, kc, :],
                                                                              start=(kc == 0), stop=(kc == KC - 1)),
                                               reads=[win, hT], writes=[bk])
                                bkv, bsum = banks[0], banks[1]
                                zmm(bkv, 256, 128)
                                Sd.add("act", lambda h: h.activation(out=sqb.h[:], in_=bkv.h[:], func=AF.Square), reads=[bkv], writes=[sqb])
                                Sd.add("pe", lambda h: h.matmul(bsum.h[:], lhsT=ones_bf.h[:], rhs=sqb.h[:], start=True, stop=True),
                                       reads=[ones_bf, sqb], writes=[bsum])
                                Sd.add("act", lambda h: h.activation(out=rk.h[:], in_=bsum.h[:], func=AF.Sqrt, scale=1.0 / 128, bias=epsb.h[:, 0:1]),
                                       reads=[bsum, epsb], writes=[rk])
                                Sd.add("dve", lambda h: h.reciprocal(out=rk.h[:], in_=rk.h[:]), reads=[rk], writes=[rk])
                                Sd.add("dve", lambda h: h.tensor_scalar(out=rk.h[:], in0=rk.h[:], scalar1=gkv.h[:, 0:1], scalar2=None, op0=ALU.mult),
                                       reads=[rk, gkv], writes=[rk])
                                Sd.add("dve", lambda h, cols=cols: h.tensor_tensor(out=kcT.h[:, cols], in0=bkv.h[:], in1=rk.h[:], op=ALU.mult),
                                       reads=[bkv, rk], writes=[kcT])
                                for t in range(4):
                                    Sd.add("pe", lambda h, t=t, g=g: h.transpose(out=ptr.h[:, t * 128:(t + 1) * 128],
                                                                                in_=kcT.h[:, g * 512 + t * 128:g * 512 + (t + 1) * 128],
                                                                                identity=ident.h[:]), reads=[kcT, ident], writes=[ptr])
                                Sd.add("dve", lambda h, g=g: h.tensor_copy(out=vc.h[:, 4 * g:4 * g + 4, :].rearrange("p a b -> p (a b)"),
                                                                          in_=ptr.h[:, 0:512]), reads=[ptr], writes=[vc])
                                bpe, brot = banks[2], banks[3]
                                zmm(bpe, 384, 64)
                                zmm(brot, 960, 64)
                                Sd.add("dve", lambda h, rc=rc: h.tensor_tensor(out=t1.h[:], in0=bpe.h[0:64, :], in1=rc.h[:], op=ALU.mult),
                                       reads=[bpe, rc], writes=[t1])
                                Sd.add("dve", lambda h, rs=rs: h.tensor_tensor(out=t2.h[:], in0=brot.h[0:64, :], in1=rs.h[:], op=ALU.mult),
                                       reads=[brot, rs], writes=[t2])
                                Sd.add("pool", lambda h, cols=cols: h.tensor_tensor(out=kpeT.h[0:64, cols], in0=t1.h[:], in1=t2.h[:], op=ALU.add),
                                       reads=[t1, t2], writes=[kpeT])
                                for cc in range(4):
                                    bk = banks[4 + cc % 2]
                                    zmm(bk, 448 + cc * 128, 128)
                                    Sd.add("act", lambda h, cc=cc, bk=bk, cols=cols: h.activation(out=finT.h[:, cc, cols], in_=bk.h[:], func=AF.Copy),
                                           reads=[bk], writes=[finT])
                            Sd.emit(1 + 3 * sq)
                        with ExitStack() as st:
                            Sd = Sched(nc, sems)
                            banks, ptrs = psum_banks(st)
                            nk2 = N2 if sq == 0 else cfg.NQ2
                            ncol2 = PC * nk2
                            xall = Buf(sb(st, "xall", [128, 128, N2], BF16))
                            ytok = Buf(sb(st, "ytok", [128, nk2, 64], BF16))
                            tms = bufs(st, "tm", [128, 512], F32, 3)
                            tus = bufs(st, "tu", [128, 512], F32, 3)
                            y2s = bufs(st, "y2", [128, 512], BF16, 3)
                            bcx, bsx = bcs[sq], bss[sq]
                            it = 0
                            for hc in range(8):
                                cc = hc // 2
                                for s2 in range(0, N2, 4):
                                    bk = banks[(s2 // 4) % 2]
                                    for j in range(4):
                                        Sd.add("pe", lambda h, s2=s2, j=j, bk=bk, cc=cc, hc=hc: h.matmul(
                                            bk.h[:, j * 128:(j + 1) * 128],
                                            lhsT=finT.h[:, cc, :].rearrange("p (a b) -> p b a", b=N2)[:, s2 + j, :],
                                            rhs=abh.h[:, hc, :], start=True, stop=True), reads=[finT, abh], writes=[bk])
                                    Sd.add("dve", lambda h, s2=s2, bk=bk: h.tensor_copy(
                                        out=xall.h[:, :, s2:s2 + 4].rearrange("p c s -> p s c"),
                                        in_=bk.h[:].rearrange("p (s c) -> p s c", s=4)),
                                        reads=[bk], writes=[xall])
                                ngh = 64 // PC
                                def s1(gp, it):
                                    b1 = banks[2 + it % 2]
                                    tm, tu, y2 = tms[it % 3], tus[it % 3], y2s[it % 3]
                                    for j in range(2):
                                        c0 = (gp + j) * PC
                                        lu = xall.h[:, c0:c0 + PC, :].rearrange("p c s -> p (c s)")
                                        lv = xall.h[:, 64 + c0:64 + c0 + PC, :].rearrange("p c s -> p (c s)")
                                        Sd.add("pe", lambda h, j=j, lu=lu, b1=b1: h.matmul(b1.h[:, j * 256:(j + 1) * 256], lhsT=lu, rhs=t1u.h[:],
                                                                                          start=True, stop=False), reads=[xall, t1u], writes=[b1])
                                        Sd.add("pe", lambda h, j=j, lv=lv, b1=b1: h.matmul(b1.h[:, j * 256:(j + 1) * 256], lhsT=lv, rhs=t1v.h[:],
                                                                                          start=False, stop=True), reads=[xall, t1v], writes=[b1])
                                    Sd.add("dve", lambda h, b1=b1, tm=tm: h.tensor_tensor(out=tm.h[:], in0=b1.h[:], in1=ta.h[:], op=ALU.mult),
                                           reads=[b1, ta], writes=[tm])
                                    y1v = b1.h[:].rearrange("p (g r k) -> p g r k", g=2, r=2)
                                    tuv = tu.h[:].rearrange("p (g r k) -> p g r k", g=2, r=2)
                                    tsb = tsn.h[:].rearrange("p (g k) -> p g k", g=2)
                                    Sd.add("dve", lambda h, y1v=y1v, tuv=tuv, tsb=tsb: h.tensor_tensor(out=tuv[:, :, 0, :], in0=y1v[:, :, 1, :], in1=tsb,
                                                                                                  op=ALU.mult), reads=[b1, tsn], writes=[tu])
                                    Sd.add("dve", lambda h, y1v=y1v, tuv=tuv, tsb=tsb: h.tensor_tensor(out=tuv[:, :, 1, :], in0=y1v[:, :, 0, :], in1=tsb,
                                                                                                  op=ALU.mult), reads=[b1, tsn], writes=[tu])
                                    tmv = tm.h[:].rearrange("p (g r k) -> p g r k", g=2, r=2)
                                    y2v = y2.h[:].rearrange("p (g r k) -> p g r k", g=2, r=2)
                                    Sd.add("pool", lambda h, tmv=tmv, tuv=tuv, y2v=y2v: h.tensor_tensor(out=y2v[:, :, 0, :], in0=tmv[:, :, 0, :],
                                                                                                    in1=tuv[:, :, 0, :], op=ALU.add),
                                           reads=[tm, tu], writes=[y2])
                                    Sd.add("pool", lambda h, tmv=tmv, tuv=tuv, y2v=y2v: h.tensor_tensor(out=y2v[:, :, 1, :], in0=tmv[:, :, 1, :],
                                                                                                    in1=tuv[:, :, 1, :], op=ALU.subtract),
                                           reads=[tm, tu], writes=[y2])

                                def s2(gp, it):
                                    b2 = banks[4 + it % 2]
                                    y2 = y2s[it % 3]
                                    for j in range(2):
                                        Sd.add("pe", lambda h, j=j, y2=y2, b2=b2: h.matmul(b2.h[:, j * ncol2:(j + 1) * ncol2], lhsT=y2.h[:, j * 256:j * 256 + 128],
                                                                                          rhs=bcx.h[:], start=True, stop=False), reads=[y2, bcx], writes=[b2])
                                        Sd.add("pe", lambda h, j=j, y2=y2, b2=b2: h.matmul(b2.h[:, j * ncol2:(j + 1) * ncol2], lhsT=y2.h[:, j * 256 + 128:j * 256 + 256],
                                                                                          rhs=bsx.h[:], start=False, stop=True), reads=[y2, bsx], writes=[b2])
                                    c0 = gp * PC
                                    Sd.add("act", lambda h, b2=b2, c0=c0: h.activation(
                                        out=ytok.h[:, :, c0:c0 + 2 * PC].rearrange("p k c -> p c k"),
                                        in_=b2.h[:, 0:2 * ncol2].rearrange("p (c k) -> p c k", k=nk2), func=AF.Copy),
                                        reads=[b2], writes=[ytok])

                                gps = list(range(0, ngh, 2))
                                for idx in range(len(gps) + 1):
                                    if idx < len(gps):
                                        s1(gps[idx], it + idx)
                                    if idx >= 1:
                                        s2(gps[idx - 1], it + idx - 1)
                                it += len(gps)
                                kstep = min(8, nk2)
                                Sd.dma("pool", [lambda h, hc=hc, k0=k0: h.dma_start(
                                    out=yf[sq].rearrange("(k2 k1) c -> k1 k2 c", k1=128)[:, k0:k0 + kstep, hc * 64:(hc + 1) * 64],
                                    in_=ytok.h[:, k0:k0 + kstep, :]) for k0 in range(0, nk2, kstep)],
                                    "yfst", reads=[ytok], writes=[Wrap(r) for r in yf_res[sq]])
                            Sd.emit(2 + 3 * sq)
                    with ExitStack() as st:
                        Sd = Sched(nc, sems)
                        banks, ptrs = psum_banks(st, 7)
                        ptr = ptrs[0]
                        wout = Buf(sb(st, "wout", [128, KC, D], BF16))
                        Sd.dma("pool", [lambda h: h.dma_start(out=wout.h[:], in_=w_out.rearrange("(kc p) n -> p kc n", p=128))],
                               "set_wout", writes=[wout])
                        a1T = Buf(sb(st, "a1T", [128, KC], F32))
                        sh1T = Buf(sb(st, "sh1T", [128, KC], F32))
                        ga1b = Buf(sb(st, "ga1b", [128, D], F32))
                        load_modT(Sd, sh1T, sq, 0)
                        load_modT(Sd, a1T, sq, 1)
                        load_mod(Sd, ga1b, sq, 2)
                        for kc in range(KC):
                            Sd.add("pool" if kc % 2 else "dve", lambda h, kc=kc: h.tensor_tensor(out=wout.h[:, kc, :], in0=wout.h[:, kc, :], in1=ga1b.h[:], op=ALU.mult),
                                   reads=[wout, ga1b], writes=[wout])
                        xts = bufs(st, "xt", [128, D], F32, 8)
                        sss = bufs(st, "ss", [128, 4], F32, 2)
                        xnb = bufs(st, "xn", [128, D], BF16, 4)
                        hTs = bufs(st, "hT", [128, KC, 512], BF16, 1)
                        rcs = bufs(st, "rc", [64, 512], F32, 1)
                        rss = bufs(st, "rs", [64, 512], F32, 1)
                        sqb2 = Buf(sb(st, "sqb2", [128, 2, 512], BF16))
                        rq = Buf(sb(st, "rq", [128, 512], F32))
                        qlT = Buf(sb(st, "qlT", [128, 2, 512], BF16))
                        qn = bufs(st, "qn", [128, 512], BF16, 2)
                        qcs = bufs(st, "qc", [128, 4, 512], BF16, 2)
                        qps = bufs(st, "qp", [128, 4, 512], BF16, 2)
                        for qpb in qps:
                            Sd.add("pool", lambda h, qpb=qpb: h.memset(qpb.h[64:128, :, :], 0.0), writes=[qpb])
                        cq = Buf(sb(st, "cq", [64, 512], F32))
                        sqr = Buf(sb(st, "sqr", [64, 512], F32))
                        t1 = Buf(sb(st, "t1", [64, 512], F32))
                        t2 = Buf(sb(st, "t2", [64, 512], F32))
                        pts = bufs(st, "pt", [128, 512], BF16, 6)
                        rec = Buf(ga1b.h[:, 0:512])
                        rec.r = ga1b.r
                        accD = Buf(sb(st, "accD", [128, 512], F32))
                        accP = Buf(sb(st, "accP", [128, 512], F32))
                        acct = Buf(ga1b.h[:, 512:1024])
                        acct.r = ga1b.r
                        ones_f = Buf(sb(st, "ones_f", [128, 128], F32))
                        Sd.add("pool", lambda h: h.memset(ones_f.h[:], 1.0), writes=[ones_f])
                        onT = Buf(sb(st, "onT", [128, 4, 512], BF16))
                        ymT = Buf(sb(st, "ymT", [128, KC, 512], BF16))
                        yfts = bufs(st, "yft", [128, 512], BF16, 2)
                        ng = n_my // 512
                        bgb = [banks[3], banks[6]]
                        bgi = [0]

                        def bgbank():
                            bgi[0] += 1
                            return bgb[bgi[0] % 2]

                        def prep(g):
                            hT = hTs[0]
                            rc, rs = rcs[0], rss[0]
                            qc, qp = qcs[g % 2], qps[g % 2]
                            pos0 = g * 512
                            rcd, rsd = (rope_c, rope_s) if sq == 0 else (rope_cm, rope_sm)
                            Sd.dma("sp", [lambda h: h.dma_start(out=rc.h[:], in_=rcd[:, pos0:pos0 + 512]),
                                          lambda h: h.dma_start(out=rs.h[:], in_=rsd[:, pos0:pos0 + 512])], "ropec", writes=[rc, rs])
                            for t in range(4):
                                xt = xts[(4 * g + t) % 8]
                                r0 = g * 512 + t * 128
                                Sd.dma("sp", [lambda h, xt=xt, r0=r0: h.dma_start(out=xt.h[:], in_=x_my[r0:r0 + 128, :])],
                                       "xc%d" % ((4 * g + t) % 8), writes=[xt])
                                front_tile(Sd, xt, sss[(4 * g + t) % 2], xnb[t])
                                yield
                            for kp in range(KC // 2):
                                ptr = ptrs[0]
                                for k2 in range(2):
                                    kc = 2 * kp + k2
                                    for t in range(4):
                                        Sd.add("pe", lambda h, kc=kc, k2=k2, t=t, ptr=ptr: h.transpose(
                                            out=ptr.h[:, (k2 * 4 + t) * 128:(k2 * 4 + t + 1) * 128], in_=xnb[t].h[:, kc * 128:(kc + 1) * 128],
                                            identity=ident.h[:]), reads=[xnb[t], ident], writes=[ptr])
                                yield
                                for k2 in range(2):
                                    kc = 2 * kp + k2
                                    Sd.add("dve", lambda h, kc=kc, k2=k2, ptr=ptr: h.tensor_scalar(
                                        out=hT.h[:, kc, 0:512], in0=ptr.h[:, k2 * 512:(k2 + 1) * 512],
                                        scalar1=a1T.h[:, kc:kc + 1], scalar2=sh1T.h[:, kc:kc + 1], op0=ALU.mult, op1=ALU.add),
                                        reads=[ptr, a1T, sh1T], writes=[hT])
                                yield
                            for c2 in range(2):
                                bk = bgbank()
                                for kc in range(KC):
                                    Sd.add("pe", lambda h, kc=kc, c2=c2, bk=bk: h.matmul(bk.h[:], lhsT=win.h[:, kc, c2 * 128:(c2 + 1) * 128], rhs=hT.h[:, kc, :],
                                                                                      start=(kc == 0), stop=(kc == KC - 1)), reads=[win, hT], writes=[bk])
                                yield
                                Sd.add("act", lambda h, c2=c2, bk=bk: h.activation(out=sqb2.h[:, c2, :], in_=bk.h[:], func=AF.Square), reads=[bk], writes=[sqb2])
                                Sd.add("act", lambda h, c2=c2, bk=bk: h.activation(out=qlT.h[:, c2, :], in_=bk.h[:], func=AF.Copy), reads=[bk], writes=[qlT])
                                yield
                            yield
                            bk = bgbank()
                            for c2 in range(2):
                                Sd.add("pe", lambda h, c2=c2, bk=bk: h.matmul(bk.h[:], lhsT=ones_bf.h[:], rhs=sqb2.h[:, c2, :], start=(c2 == 0), stop=(c2 == 1)),
                                       reads=[ones_bf, sqb2], writes=[bk])
                            yield
                            Sd.add("act", lambda h, bk=bk: h.activation(out=rq.h[:], in_=bk.h[:], func=AF.Sqrt, scale=1.0 / 256, bias=epsb.h[:, 0:1]),
                                   reads=[bk, epsb], writes=[rq])
                            yield
                            Sd.add("dve", lambda h: h.reciprocal(out=rq.h[:], in_=rq.h[:]), reads=[rq], writes=[rq])
                            Sd.add("dve", lambda h: h.tensor_tensor(out=cq.h[:], in0=rc.h[:], in1=rq.h[0:64, :], op=ALU.mult), reads=[rc, rq], writes=[cq])
                            Sd.add("dve", lambda h: h.tensor_tensor(out=sqr.h[:], in0=rs.h[:], in1=rq.h[0:64, :], op=ALU.mult), reads=[rs, rq], writes=[sqr])
                            yield
                            yield
                            for hd in range(4):
                                qnb = qn[hd % 2]
                                bn = bgbank()
                                for kc in range(2):
                                    Sd.add("pe", lambda h, kc=kc, hd=hd, bn=bn: h.matmul(bn.h[:], lhsT=wuq.h[:, kc, hd * 192:hd * 192 + 128], rhs=qlT.h[:, kc, :],
                                                                                        start=(kc == 0), stop=(kc == 1)), reads=[wuq, qlT], writes=[bn])
                                yield
                                Sd.add("dve", lambda h, qnb=qnb, bn=bn: h.tensor_tensor(out=qnb.h[:], in0=bn.h[:], in1=rq.h[:], op=ALU.mult), reads=[bn, rq], writes=[qnb])
                                yield
                                yield
                                bc_ = bgbank()
                                Sd.add("pe", lambda h, hd=hd, qnb=qnb, bc_=bc_: h.matmul(bc_.h[:], lhsT=wukT.h[:, hd, :], rhs=qnb.h[:], start=True, stop=True),
                                       reads=[wukT, qnb], writes=[bc_])
                                yield
                                Sd.add("act", lambda h, hd=hd, bc_=bc_: h.activation(out=qc.h[:, hd, :], in_=bc_.h[:], func=AF.Copy), reads=[bc_], writes=[qc])
                                yield
                                bp = bgbank()
                                for kc in range(2):
                                    Sd.add("pe", lambda h, kc=kc, hd=hd, bp=bp: h.matmul(bp.h[0:64, :], lhsT=wuq.h[:, kc, hd * 192 + 128:hd * 192 + 192], rhs=qlT.h[:, kc, :],
                                                                                        start=(kc == 0), stop=(kc == 1)), reads=[wuq, qlT], writes=[bp])
                                yield
                                Sd.add("dve", lambda h, bp=bp: h.tensor_tensor(out=t1.h[:], in0=bp.h[0:64, :], in1=cq.h[:], op=ALU.mult), reads=[bp, cq], writes=[t1])
                                yield
                                br = bgbank()
                                for kc in range(2):
                                    Sd.add("pe", lambda h, kc=kc, hd=hd, br=br: h.matmul(br.h[0:64, :], lhsT=wuq.h[:, kc, 768 + hd * 64:768 + hd * 64 + 64], rhs=qlT.h[:, kc, :],
                                                                                        start=(kc == 0), stop=(kc == 1)), reads=[wuq, qlT], writes=[br])
                                yield
                                Sd.add("dve", lambda h, br=br: h.tensor_tensor(out=t2.h[:], in0=br.h[0:64, :], in1=sqr.h[:], op=ALU.mult), reads=[br, sqr], writes=[t2])
                                yield
                                Sd.add("pool", lambda h, hd=hd, qp=qp: h.tensor_tensor(out=qp.h[0:64, hd, :], in0=t1.h[:], in1=t2.h[:], op=ALU.add), reads=[t1, t2], writes=[qp])
                                yield

                        def post(g):
                            gx = [xts[(4 * g + t) % 8] for t in range(4)]
                            for t in range(4):
                                yft = yfts[t % 2]
                                r0 = g * 512 + t * 128
                                Sd.dma("sp", [lambda h, yft=yft, r0=r0: h.dma_start(out=yft.h[:], in_=yf[sq][r0:r0 + 128, :])], "yft%d" % (t % 2),
                                       reads=[Wrap(yf_res[sq][r0 // 128])], writes=[yft])
                                yield
                                ptr = ptrs[0]
                                for cc in range(4):
                                    Sd.add("pe", lambda h, cc=cc, yft=yft, ptr=ptr: h.transpose(out=ptr.h[:, cc * 128:(cc + 1) * 128], in_=yft.h[:, cc * 128:(cc + 1) * 128],
                                                                                              identity=ident.h[:]), reads=[yft, ident], writes=[ptr])
                                yield
                                Sd.add("act", lambda h, t=t, ptr=ptr: h.activation(out=ymT.h[:, 4:8, t * 128:(t + 1) * 128],
                                                                                   in_=ptr.h[:, 0:512].rearrange("p (k c) -> p k c", k=4), func=AF.Copy),
                                       reads=[ptr], writes=[ymT])
                                yield
                            yield
                            for t in range(4):
                                for n2 in range(2):
                                    bk = bgbank()
                                    for kc in range(KC):
                                        Sd.add("pe", lambda h, kc=kc, t=t, n2=n2, bk=bk: h.matmul(bk.h[:], lhsT=ymT.h[:, kc, t * 128:(t + 1) * 128],
                                                                                                rhs=wout.h[:, kc, n2 * 512:(n2 + 1) * 512],
                                                                                                start=(kc == 0), stop=(kc == KC - 1)), reads=[ymT, wout], writes=[bk])
                                    yield
                                    Sd.add("dve", lambda h, n2=n2, bk=bk, xg=gx[t]: h.tensor_tensor(out=xg.h[:, n2 * 512:(n2 + 1) * 512], in0=bk.h[:],
                                                                                               in1=xg.h[:, n2 * 512:(n2 + 1) * 512], op=ALU.add),
                                           reads=[bk, gx[t]], writes=[gx[t]])
                                    yield
                                r0 = x1_off + g * 512 + t * 128
                                Sd.dma("pool", [lambda h, xg=gx[t], r0=r0: h.dma_start(out=x1scr[r0:r0 + 128, :], in_=xg.h[:])], "x1st%d" % ((4 * g + t) % 8),
                                       reads=[gx[t]], writes=[Wrap(x1_res[r0 // 128])])

                        def finish_head(hd):
                            bo, bm = banks[4 + hd % 2], bgbank()
                            Sd.add("pe", lambda h, bm=bm: h.matmul(bm.h[:], lhsT=ones_f.h[:], rhs=acct.h[:], start=True, stop=True),
                                   reads=[ones_f, acct], writes=[bm])
                            Sd.add("dve", lambda h, bm=bm: h.reciprocal(out=rec.h[:], in_=bm.h[:]), reads=[bm], writes=[rec])
                            Sd.add("dve", lambda h, hd=hd, bo=bo: h.tensor_tensor(out=onT.h[:, hd, :], in0=bo.h[:], in1=rec.h[:], op=ALU.mult),
                                   reads=[bo, rec], writes=[onT])

                        def vup_head(hd):
                            bk = bgbank()
                            Sd.add("pe", lambda h, hd=hd, bk=bk: h.matmul(bk.h[:], lhsT=wukv.h[:, hd * 256 + 128:hd * 256 + 256], rhs=onT.h[:, hd, :], start=True, stop=True),
                                   reads=[wukv, onT], writes=[bk])
                            Sd.add("act", lambda h, hd=hd, bk=bk: h.activation(out=ymT.h[:, hd, :], in_=bk.h[:], func=AF.Copy), reads=[bk], writes=[ymT])

                        for _ in prep(0):
                            pass
                        blocks = [(g, hd, j) for g in range(ng) for hd in range(4) for j in range(NT)]
                        LAG = 3
                        NS = 3
                        pending = []
                        bg = []
                        for i in range(len(blocks) + LAG + 16):
                            if i < len(blocks):
                                g, hd, j = blocks[i]
                                if hd == 0 and j == 0:
                                    for kind, gg, gen in bg:
                                        for _ in gen:
                                            pass
                                    bg = []
                                    if g == 0 and ng > 1:
                                        bg.append(("prep", 1, prep(1)))
                                qc, qp = qcs[g % 2], qps[g % 2]
                                bs_ = banks[i % NS]
                                pt = pts[i % 6]
                                Sd.add("pe", lambda h, j=j, hd=hd, bs_=bs_, qc=qc: h.matmul(bs_.h[:], lhsT=kcT.h[:, j * 128:(j + 1) * 128], rhs=qc.h[:, hd, :],
                                                                                    start=True, stop=False), reads=[kcT, qc], writes=[bs_])
                                Sd.add("pe", lambda h, j=j, hd=hd, bs_=bs_, qp=qp: h.matmul(bs_.h[:], lhsT=kpeT.h[:, j * 128:(j + 1) * 128], rhs=qp.h[:, hd, :],
                                                                                    start=False, stop=True), reads=[kpeT, qp], writes=[bs_])
                                Sd.add("act", lambda h, bs_=bs_, pt=pt: h.activation(out=pt.h[:], in_=bs_.h[:], func=AF.Exp, scale=SCALE), reads=[bs_], writes=[pt])
                                eng, acc = ("dve", accD) if j % 2 == 0 else ("pool", accP)
                                if j < 2:
                                    Sd.add(eng, lambda h, pt=pt, acc=acc: h.tensor_copy(out=acc.h[:], in_=pt.h[:]), reads=[pt], writes=[acc])
                                else:
                                    Sd.add(eng, lambda h, pt=pt, acc=acc: h.tensor_tensor(out=acc.h[:], in0=acc.h[:], in1=pt.h[:], op=ALU.add),
                                           reads=[pt, acc], writes=[acc])
                                if j == NT - 1:
                                    Sd.add("pool", lambda h: h.tensor_tensor(out=acct.h[:], in0=accD.h[:], in1=accP.h[:], op=ALU.add),
                                           reads=[accD, accP], writes=[acct])
                            if LAG <= i < len(blocks) + LAG:
                                g2, hd2, j2 = blocks[i - LAG]
                                bo = banks[4 + hd2 % 2]
                                pt2 = pts[(i - LAG) % 6]
                                Sd.add("pe", lambda h, j2=j2, pt2=pt2, bo=bo: h.matmul(bo.h[:], lhsT=vc.h[:, j2, :], rhs=pt2.h[:], start=(j2 == 0), stop=(j2 == NT - 1)),
                                       reads=[vc, pt2], writes=[bo])
                                if j2 == NT - 1:
                                    pending.append([i + 4, hd2, 0, g2])
                            if pending and pending[0][0] <= i:
                                if pending[0][2] == 0:
                                    finish_head(pending[0][1])
                                    pending[0][2] = 1
                                    pending[0][0] = i + 6
                                else:
                                    pg = pending[0][3]
                                    while bg and bg[0][0] == "post" and bg[0][1] < pg:
                                        for _ in bg[0][2]:
                                            pass
                                        bg.pop(0)
                                    ph = pending.pop(0)[1]
                                    vup_head(ph)
                                    if ph == 3:
                                        bg.append(("post", pg, post(pg)))
                                        if pg + 2 < ng:
                                            bg.append(("prep", pg + 2, prep(pg + 2)))
                            elif bg and i % 2 == 1:
                                try:
                                    next(bg[0][2])
                                except StopIteration:
                                    bg.pop(0)
                        assert not pending
                        for kind, gg, gen in bg:
                            for _ in gen:
                                pass
                        Sd.emit(3 + 3 * sq)

        with ExitStack() as st:
            Sd = Sched(nc, sems)
            banks, ptrs = psum_banks(st)
            wg = Buf(sb(st, "wg", [128, KC, DFF], BF16))
            wu = Buf(sb(st, "wu", [128, KC, DFF], BF16))
            wd = Buf(sb(st, "wd", [128, FC, D], BF16))
            for kc in range(KC):
                Sd.dma("pool", [lambda h, kc=kc: h.dma_start(out=wg.h[:, kc, :], in_=w_gate[kc * 128:(kc + 1) * 128, :]),
                                lambda h, kc=kc: h.dma_start(out=wu.h[:, kc, :], in_=w_up[kc * 128:(kc + 1) * 128, :])], "set_wgu", writes=[wg, wu])
            for f0 in range(0, FC, 2):
                Sd.dma("pool", [lambda h, f0=f0: h.dma_start(out=wd.h[:, f0:f0 + 2, :], in_=w_down[f0 * 128:(f0 + 2) * 128, :].rearrange("(f p) n -> p f n", p=128))],
                       "set_wd", writes=[wd])
            gfb = Buf(sb(st, "gfb", [128, D], F32))
            Sd.dma("sp", [lambda h: h.dma_start(out=gfb.h[:], in_=g_final.partition_broadcast(128))], "set_gf", writes=[gfb])
            a2T = Buf(sb(st, "a2T", [128, KC], F32))
            sh2T = Buf(sb(st, "sh2T", [128, KC], F32))
            ga2b = Buf(sb(st, "ga2b", [128, D], F32))
            xts = bufs(st, "xt", [128, D], F32, 3)
            junk = Buf(sb(st, "junk", [128, D], BF16))
            sss = bufs(st, "ss", [128, 4], F32, 2)
            tmps = bufs(st, "tmp", [128, D], F32, 2)
            hbs = bufs(st, "hb", [128, D], BF16, 4)
            hTs = bufs(st, "hT", [128, KC, 256], BF16, 2)
            sgs = bufs(st, "sg", [128, 256], F32, 2)
            actT = Buf(sb(st, "actT", [128, FC, 256], BF16))
            mts = bufs(st, "mt", [128, 512], F32, 2)
            yts = bufs(st, "yt", [128, D], F32, 2)
            ots = tmps
            ss3 = bufs(st, "ss3", [128, 4], F32, 2)
            ti = 0
            ngrp = (S + QS) // 256
            for g in range(ngrp):
                r_base = g * 256
                sq = 0 if r_base < S else 1
                if g == 0 or r_base == S:
                    load_modT(Sd, sh2T, sq, 3)
                    load_modT(Sd, a2T, sq, 4)
                    load_mod(Sd, ga2b, sq, 5)
                hT = hTs[g % 2]
                gx = [xts[(2 * g + t) % 3] for t in range(2)]
                for t in range(2):
                    xt = gx[t]
                    r0 = r_base + t * 128
                    Sd.dma("sp", [lambda h, xt=xt, r0=r0: h.dma_start(out=xt.h[:], in_=x1scr[r0:r0 + 128, :])], "xd%d" % ((2 * g + t) % 3),
                           reads=[Wrap(x1_res[r0 // 128])], writes=[xt])
                    front_tile(Sd, xt, sss[ti % 2], hbs[(g % 2) * 2 + t])
                    ti += 1
                front_group(Sd, hbs[(g % 2) * 2:(g % 2) * 2 + 2], a2T, sh2T, ptrs, hT, 2)
                for fc in range(FC):
                    bk = banks[fc % 3]
                    sg = sgs[fc % 2]
                    for kc in range(KC):
                        Sd.add("pe", lambda h, kc=kc, fc=fc, bk=bk, hT=hT: h.matmul(bk.h[:, 0:256], lhsT=wg.h[:, kc, fc * 128:(fc + 1) * 128], rhs=hT.h[:, kc, :],
                                                                            start=(kc == 0), stop=(kc == KC - 1)), reads=[wg, hT], writes=[bk])
                    for kc in range(KC):
                        Sd.add("pe", lambda h, kc=kc, fc=fc, bk=bk, hT=hT: h.matmul(bk.h[:, 256:512], lhsT=wu.h[:, kc, fc * 128:(fc + 1) * 128], rhs=hT.h[:, kc, :],
                                                                            start=(kc == 0), stop=(kc == KC - 1)), reads=[wu, hT], writes=[bk])
                    Sd.add("act", lambda h, bk=bk, sg=sg: h.activation(out=sg.h[:], in_=bk.h[:, 0:256], func=AF.Silu), reads=[bk], writes=[sg])
                    Sd.add("dve", lambda h, fc=fc, bk=bk, sg=sg: h.tensor_tensor(out=actT.h[:, fc, :], in0=bk.h[:, 256:512], in1=sg.h[:], op=ALU.mult),
                           reads=[bk, sg], writes=[actT])
                for t in range(2):
                    yt = yts[t]
                    ot = ots[t]
                    s3 = ss3[t]
                    for n2 in range(2):
                        bk = banks[3 + (2 * t + n2) % 3]
                        mt = mts[n2]
                        for fc in range(FC):
                            Sd.add("pe", lambda h, fc=fc, t=t, n2=n2, bk=bk: h.matmul(bk.h[:], lhsT=actT.h[:, fc, t * 128:(t + 1) * 128],
                                                                                    rhs=wd.h[:, fc, n2 * 512:(n2 + 1) * 512],
                                                                                    start=(fc == 0), stop=(fc == FC - 1)), reads=[actT, wd], writes=[bk])
                        Sd.add("dve", lambda h, n2=n2, bk=bk, mt=mt: h.tensor_tensor(out=mt.h[:], in0=bk.h[:], in1=ga2b.h[:, n2 * 512:(n2 + 1) * 512], op=ALU.mult),
                               reads=[bk, ga2b], writes=[mt])
                        Sd.add("pool", lambda h, n2=n2, mt=mt, xg=gx[t], yt=yt: h.tensor_tensor(out=yt.h[:, n2 * 512:(n2 + 1) * 512], in0=mt.h[:],
                                                                                       in1=xg.h[:, n2 * 512:(n2 + 1) * 512], op=ALU.add),
                               reads=[mt, gx[t]], writes=[yt])
                    Sd.add("act", lambda h, yt=yt, s3=s3: h.activation(out=junk.h[:], in_=yt.h[:], func=AF.Square, accum_out=s3.h[:, 0:1]),
                           reads=[yt], writes=[junk, s3])
                    Sd.add("act", lambda h, s3=s3: h.activation(out=s3.h[:, 1:2], in_=s3.h[:, 0:1], func=AF.Sqrt, scale=1.0 / D, bias=epsb.h[:, 0:1]),
                           reads=[s3, epsb], writes=[s3])
                    Sd.add("dve", lambda h, s3=s3: h.reciprocal(out=s3.h[:, 2:3], in_=s3.h[:, 1:2]), reads=[s3], writes=[s3])
                    Sd.add("act", lambda h, yt=yt, ot=ot, s3=s3: h.activation(out=ot.h[:], in_=yt.h[:], func=AF.Identity, scale=s3.h[:, 2:3]),
                           reads=[yt, s3], writes=[ot])
                    Sd.add("pool", lambda h, ot=ot: h.tensor_tensor(out=ot.h[:], in0=ot.h[:], in1=gfb.h[:], op=ALU.mult), reads=[ot, gfb], writes=[ot])
                    r0 = r_base + t * 128
                    if r0 < S:
                        dst = y_p[r0:r0 + 128, :]
                    else:
                        dst = y_s[r0 - S:r0 - S + 128, :]
                    Sd.dma("pool", [lambda h, ot=ot, dst=dst: h.dma_start(out=dst, in_=ot.h[:])], "yst%d" % t, reads=[ot])
            Sd.emit(7)
    return nc


def _tables(cfg):
    S, N2, PC = cfg.S, cfg.N2, cfg.PC
    f64 = np.float64
    pos = np.arange(S, dtype=np.float32)
    inv = (1.0 / (np.float32(10000.0) ** (np.arange(0, 64, 2, dtype=np.float32) / np.float32(64)))).astype(np.float32)
    ang = (pos[None, :] * inv[:, None]).astype(np.float32)
    rope_c = np.concatenate([np.cos(ang), np.cos(ang)], 0).astype(np.float32)
    rope_s = np.concatenate([np.sin(ang), np.sin(ang)], 0).astype(np.float32)
    norm = 1.0 / math.sqrt(S * 64)
    m = np.arange(64, dtype=f64)
    C64 = np.cos(2 * np.pi * np.outer(m, m) / 64) * norm
    S64 = np.sin(2 * np.pi * np.outer(m, m) / 64) * norm
    cb = np.zeros((128, 128), f64)
    sbm = np.zeros((128, 128), f64)
    for gl in range(2):
        cb[gl * 64:(gl + 1) * 64, gl * 64:(gl + 1) * 64] = C64
        sbm[gl * 64:(gl + 1) * 64, gl * 64:(gl + 1) * 64] = -S64
    s1 = np.arange(128, dtype=f64)
    C1 = np.cos(2 * np.pi * np.outer(s1, s1) / 128)
    S1 = np.sin(2 * np.pi * np.outer(s1, s1) / 128)
    t1u = np.concatenate([C1, -S1], 1)
    t1v = np.concatenate([S1, C1], 1)
    s2 = np.arange(N2, dtype=f64)
    Tc = np.cos(2 * np.pi * np.outer(s2, s1) / S)
    Ts = np.sin(2 * np.pi * np.outer(s2, s1) / S)
    Tcp = np.tile(Tc, (PC, 1))
    Tsp = np.tile(Ts, (PC, 1))
    ta = np.tile(Tcp, (1, 4))
    C2 = np.cos(2 * np.pi * np.outer(s2, s2) / N2)
    S2 = np.sin(2 * np.pi * np.outer(s2, s2) / N2)

    def blk(M, k2s):
        n = len(k2s)
        out = np.zeros((128, PC * n), f64)
        for c in range(PC):
            out[c * N2:(c + 1) * N2, c * n:(c + 1) * n] = M[:, k2s]
        return out

    full = np.arange(N2)
    tb = dict(rope_c=rope_c, rope_s=rope_s, cb=cb, sb=sbm, t1u=t1u, t1v=t1v, ta=ta, ts=Tsp,
              bc_p=blk(C2, full), bs_p=blk(S2, full))
    tb["ts"] = np.tile(Tsp, (1, 2))
    tb = {k: np.ascontiguousarray(v, dtype=np.float32) for k, v in tb.items()}
    bq = []
    for r in range(4):
        k2s = np.arange(cfg.NQ2 * r, cfg.NQ2 * (r + 1))
        bq.append((np.ascontiguousarray(blk(C2, k2s), dtype=np.float32), np.ascontiguousarray(blk(S2, k2s), dtype=np.float32)))
    return tb, bq


_CACHE = {}


def _run(cfg, inputs):
    S, QS = cfg.S, cfg.QS
    f = lambda a: np.ascontiguousarray(np.asarray(a), dtype=np.float32)
    x_prompt, x_sample = f(inputs["x_prompt"]), f(inputs["x_sample"])
    c_prompt, c_sample = f(inputs["c_prompt"]), f(inputs["c_sample"])
    tb, bq = _tables(cfg)
    shared = dict(
        w_ada=f(inputs["w_ada"][0]), b_ada=f(inputs["b_ada"][0]).reshape(1, -1), g_mix=f(inputs["g_mix"][0]).reshape(1, -1),
        g_ffn=f(inputs["g_ffn"][0]).reshape(1, -1), g_final=f(inputs["g_final"]).reshape(1, -1), w_in=f(inputs["w_in"][0]),
        gqT=f(f(inputs["g_q_lat"][0]).reshape(2, 128).T), w_uq=f(inputs["w_uq"][0]),
        gkvT=f(f(inputs["g_kv_lat"][0]).reshape(128, 1)), w_ukv=f(inputs["w_ukv"][0]), w_four=f(inputs["w_four"][0]),
        w_out=f(inputs["w_out"][0]), w_gate=f(inputs["w_gate"][0]), w_up=f(inputs["w_up"][0]), w_down=f(inputs["w_down"][0]),
    )
    shared.update(tb)
    in_maps = []
    for i in range(N_CORES):
        sidx, r = i // 4, i % 4
        cc = np.stack([c_prompt[i], c_sample[sidx]], 0)
        cT = f(cc.reshape(2, KC, 128).transpose(2, 1, 0))
        m = dict(shared)
        m.update(xs_p=x_prompt[i], xs_s=x_sample[sidx], xm_s=f(x_sample[sidx, r * QS:(r + 1) * QS]), cT=cT,
                 bc_s=bq[r][0], bs_s=bq[r][1],
                 rope_cm=f(tb["rope_c"][:, r * QS:(r + 1) * QS]), rope_sm=f(tb["rope_s"][:, r * QS:(r + 1) * QS]))
        in_maps.append(m)
    nc = build_program(cfg)
    res = run_bass_kernel_spmd(nc, in_maps, core_ids=list(range(N_CORES)))
    y_prompt = np.stack([np.asarray(res.results[i]["y_p"], dtype=np.float32) for i in range(N_CORES)], 0)
    y_sample = np.zeros((2, S, D), np.float32)
    for i in range(N_CORES):
        y_sample[i // 4, (i % 4) * QS:(i % 4 + 1) * QS] = np.asarray(res.results[i]["y_s"], dtype=np.float32)
    return y_prompt, y_sample


def kernel(**inputs):
    S = int(np.asarray(inputs["x_prompt"]).shape[1])
    return _run(Cfg(S), inputs)
```

```python
import math
from contextlib import ExitStack

import numpy as np
import concourse.bass as bass
import concourse.mybir as mybir
from concourse.bass_utils import run_bass_kernel_spmd

F32 = mybir.dt.float32
BF16 = mybir.dt.bfloat16
AF = mybir.ActivationFunctionType
ALU = mybir.AluOpType

D = 1024
KC = 8
DIN = 960
DFF = 2816
FC = DFF // 128
EPS = 1e-6
SCALE = 1.0 / math.sqrt(192.0)
N_CORES = 8
import os
_PH = os.environ.get("K_PHASES")
PHASES = None if not _PH else set(int(v) for v in _PH.split(","))
MAXOPS = int(os.environ.get("K_MAXOPS", "0"))


class Res:
    __slots__ = ("name", "writer", "readers")

    def __init__(self, name=""):
        self.name = name
        self.writer = None
        self.readers = []


class Buf:
    __slots__ = ("h", "r")

    def __init__(self, h, name=""):
        self.h = h
        self.r = Res(name)


class Op:
    __slots__ = ("eng", "fn", "deps", "signaled", "cnt", "sem", "is_dma", "ndma", "ph")

    def __init__(self, eng, fn):
        self.eng = eng
        self.fn = fn
        self.deps = []
        self.signaled = False
        self.cnt = 0
        self.sem = None
        self.is_dma = False
        self.ndma = 0
        self.ph = None


ENGS = ("pe", "act", "dve", "pool", "sp")
HANDLES = {"pe": "tensor", "act": "scalar", "dve": "vector", "pool": "gpsimd", "sp": "sync"}


class SemTable:
    def __init__(self, nc, stack):
        self.nc = nc
        self.stack = stack
        self.eng = {e: [stack.enter_context(nc.semaphore("s_" + e)), 0] for e in ENGS if e != "sp"}
        self.dma = {}

    def dma_sem(self, key):
        if key not in self.dma:
            self.dma[key] = [self.stack.enter_context(self.nc.semaphore("d_%d" % len(self.dma))), 0]
        return self.dma[key]


class Sched:
    def __init__(self, nc, sems):
        self.nc = nc
        self.sems = sems
        self.ops = {e: [] for e in ENGS}

    def _deps(self, op, reads, writes):
        deps = []
        for r in reads:
            if r.writer is not None:
                deps.append((0, r.writer))
        for w in writes:
            if w.writer is not None:
                deps.append((1, w.writer))
            for rd in w.readers:
                deps.append((2, rd))
        out = []
        seen = set()
        for kind, d in deps:
            if d is op or id(d) in seen or d.ph is not op.ph:
                continue
            if (not d.is_dma) and (not op.is_dma) and d.eng == op.eng:
                if op.eng == "pe" or kind != 0:
                    continue
            seen.add(id(d))
            d.signaled = True
            out.append(d)
        op.deps = out
        for r in reads:
            if not op.is_dma:
                r.readers = [x for x in r.readers if x.is_dma or x.eng != op.eng or x.ph is not op.ph]
            r.readers.append(op)
        for w in writes:
            w.writer = op
            w.readers = []

    def add(self, eng, fn, reads=(), writes=()):
        self.nrec = getattr(self, "nrec", 0) + 1
        if MAXOPS and self.nrec > MAXOPS:
            return None
        op = Op(eng, fn)
        op.ph = self
        self._deps(op, [b.r for b in reads], [b.r for b in writes])
        self.ops[eng].append(op)
        return op

    def dma(self, eng, fns, key, reads=(), writes=()):
        self.nrec = getattr(self, "nrec", 0) + 1
        if MAXOPS and self.nrec > MAXOPS:
            return None
        op = Op(eng, fns)
        op.is_dma = True
        op.ndma = len(fns)
        op.sem = key
        op.ph = self
        self._deps(op, [b.r for b in reads], [b.r for b in writes])
        self.ops[eng].append(op)
        return op

    def emit(self, phase=None):
        nc = self.nc
        if PHASES is not None and phase not in PHASES:
            return
        for e in ENGS:
            for op in self.ops[e]:
                if op.is_dma:
                    ent = self.sems.dma_sem(op.sem)
                    ent[1] += 16 * op.ndma
                    op.cnt = ent[1]
                    op.sem = ent[0]
                elif op.signaled:
                    ent = self.sems.eng[e]
                    ent[1] += 1
                    op.cnt = ent[1]
                    op.sem = ent[0]

        def run(e, h):
            waited = {}
            for op in self.ops[e]:
                for d in op.deps:
                    key = id(d.sem)
                    if waited.get(key, 0) >= d.cnt:
                        continue
                    waited[key] = d.cnt
                    if os.environ.get("K_TRACE"):
                        print("TRACE", e, "wait", d.sem.name if hasattr(d.sem, "name") else d.sem, d.cnt, "dep_eng", d.eng, "dma" if d.is_dma else "")
                    h.wait_ge(d.sem, d.cnt)
                if os.environ.get("K_TRACE"):
                    print("TRACE", e, "op", "dma" if op.is_dma else "", "sig" if op.signaled else "", op.cnt)
                if op.is_dma:
                    for f in op.fn:
                        f(h).then_inc(op.sem, 16)
                else:
                    ins = op.fn(h)
                    if op.signaled:
                        ins.then_inc(op.sem, 1)
            done = {}
            for op in self.ops[e]:
                if op.is_dma:
                    done[id(op.sem)] = (op.sem, op.cnt)
            for s, v in done.values():
                h.wait_ge(s, v)

        with nc.Block() as block:
            for e in ENGS:
                if self.ops[e]:
                    getattr(block, HANDLES[e])(lambda h, e=e: run(e, h))


class Cfg:
    def __init__(self, S):
        self.S = S
        self.N2 = S // 128
        self.PC = 128 // self.N2
        self.NT = S // 128
        self.QS = S // 4
        self.NQ2 = self.N2 // 4
        assert self.N2 * self.PC == 128 and self.QS % 512 == 0


def build_program(cfg):
    S, N2, PC, NT, QS = cfg.S, cfg.N2, cfg.PC, cfg.NT, cfg.QS
    NG = 128 // PC
    nc = bass.Bass("TRN2", target_bir_lowering=False)

    def din(name, shape, dt=F32):
        return nc.dram_tensor(name, list(shape), dt, kind="ExternalInput").ap()

    xs = [din("xs_p", [S, D]), din("xs_s", [S, D])]
    xm_s = din("xm_s", [QS, D])
    cT = din("cT", [128, KC, 2])
    w_ada = din("w_ada", [D, 6 * D])
    b_ada = din("b_ada", [1, 6 * D])
    g_mix = din("g_mix", [1, D])
    g_ffn = din("g_ffn", [1, D])
    g_final = din("g_final", [1, D])
    w_in = din("w_in", [D, DIN])
    gqT = din("gqT", [128, 2])
    w_uq = din("w_uq", [256, 768])
    gkvT = din("gkvT", [128, 1])
    w_ukv = din("w_ukv", [128, 1024])
    w_four = din("w_four", [8, 64, 64])
    w_out = din("w_out", [D, D])
    w_gate = din("w_gate", [D, DFF])
    w_up = din("w_up", [D, DFF])
    w_down = din("w_down", [DFF, D])
    rope_c = din("rope_c", [64, S])
    rope_s = din("rope_s", [64, S])
    rope_cm = din("rope_cm", [64, QS])
    rope_sm = din("rope_sm", [64, QS])
    cb_d = din("cb", [128, 128])
    sb_d = din("sb", [128, 128])
    t1u_d = din("t1u", [128, 256])
    t1v_d = din("t1v", [128, 256])
    ta_d = din("ta", [128, 512])
    ts_d = din("ts", [128, 256])
    bc_d = [din("bc_p", [128, 128]), din("bc_s", [128, 32])]
    bs_d = [din("bs_p", [128, 128]), din("bs_s", [128, 32])]
    y_p = nc.dram_tensor("y_p", [S, D], F32, kind="ExternalOutput").ap()
    y_s = nc.dram_tensor("y_s", [QS, D], F32, kind="ExternalOutput").ap()
    yf = [nc.dram_tensor("yf_p", [S, 512], BF16).ap(), nc.dram_tensor("yf_s", [QS, 512], BF16).ap()]
    x1scr = nc.dram_tensor("x1scr", [S + QS, D], F32).ap()
    modscr = nc.dram_tensor("modscr", [2, 6, D], F32).ap()
    yf_res = [[Res() for _ in range(S // 128)], [Res() for _ in range(QS // 128)]]
    x1_res = [Res() for _ in range((S + QS) // 128)]

    with ExitStack() as top:
        top.enter_context(nc.allow_low_precision("bf16 matmul operands, fp32 accumulation"))
        sems = SemTable(nc, top)

        uid = [0]

        def sb(st, name, shape, dt):
            uid[0] += 1
            return st.enter_context(nc.sbuf_tensor("%s_%d" % (name, uid[0]), list(shape), dt))

        def bufs(st, name, shape, dt, n):
            return [Buf(sb(st, "%s%d" % (name, i), shape, dt)) for i in range(n)]

        def psum_banks(st, nf=6):
            uid[0] += 1
            banks = [Buf(st.enter_context(nc.psum_tensor("pb%d_%d" % (i, uid[0]), [128, 512], F32))) for i in range(nf)]
            ptr = [Buf(st.enter_context(nc.psum_tensor("ptr%d_%d" % (i, uid[0]), [128, 1024], BF16))) for i in range(8 - nf)]
            if len(ptr) == 1:
                ptr = [ptr[0], ptr[0]]
            return banks, ptr

        class Wrap:
            def __init__(self, r):
                self.r = r

        ident = Buf(sb(top, "ident", [128, 128], BF16))
        ones_bf = Buf(sb(top, "ones_bf", [128, 128], BF16))
        epsb = Buf(sb(top, "epsb", [128, 1], F32))

        with ExitStack() as w1:
            win = Buf(sb(w1, "win", [128, KC, 1024], BF16))
            wuq = Buf(sb(w1, "wuq", [128, 2, 1024], BF16))
            wukv = Buf(sb(w1, "wukv", [128, 1024], BF16))
            wukT = Buf(sb(w1, "wukT", [128, 4, 128], BF16))
            gkv = Buf(sb(w1, "gkv", [128, 1], F32))
            abh = Buf(sb(w1, "abh", [128, 8, 128], BF16))
            t1u = Buf(sb(w1, "t1u", [128, 256], BF16))
            t1v = Buf(sb(w1, "t1v", [128, 256], BF16))
            ta = Buf(sb(w1, "ta", [128, 512], F32))
            tsn = Buf(sb(w1, "tsn", [128, 256], F32))
            bcs = [Buf(sb(w1, "bc0", [128, 128], BF16)), Buf(sb(w1, "bc1", [128, 32], BF16))]
            bss = [Buf(sb(w1, "bs0", [128, 128], BF16)), Buf(sb(w1, "bs1", [128, 32], BF16))]

            with ExitStack() as st:
                Sd = Sched(nc, sems)
                banks, ptr = psum_banks(st)
                ptr = ptr[0]
                identf = Buf(sb(st, "identf", [128, 128], F32))
                Sd.add("pool", lambda h: h.memset(identf.h[:], 0.0), writes=[identf])
                Sd.add("pool", lambda h: h.affine_select(out=identf.h[:], in_=identf.h[:], pattern=[[-1, 128]],
                                                         compare_op=ALU.not_equal, fill=1.0, base=0,
                                                         channel_multiplier=1), reads=[identf], writes=[identf])
                Sd.add("dve", lambda h: h.tensor_copy(out=ident.h[:], in_=identf.h[:]), reads=[identf], writes=[ident])
                Sd.add("pool", lambda h: h.memset(ones_bf.h[:], 1.0), writes=[ones_bf])
                Sd.add("pool", lambda h: h.memset(epsb.h[:], EPS), writes=[epsb])
                Sd.dma("pool", [lambda h: h.dma_start(out=win.h[:, :, 0:DIN], in_=w_in.rearrange("(kc p) n -> p kc n", p=128)),
                                lambda h: h.dma_start(out=win.h[:, :, 960:992], in_=w_in[:, 416:448].rearrange("(kc p) n -> p kc n", p=128)),
                                lambda h: h.dma_start(out=win.h[:, :, 992:1024], in_=w_in[:, 384:416].rearrange("(kc p) n -> p kc n", p=128))],
                       "set_win", writes=[win])
                Sd.add("pool", lambda h: h.tensor_scalar(out=win.h[:, :, 960:992], in0=win.h[:, :, 960:992], scalar1=-1.0,
                                                         scalar2=None, op0=ALU.mult), reads=[win], writes=[win])
                Sd.dma("pool", [lambda h: h.dma_start(out=wukv.h[:], in_=w_ukv),
                                lambda h: h.dma_start(out=t1u.h[:], in_=t1u_d),
                                lambda h: h.dma_start(out=t1v.h[:], in_=t1v_d),
                                lambda h: h.dma_start(out=bcs[0].h[:], in_=bc_d[0]),
                                lambda h: h.dma_start(out=bcs[1].h[:], in_=bc_d[1]),
                                lambda h: h.dma_start(out=bss[0].h[:], in_=bs_d[0]),
                                lambda h: h.dma_start(out=bss[1].h[:], in_=bs_d[1])],
                       "set_misc", writes=[wukv, t1u, t1v, bcs[0], bcs[1], bss[0], bss[1]])
                Sd.dma("sp", [lambda h: h.dma_start(out=ta.h[:], in_=ta_d),
                              lambda h: h.dma_start(out=tsn.h[:], in_=ts_d),
                              lambda h: h.dma_start(out=gkv.h[:], in_=gkvT)], "set_f32", writes=[ta, tsn, gkv])
                wuqf = Buf(sb(st, "wuqf", [128, 2, 768], F32))
                gq = Buf(sb(st, "gq", [128, 2], F32))
                Sd.dma("sp", [lambda h: h.dma_start(out=wuqf.h[:], in_=w_uq.rearrange("(kc p) n -> p kc n", p=128)),
                              lambda h: h.dma_start(out=gq.h[:], in_=gqT)], "set_wuq", writes=[wuqf, gq])
                for kc in range(2):
                    Sd.add("dve", lambda h, kc=kc: h.tensor_scalar(out=wuqf.h[:, kc, :], in0=wuqf.h[:, kc, :], scalar1=gq.h[:, kc:kc + 1],
                                                                   scalar2=None, op0=ALU.mult), reads=[wuqf, gq], writes=[wuqf])
                    Sd.add("dve", lambda h, kc=kc: h.tensor_copy(out=wuq.h[:, kc, 0:768], in_=wuqf.h[:, kc, :]), reads=[wuqf], writes=[wuq])
                    for hd in range(4):
                        b0 = hd * 192 + 128
                        Sd.add("dve", lambda h, kc=kc, hd=hd, b0=b0: h.tensor_scalar(
                            out=wuq.h[:, kc, 768 + hd * 64:768 + hd * 64 + 32], in0=wuqf.h[:, kc, b0 + 32:b0 + 64],
                            scalar1=-1.0, scalar2=None, op0=ALU.mult), reads=[wuqf], writes=[wuq])
                        Sd.add("dve", lambda h, kc=kc, hd=hd, b0=b0: h.tensor_copy(
                            out=wuq.h[:, kc, 768 + hd * 64 + 32:768 + hd * 64 + 64], in_=wuqf.h[:, kc, b0:b0 + 32]),
                            reads=[wuqf], writes=[wuq])
                for hd in range(4):
                    Sd.add("pe", lambda h, hd=hd: h.transpose(out=ptr.h[:, hd * 128:(hd + 1) * 128],
                                                              in_=wukv.h[:, hd * 256:hd * 256 + 128], identity=ident.h[:]),
                           reads=[wukv, ident], writes=[ptr])
                Sd.add("dve", lambda h: h.tensor_copy(out=wukT.h[:].rearrange("p a b -> p (a b)"), in_=ptr.h[:, 0:512]),
                       reads=[ptr], writes=[wukT])
                cbt = Buf(sb(st, "cbt", [128, 128], F32))
                sbt = Buf(sb(st, "sbt", [128, 128], F32))
                wblk = Buf(sb(st, "wblk", [128, 4, 128], F32))
                Sd.add("pool", lambda h: h.memset(wblk.h[:], 0.0), writes=[wblk])
                fl = [lambda h: h.dma_start(out=cbt.h[:], in_=cb_d), lambda h: h.dma_start(out=sbt.h[:], in_=sb_d)]
                for cc in range(4):
                    for gl in range(2):
                        fl.append(lambda h, cc=cc, gl=gl: h.dma_start(out=wblk.h[gl * 64:(gl + 1) * 64, cc, gl * 64:(gl + 1) * 64],
                                                                      in_=w_four[2 * cc + gl]))
                Sd.dma("sp", fl, "set_four", writes=[cbt, sbt, wblk])
                for cc in range(4):
                    bk = banks[cc % 2]
                    Sd.add("pe", lambda h, cc=cc, bk=bk: h.matmul(bk.h[:, 0:128], lhsT=cbt.h[:], rhs=wblk.h[:, cc, :], start=True, stop=True),
                           reads=[cbt, wblk], writes=[bk])
                    Sd.add("pe", lambda h, cc=cc, bk=bk: h.matmul(bk.h[:, 128:256], lhsT=sbt.h[:], rhs=wblk.h[:, cc, :], start=True, stop=True),
                           reads=[sbt, wblk], writes=[bk])
                    Sd.add("dve", lambda h, cc=cc, bk=bk: h.tensor_copy(
                        out=abh.h[:, 2 * cc:2 * cc + 2, :].rearrange("p f (u c) -> p u f c", u=2),
                        in_=bk.h[:, 0:256].rearrange("p (u f c) -> p u f c", u=2, f=2)), reads=[bk], writes=[abh])
                cts = Buf(sb(st, "cts", [128, KC, 2], F32))
                ctb = Buf(sb(st, "ctb", [128, KC, 2], BF16))
                modr = Buf(sb(st, "modr", [2, 6 * D], F32))
                bad = Buf(sb(st, "bad", [2, 6 * D], F32))
                gm2 = Buf(sb(st, "gm2", [2, D], F32))
                gf2 = Buf(sb(st, "gf2", [2, D], F32))
                Sd.dma("sp", [lambda h: h.dma_start(out=cts.h[:], in_=cT),
                              lambda h: h.dma_start(out=bad.h[:], in_=b_ada.partition_broadcast(2)),
                              lambda h: h.dma_start(out=gm2.h[:], in_=g_mix.partition_broadcast(2)),
                              lambda h: h.dma_start(out=gf2.h[:], in_=g_ffn.partition_broadcast(2))],
                       "set_c", writes=[cts, bad, gm2, gf2])
                Sd.add("act", lambda h: h.activation(out=ctb.h[:], in_=cts.h[:], func=AF.Silu), reads=[cts], writes=[ctb])
                wa = bufs(st, "wa", [128, KC, 512], BF16, 2)
                for nt in range(12):
                    wb = wa[nt % 2]
                    bk = banks[2 + nt % 2]
                    Sd.dma("pool", [lambda h, nt=nt, wb=wb: h.dma_start(
                        out=wb.h[:], in_=w_ada[:, nt * 512:(nt + 1) * 512].rearrange("(kc p) n -> p kc n", p=128))],
                        "wa%d" % (nt % 2), writes=[wb])
                    for kc in range(KC):
                        Sd.add("pe", lambda h, kc=kc, wb=wb, bk=bk: h.matmul(bk.h[0:2, :], lhsT=ctb.h[:, kc, :], rhs=wb.h[:, kc, :],
                                                                            start=(kc == 0), stop=(kc == KC - 1)),
                               reads=[ctb, wb], writes=[bk])
                    Sd.add("dve", lambda h, nt=nt, bk=bk: h.tensor_tensor(out=modr.h[:, nt * 512:(nt + 1) * 512], in0=bk.h[0:2, :],
                                                                         in1=bad.h[:, nt * 512:(nt + 1) * 512], op=ALU.add),
                           reads=[bk, bad], writes=[modr])
                Sd.add("dve", lambda h: h.scalar_tensor_tensor(out=modr.h[:, D:2 * D], in0=modr.h[:, D:2 * D], scalar=1.0, in1=gm2.h[:],
                                                               op0=ALU.add, op1=ALU.mult), reads=[modr, gm2], writes=[modr])
                Sd.add("dve", lambda h: h.scalar_tensor_tensor(out=modr.h[:, 4 * D:5 * D], in0=modr.h[:, 4 * D:5 * D], scalar=1.0, in1=gf2.h[:],
                                                               op0=ALU.add, op1=ALU.mult), reads=[modr, gf2], writes=[modr])
                mod_res = Wrap(Res())
                Sd.dma("pool", [lambda h: h.dma_start(out=modscr.rearrange("s j d -> s (j d)"), in_=modr.h[:])], "modst",
                       reads=[modr], writes=[mod_res])
                Sd.emit(0)

            def load_mod(Sd, dst, s, j):
                Sd.dma("sp", [lambda h: h.dma_start(out=dst.h[:], in_=modscr[s, j:j + 1, :].partition_broadcast(128))],
                       "mod_%d" % j, writes=[dst])

            def load_modT(Sd, dst, s, j):
                Sd.dma("sp", [lambda h: h.dma_start(out=dst.h[:], in_=modscr[s, j, :].rearrange("(kc p) -> p kc", p=128),
                                                    allow_slow_non_contiguous=True)], "modT_%d" % j, writes=[dst])

            def front_tile(Sd, xt, ss, xn):
                Sd.add("act", lambda h: h.activation(out=xn.h[:], in_=xt.h[:], func=AF.Square, accum_out=ss.h[:, 0:1]),
                       reads=[xt], writes=[xn, ss])
                Sd.add("act", lambda h: h.activation(out=ss.h[:, 1:2], in_=ss.h[:, 0:1], func=AF.Sqrt, scale=1.0 / D, bias=epsb.h[:, 0:1]),
                       reads=[ss, epsb], writes=[ss])
                Sd.add("dve", lambda h: h.reciprocal(out=ss.h[:, 2:3], in_=ss.h[:, 1:2]), reads=[ss], writes=[ss])
                Sd.add("act", lambda h: h.activation(out=xn.h[:], in_=xt.h[:], func=AF.Identity, scale=ss.h[:, 2:3]),
                       reads=[xt, ss], writes=[xn])

            def front_group(Sd, xns, aT, shT, ptrs, hT, G):
                for kp in range(KC // 2):
                    ptr = ptrs[kp % 2]
                    for k2 in range(2):
                        kc = 2 * kp + k2
                        for t in range(G):
                            Sd.add("pe", lambda h, kc=kc, k2=k2, t=t, ptr=ptr: h.transpose(
                                out=ptr.h[:, (k2 * G + t) * 128:(k2 * G + t + 1) * 128], in_=xns[t].h[:, kc * 128:(kc + 1) * 128],
                                identity=ident.h[:]), reads=[xns[t], ident], writes=[ptr])
                    for k2 in range(2):
                        kc = 2 * kp + k2
                        Sd.add("dve", lambda h, kc=kc, k2=k2, ptr=ptr: h.tensor_scalar(
                            out=hT.h[:, kc, 0:G * 128], in0=ptr.h[:, k2 * G * 128:(k2 + 1) * G * 128],
                            scalar1=aT.h[:, kc:kc + 1], scalar2=shT.h[:, kc:kc + 1], op0=ALU.mult, op1=ALU.add),
                            reads=[ptr, aT, shT], writes=[hT])

            for sq in range(2):
                x_seq = xs[sq]
                x_my = xs[0] if sq == 0 else xm_s
                n_my = S if sq == 0 else QS
                x1_off = 0 if sq == 0 else S
                with ExitStack() as sst:
                    kcT = Buf(sb(sst, "kcT", [128, S], BF16))
                    kpeT = Buf(sb(sst, "kpeT", [128, S], BF16))
                    vc = Buf(sb(sst, "vc", [128, NT, 128], BF16))
                    with ExitStack() as fst:
                        finT = Buf(sb(fst, "finT", [128, 4, S], BF16))
                        with ExitStack() as st:
                            Sd = Sched(nc, sems)
                            banks, ptrs = psum_banks(st)
                            ptr = ptrs[0]
                            a1T = Buf(sb(st, "a1T", [128, KC], F32))
                            sh1T = Buf(sb(st, "sh1T", [128, KC], F32))
                            load_modT(Sd, sh1T, sq, 0)
                            load_modT(Sd, a1T, sq, 1)
                            xts = bufs(st, "xt", [128, D], F32, 3)
                            sss = bufs(st, "ss", [128, 4], F32, 2)
                            xnb = bufs(st, "xn", [128, D], BF16, 8)
                            hTs = bufs(st, "hT", [128, KC, 512], BF16, 2)
                            rcs = bufs(st, "rc", [64, 512], F32, 2)
                            rss = bufs(st, "rs", [64, 512], F32, 2)
                            sqb = Buf(sb(st, "sqb", [128, 512], BF16))
                            rk = Buf(sb(st, "rk", [128, 512], F32))
                            t1 = Buf(sb(st, "t1", [64, 512], F32))
                            t2 = Buf(sb(st, "t2", [64, 512], F32))
                            def pa_tiles(g):
                                for t in range(4):
                                    ti = 4 * g + t
                                    xt = xts[ti % 3]
                                    r0 = g * 512 + t * 128
                                    Sd.dma("sp", [lambda h, xt=xt, r0=r0: h.dma_start(out=xt.h[:], in_=x_seq[r0:r0 + 128, :])],
                                           "xt%d" % (ti % 3), writes=[xt])
                                    front_tile(Sd, xt, sss[ti % 2], xnb[(g % 2) * 4 + t])

                            Sd.add("pool", lambda h: h.memset(kpeT.h[64:128, :], 0.0), writes=[kpeT])
                            pa_tiles(0)
                            for g in range(S // 512):
                                hT = hTs[g % 2]
                                rc, rs = rcs[g % 2], rss[g % 2]
                                Sd.dma("sp", [lambda h, g=g, rc=rc: h.dma_start(out=rc.h[:], in_=rope_c[:, g * 512:(g + 1) * 512]),
                                              lambda h, g=g, rs=rs: h.dma_start(out=rs.h[:], in_=rope_s[:, g * 512:(g + 1) * 512])],
                                       "rope%d" % (g % 2), writes=[rc, rs])
                                front_group(Sd, xnb[(g % 2) * 4:(g % 2) * 4 + 4], a1T, sh1T, ptrs, hT, 4)
                                if g + 1 < S // 512:
                                    pa_tiles(g + 1)
                                cols = slice(g * 512, (g + 1) * 512)

                                def zmm(bk, c0, m, hT=hT):
                                    for kc in range(KC):
                                        Sd.add("pe", lambda h, kc=kc: h.matmul(bk.h[0:m, :], lhsT=win.h[:, kc, c0:c0 + m], rhs=hT.h[:, kc, :],
                                                                              start=(kc == 0), stop=(kc == KC - 1)),
                                               reads=[win, hT], writes=[bk])
                                bkv, bsum = banks[0], banks[1]
                                zmm(bkv, 256, 128)
                                Sd.add("act", lambda h: h.activation(out=sqb.h[:], in_=bkv.h[:], func=AF.Square), reads=[bkv], writes=[sqb])
                                Sd.add("pe", lambda h: h.matmul(bsum.h[:], lhsT=ones_bf.h[:], rhs=sqb.h[:], start=True, stop=True),
                                       reads=[ones_bf, sqb], writes=[bsum])
                                Sd.add("act", lambda h: h.activation(out=rk.h[:], in_=bsum.h[:], func=AF.Sqrt, scale=1.0 / 128, bias=epsb.h[:, 0:1]),
                                       reads=[bsum, epsb], writes=[rk])
                                Sd.add("dve", lambda h: h.reciprocal(out=rk.h[:], in_=rk.h[:]), reads=[rk], writes=[rk])
                                Sd.add("dve", lambda h: h.tensor_scalar(out=rk.h[:], in0=rk.h[:], scalar1=gkv.h[:, 0:1], scalar2=None, op0=ALU.mult),
                                       reads=[rk, gkv], writes=[rk])
                                Sd.add("dve", lambda h, cols=cols: h.tensor_tensor(out=kcT.h[:, cols], in0=bkv.h[:], in1=rk.h[:], op=ALU.mult),
                                       reads=[bkv, rk], writes=[kcT])
                                for t in range(4):
                                    Sd.add("pe", lambda h, t=t, g=g: h.transpose(out=ptr.h[:, t * 128:(t + 1) * 128],
                                                                                in_=kcT.h[:, g * 512 + t * 128:g * 512 + (t + 1) * 128],
                                                                                identity=ident.h[:]), reads=[kcT, ident], writes=[ptr])
                                Sd.add("dve", lambda h, g=g: h.tensor_copy(out=vc.h[:, 4 * g:4 * g + 4, :].rearrange("p a b -> p (a b)"),
                                                                          in_=ptr.h[:, 0:512]), reads=[ptr], writes=[vc])
                                bpe, brot = banks[2], banks[3]
                                zmm(bpe, 384, 64)
                                zmm(brot, 960, 64)
                                Sd.add("dve", lambda h, rc=rc: h.tensor_tensor(out=t1.h[:], in0=bpe.h[0:64, :], in1=rc.h[:], op=ALU.mult),
                                       reads=[bpe, rc], writes=[t1])
                                Sd.add("dve", lambda h, rs=rs: h.tensor_tensor(out=t2.h[:], in0=brot.h[0:64, :], in1=rs.h[:], op=ALU.mult),
                                       reads=[brot, rs], writes=[t2])
                                Sd.add("pool", lambda h, cols=cols: h.tensor_tensor(out=kpeT.h[0:64, cols], in0=t1.h[:], in1=t2.h[:], op=ALU.add),
                                       reads=[t1, t2], writes=[kpeT])
                                for cc in range(4):
                                    bk = banks[4 + cc % 2]
                                    zmm(bk, 448 + cc * 128, 128)
                                    Sd.add("act", lambda h, cc=cc, bk=bk, cols=cols: h.activation(out=finT.h[:, cc, cols], in_=bk.h[:], func=AF.Copy),
                                           reads=[bk], writes=[finT])
                            Sd.emit(1 + 3 * sq)
                        with ExitStack() as st:
                            Sd = Sched(nc, sems)
                            banks, ptrs = psum_banks(st)
                            nk2 = N2 if sq == 0 else cfg.NQ2
                            ncol2 = PC * nk2
                            xall = Buf(sb(st, "xall", [128, 128, N2], BF16))
                            ytok = Buf(sb(st, "ytok", [128, nk2, 64], BF16))
                            tms = bufs(st, "tm", [128, 512], F32, 3)
                            tus = bufs(st, "tu", [128, 512], F32, 3)
                            y2s = bufs(st, "y2", [128, 512], BF16, 3)
                            bcx, bsx = bcs[sq], bss[sq]
                            it = 0
                            for hc in range(8):
                                cc = hc // 2
                                for s2 in range(0, N2, 4):
                                    bk = banks[(s2 // 4) % 2]
                                    for j in range(4):
                                        Sd.add("pe", lambda h, s2=s2, j=j, bk=bk, cc=cc, hc=hc: h.matmul(
                                            bk.h[:, j * 128:(j + 1) * 128],
                                            lhsT=finT.h[:, cc, :].rearrange("p (a b) -> p b a", b=N2)[:, s2 + j, :],
                                            rhs=abh.h[:, hc, :], start=True, stop=True), reads=[finT, abh], writes=[bk])
                                    Sd.add("dve", lambda h, s2=s2, bk=bk: h.tensor_copy(
                                        out=xall.h[:, :, s2:s2 + 4].rearrange("p c s -> p s c"),
                                        in_=bk.h[:].rearrange("p (s c) -> p s c", s=4)),
                                        reads=[bk], writes=[xall])
                                ngh = 64 // PC
                                def s1(gp, it):
                                    b1 = banks[2 + it % 2]
                                    tm, tu, y2 = tms[it % 3], tus[it % 3], y2s[it % 3]
                                    for j in range(2):
                                        c0 = (gp + j) * PC
                                        lu = xall.h[:, c0:c0 + PC, :].rearrange("p c s -> p (c s)")
                                        lv = xall.h[:, 64 + c0:64 + c0 + PC, :].rearrange("p c s -> p (c s)")
                                        Sd.add("pe", lambda h, j=j, lu=lu, b1=b1: h.matmul(b1.h[:, j * 256:(j + 1) * 256], lhsT=lu, rhs=t1u.h[:],
                                                                                          start=True, stop=False), reads=[xall, t1u], writes=[b1])
                                        Sd.add("pe", lambda h, j=j, lv=lv, b1=b1: h.matmul(b1.h[:, j * 256:(j + 1) * 256], lhsT=lv, rhs=t1v.h[:],
                                                                                          start=False, stop=True), reads=[xall, t1v], writes=[b1])
                                    Sd.add("dve", lambda h, b1=b1, tm=tm: h.tensor_tensor(out=tm.h[:], in0=b1.h[:], in1=ta.h[:], op=ALU.mult),
                                           reads=[b1, ta], writes=[tm])
                                    y1v = b1.h[:].rearrange("p (g r k) -> p g r k", g=2, r=2)
                                    tuv = tu.h[:].rearrange("p (g r k) -> p g r k", g=2, r=2)
                                    tsb = tsn.h[:].rearrange("p (g k) -> p g k", g=2)
                                    Sd.add("dve", lambda h, y1v=y1v, tuv=tuv, tsb=tsb: h.tensor_tensor(out=tuv[:, :, 0, :], in0=y1v[:, :, 1, :], in1=tsb,
                                                                                                  op=ALU.mult), reads=[b1, tsn], writes=[tu])
                                    Sd.add("dve", lambda h, y1v=y1v, tuv=tuv, tsb=tsb: h.tensor_tensor(out=tuv[:, :, 1, :], in0=y1v[:, :, 0, :], in1=tsb,
                                                                                                  op=ALU.mult), reads=[b1, tsn], writes=[tu])
                                    tmv = tm.h[:].rearrange("p (g r k) -> p g r k", g=2, r=2)
                                    y2v = y2.h[:].rearrange("p (g r k) -> p g r k", g=2, r=2)
                                    Sd.add("pool", lambda h, tmv=tmv, tuv=tuv, y2v=y2v: h.tensor_tensor(out=y2v[:, :, 0, :], in0=tmv[:, :, 0, :],
                                                                                                    in1=tuv[:, :, 0, :], op=ALU.add),
                                           reads=[tm, tu], writes=[y2])
                                    Sd.add("pool", lambda h, tmv=tmv, tuv=tuv, y2v=y2v: h.tensor_tensor(out=y2v[:, :, 1, :], in0=tmv[:, :, 1, :],
                                                                                                    in1=tuv[:, :, 1, :], op=ALU.subtract),
                                           reads=[tm, tu], writes=[y2])

                                def s2(gp, it):
                                    b2 = banks[4 + it % 2]
                                    y2 = y2s[it % 3]
                                    for j in range(2):
                                        Sd.add("pe", lambda h, j=j, y2=y2, b2=b2: h.matmul(b2.h[:, j * ncol2:(j + 1) * ncol2], lhsT=y2.h[:, j * 256:j * 256 + 128],
                                                                                          rhs=bcx.h[:], start=True, stop=False), reads=[y2, bcx], writes=[b2])
                                        Sd.add("pe", lambda h, j=j, y2=y2, b2=b2: h.matmul(b2.h[:, j * ncol2:(j + 1) * ncol2], lhsT=y2.h[:, j * 256 + 128:j * 256 + 256],
                                                                                          rhs=bsx.h[:], start=False, stop=True), reads=[y2, bsx], writes=[b2])
                                    c0 = gp * PC
                                    Sd.add("act", lambda h, b2=b2, c0=c0: h.activation(
                                        out=ytok.h[:, :, c0:c0 + 2 * PC].rearrange("p k c -> p c k"),
                                        in_=b2.h[:, 0:2 * ncol2].rearrange("p (c k) -> p c k", k=nk2), func=AF.Copy),
                                        reads=[b2], writes=[ytok])

                                gps = list(range(0, ngh, 2))
                                for idx in range(len(gps) + 1):
                                    if idx < len(gps):
                                        s1(gps[idx], it + idx)
                                    if idx >= 1:
                                        s2(gps[idx - 1], it + idx - 1)
                                it += len(gps)
                                kstep = min(8, nk2)
                                Sd.dma("pool", [lambda h, hc=hc, k0=k0: h.dma_start(
                                    out=yf[sq].rearrange("(k2 k1) c -> k1 k2 c", k1=128)[:, k0:k0 + kstep, hc * 64:(hc + 1) * 64],
                                    in_=ytok.h[:, k0:k0 + kstep, :]) for k0 in range(0, nk2, kstep)],
                                    "yfst", reads=[ytok], writes=[Wrap(r) for r in yf_res[sq]])
                            Sd.emit(2 + 3 * sq)
                    with ExitStack() as st:
                        Sd = Sched(nc, sems)
                        banks, ptrs = psum_banks(st, 7)
                        ptr = ptrs[0]
                        wout = Buf(sb(st, "wout", [128, KC, D], BF16))
                        Sd.dma("pool", [lambda h: h.dma_start(out=wout.h[:], in_=w_out.rearrange("(kc p) n -> p kc n", p=128))],
                               "set_wout", writes=[wout])
                        a1T = Buf(sb(st, "a1T", [128, KC], F32))
                        sh1T = Buf(sb(st, "sh1T", [128, KC], F32))
                        ga1b = Buf(sb(st, "ga1b", [128, D], F32))
                        load_modT(Sd, sh1T, sq, 0)
                        load_modT(Sd, a1T, sq, 1)
                        load_mod(Sd, ga1b, sq, 2)
                        for kc in range(KC):
                            Sd.add("pool" if kc % 2 else "dve", lambda h, kc=kc: h.tensor_tensor(out=wout.h[:, kc, :], in0=wout.h[:, kc, :], in1=ga1b.h[:], op=ALU.mult),
                                   reads=[wout, ga1b], writes=[wout])
                        xts = bufs(st, "xt", [128, D], F32, 8)
                        sss = bufs(st, "ss", [128, 4], F32, 2)
                        xnb = bufs(st, "xn", [128, D], BF16, 4)
                        hTs = bufs(st, "hT", [128, KC, 512], BF16, 1)
                        rcs = bufs(st, "rc", [64, 512], F32, 1)
                        rss = bufs(st, "rs", [64, 512], F32, 1)
                        sqb2 = Buf(sb(st, "sqb2", [128, 2, 512], BF16))
                        rq = Buf(sb(st, "rq", [128, 512], F32))
                        qlT = Buf(sb(st, "qlT", [128, 2, 512], BF16))
                        qn = bufs(st, "qn", [128, 512], BF16, 2)
                        qcs = bufs(st, "qc", [128, 4, 512], BF16, 2)
                        qps = bufs(st, "qp", [128, 4, 512], BF16, 2)
                        for qpb in qps:
                            Sd.add("pool", lambda h, qpb=qpb: h.memset(qpb.h[64:128, :, :], 0.0), writes=[qpb])
                        cq = Buf(sb(st, "cq", [64, 512], F32))
                        sqr = Buf(sb(st, "sqr", [64, 512], F32))
                        t1 = Buf(sb(st, "t1", [64, 512], F32))
                        t2 = Buf(sb(st, "t2", [64, 512], F32))
                        pts = bufs(st, "pt", [128, 512], BF16, 6)
                        rec = Buf(ga1b.h[:, 0:512])
                        rec.r = ga1b.r
                        accD = Buf(sb(st, "accD", [128, 512], F32))
                        accP = Buf(sb(st, "accP", [128, 512], F32))
                        acct = Buf(ga1b.h[:, 512:1024])
                        acct.r = ga1b.r
                        ones_f = Buf(sb(st, "ones_f", [128, 128], F32))
                        Sd.add("pool", lambda h: h.memset(ones_f.h[:], 1.0), writes=[ones_f])
                        onT = Buf(sb(st, "onT", [128, 4, 512], BF16))
                        ymT = Buf(sb(st, "ymT", [128, KC, 512], BF16))
                        yfts = bufs(st, "yft", [128, 512], BF16, 2)
                        ng = n_my // 512
                        bgb = [banks[3], banks[6]]
                        bgi = [0]

                        def bgbank():
                            bgi[0] += 1
                            return bgb[bgi[0] % 2]

                        def prep(g):
                            hT = hTs[0]
                            rc, rs = rcs[0], rss[0]
                            qc, qp = qcs[g % 2], qps[g % 2]
                            pos0 = g * 512
                            rcd, rsd = (rope_c, rope_s) if sq == 0 else (rope_cm, rope_sm)
                            Sd.dma("sp", [lambda h: h.dma_start(out=rc.h[:], in_=rcd[:, pos0:pos0 + 512]),
                                          lambda h: h.dma_start(out=rs.h[:], in_=rsd[:, pos0:pos0 + 512])], "ropec", writes=[rc, rs])
                            for t in range(4):
                                xt = xts[(4 * g + t) % 8]
                                r0 = g * 512 + t * 128
                                Sd.dma("sp", [lambda h, xt=xt, r0=r0: h.dma_start(out=xt.h[:], in_=x_my[r0:r0 + 128, :])],
                                       "xc%d" % ((4 * g + t) % 8), writes=[xt])
                                front_tile(Sd, xt, sss[(4 * g + t) % 2], xnb[t])
                                yield
                            for kp in range(KC // 2):
                                ptr = ptrs[0]
                                for k2 in range(2):
                                    kc = 2 * kp + k2
                                    for t in range(4):
                                        Sd.add("pe", lambda h, kc=kc, k2=k2, t=t, ptr=ptr: h.transpose(
                                            out=ptr.h[:, (k2 * 4 + t) * 128:(k2 * 4 + t + 1) * 128], in_=xnb[t].h[:, kc * 128:(kc + 1) * 128],
                                            identity=ident.h[:]), reads=[xnb[t], ident], writes=[ptr])
                                yield
                                for k2 in range(2):
                                    kc = 2 * kp + k2
                                    Sd.add("dve", lambda h, kc=kc, k2=k2, ptr=ptr: h.tensor_scalar(
                                        out=hT.h[:, kc, 0:512], in0=ptr.h[:, k2 * 512:(k2 + 1) * 512],
                                        scalar1=a1T.h[:, kc:kc + 1], scalar2=sh1T.h[:, kc:kc + 1], op0=ALU.mult, op1=ALU.add),
                                        reads=[ptr, a1T, sh1T], writes=[hT])
                                yield
                            for c2 in range(2):
                                bk = bgbank()
                                for kc in range(KC):
                                    Sd.add("pe", lambda h, kc=kc, c2=c2, bk=bk: h.matmul(bk.h[:], lhsT=win.h[:, kc, c2 * 128:(c2 + 1) * 128], rhs=hT.h[:, kc, :],
                                                                                      start=(kc == 0), stop=(kc == KC - 1)), reads=[win, hT], writes=[bk])
                                yield
                                Sd.add("act", lambda h, c2=c2, bk=bk: h.activation(out=sqb2.h[:, c2, :], in_=bk.h[:], func=AF.Square), reads=[bk], writes=[sqb2])
                                Sd.add("act", lambda h, c2=c2, bk=bk: h.activation(out=qlT.h[:, c2, :], in_=bk.h[:], func=AF.Copy), reads=[bk], writes=[qlT])
                                yield
                            yield
                            bk = bgbank()
                            for c2 in range(2):
                                Sd.add("pe", lambda h, c2=c2, bk=bk: h.matmul(bk.h[:], lhsT=ones_bf.h[:], rhs=sqb2.h[:, c2, :], start=(c2 == 0), stop=(c2 == 1)),
                                       reads=[ones_bf, sqb2], writes=[bk])
                            yield
                            Sd.add("act", lambda h, bk=bk: h.activation(out=rq.h[:], in_=bk.h[:], func=AF.Sqrt, scale=1.0 / 256, bias=epsb.h[:, 0:1]),
                                   reads=[bk, epsb], writes=[rq])
                            yield
                            Sd.add("dve", lambda h: h.reciprocal(out=rq.h[:], in_=rq.h[:]), reads=[rq], writes=[rq])
                            Sd.add("dve", lambda h: h.tensor_tensor(out=cq.h[:], in0=rc.h[:], in1=rq.h[0:64, :], op=ALU.mult), reads=[rc, rq], writes=[cq])
                            Sd.add("dve", lambda h: h.tensor_tensor(out=sqr.h[:], in0=rs.h[:], in1=rq.h[0:64, :], op=ALU.mult), reads=[rs, rq], writes=[sqr])
                            yield
                            yield
                            for hd in range(4):
                                qnb = qn[hd % 2]
                                bn = bgbank()
                                for kc in range(2):
                                    Sd.add("pe", lambda h, kc=kc, hd=hd, bn=bn: h.matmul(bn.h[:], lhsT=wuq.h[:, kc, hd * 192:hd * 192 + 128], rhs=qlT.h[:, kc, :],
                                                                                        start=(kc == 0), stop=(kc == 1)), reads=[wuq, qlT], writes=[bn])
                                yield
                                Sd.add("dve", lambda h, qnb=qnb, bn=bn: h.tensor_tensor(out=qnb.h[:], in0=bn.h[:], in1=rq.h[:], op=ALU.mult), reads=[bn, rq], writes=[qnb])
                                yield
                                yield
                                bc_ = bgbank()
                                Sd.add("pe", lambda h, hd=hd, qnb=qnb, bc_=bc_: h.matmul(bc_.h[:], lhsT=wukT.h[:, hd, :], rhs=qnb.h[:], start=True, stop=True),
                                       reads=[wukT, qnb], writes=[bc_])
                                yield
                                Sd.add("act", lambda h, hd=hd, bc_=bc_: h.activation(out=qc.h[:, hd, :], in_=bc_.h[:], func=AF.Copy), reads=[bc_], writes=[qc])
                                yield
                                bp = bgbank()
                                for kc in range(2):
                                    Sd.add("pe", lambda h, kc=kc, hd=hd, bp=bp: h.matmul(bp.h[0:64, :], lhsT=wuq.h[:, kc, hd * 192 + 128:hd * 192 + 192], rhs=qlT.h[:, kc, :],
                                                                                        start=(kc == 0), stop=(kc == 1)), reads=[wuq, qlT], writes=[bp])
                                yield
                                Sd.add("dve", lambda h, bp=bp: h.tensor_tensor(out=t1.h[:], in0=bp.h[0:64, :], in1=cq.h[:], op=ALU.mult), reads=[bp, cq], writes=[t1])
                                yield
                                br = bgbank()
                                for kc in range(2):
                                    Sd.add("pe", lambda h, kc=kc, hd=hd, br=br: h.matmul(br.h[0:64, :], lhsT=wuq.h[:, kc, 768 + hd * 64:768 + hd * 64 + 64], rhs=qlT.h[:, kc, :],
                                                                                        start=(kc == 0), stop=(kc == 1)), reads=[wuq, qlT], writes=[br])
                                yield
                                Sd.add("dve", lambda h, br=br: h.tensor_tensor(out=t2.h[:], in0=br.h[0:64, :], in1=sqr.h[:], op=ALU.mult), reads=[br, sqr], writes=[t2])
                                yield
                                Sd.add("pool", lambda h, hd=hd, qp=qp: h.tensor_tensor(out=qp.h[0:64, hd, :], in0=t1.h[:], in1=t2.h[:], op=ALU.add), reads=[t1, t2], writes=[qp])
                                yield

                        def post(g):
                            gx = [xts[(4 * g + t) % 8] for t in range(4)]
                            for t in range(4):
                                yft = yfts[t % 2]
                                r0 = g * 512 + t * 128
                                Sd.dma("sp", [lambda h, yft=yft, r0=r0: h.dma_start(out=yft.h[:], in_=yf[sq][r0:r0 + 128, :])], "yft%d" % (t % 2),
                                       reads=[Wrap(yf_res[sq][r0 // 128])], writes=[yft])
                                yield
                                ptr = ptrs[0]
                                for cc in range(4):
                                    Sd.add("pe", lambda h, cc=cc, yft=yft, ptr=ptr: h.transpose(out=ptr.h[:, cc * 128:(cc + 1) * 128], in_=yft.h[:, cc * 128:(cc + 1) * 128],
                                                                                              identity=ident.h[:]), reads=[yft, ident], writes=[ptr])
                                yield
                                Sd.add("act", lambda h, t=t, ptr=ptr: h.activation(out=ymT.h[:, 4:8, t * 128:(t + 1) * 128],
                                                                                   in_=ptr.h[:, 0:512].rearrange("p (k c) -> p k c", k=4), func=AF.Copy),
                                       reads=[ptr], writes=[ymT])
                                yield
                            yield
                            for t in range(4):
                                for n2 in range(2):
                                    bk = bgbank()
                                    for kc in range(KC):
                                        Sd.add("pe", lambda h, kc=kc, t=t, n2=n2, bk=bk: h.matmul(bk.h[:], lhsT=ymT.h[:, kc, t * 128:(t + 1) * 128],
                                                                                                rhs=wout.h[:, kc, n2 * 512:(n2 + 1) * 512],
                                                                                                start=(kc == 0), stop=(kc == KC - 1)), reads=[ymT, wout], writes=[bk])
                                    yield
                                    Sd.add("dve", lambda h, n2=n2, bk=bk, xg=gx[t]: h.tensor_tensor(out=xg.h[:, n2 * 512:(n2 + 1) * 512], in0=bk.h[:],
                                                                                               in1=xg.h[:, n2 * 512:(n2 + 1) * 512], op=ALU.add),
                                           reads=[bk, gx[t]], writes=[gx[t]])
                                    yield
                                r0 = x1_off + g * 512 + t * 128
                                Sd.dma("pool", [lambda h, xg=gx[t], r0=r0: h.dma_start(out=x1scr[r0:r0 + 128, :], in_=xg.h[:])], "x1st%d" % ((4 * g + t) % 8),
                                       reads=[gx[t]], writes=[Wrap(x1_res[r0 // 128])])

                        def finish_head(hd):
                            bo, bm = banks[4 + hd % 2], bgbank()
                            Sd.add("pe", lambda h, bm=bm: h.matmul(bm.h[:], lhsT=ones_f.h[:], rhs=acct.h[:], start=True, stop=True),
                                   reads=[ones_f, acct], writes=[bm])
                            Sd.add("dve", lambda h, bm=bm: h.reciprocal(out=rec.h[:], in_=bm.h[:]), reads=[bm], writes=[rec])
                            Sd.add("dve", lambda h, hd=hd, bo=bo: h.tensor_tensor(out=onT.h[:, hd, :], in0=bo.h[:], in1=rec.h[:], op=ALU.mult),
                                   reads=[bo, rec], writes=[onT])

                        def vup_head(hd):
                            bk = bgbank()
                            Sd.add("pe", lambda h, hd=hd, bk=bk: h.matmul(bk.h[:], lhsT=wukv.h[:, hd * 256 + 128:hd * 256 + 256], rhs=onT.h[:, hd, :], start=True, stop=True),
                                   reads=[wukv, onT], writes=[bk])
                            Sd.add("act", lambda h, hd=hd, bk=bk: h.activation(out=ymT.h[:, hd, :], in_=bk.h[:], func=AF.Copy), reads=[bk], writes=[ymT])

                        for _ in prep(0):
                            pass
                        blocks = [(g, hd, j) for g in range(ng) for hd in range(4) for j in range(NT)]
                        LAG = 3
                        NS = 3
                        pending = []
                        bg = []
                        for i in range(len(blocks) + LAG + 24):
                            if i < len(blocks):
                                g, hd, j = blocks[i]
                                if hd == 0 and j == 0:
                                    for kind, gg, gen in bg:
                                        for _ in gen:
                                            pass
                                    bg = []
                                    if g == 0 and ng > 1:
                                        bg.append(("prep", 1, prep(1)))
                                qc, qp = qcs[g % 2], qps[g % 2]
                                bs_ = banks[i % NS]
                                pt = pts[i % 6]
                                Sd.add("pe", lambda h, j=j, hd=hd, bs_=bs_, qc=qc: h.matmul(bs_.h[:], lhsT=kcT.h[:, j * 128:(j + 1) * 128], rhs=qc.h[:, hd, :],
                                                                                    start=True, stop=False), reads=[kcT, qc], writes=[bs_])
                                Sd.add("pe", lambda h, j=j, hd=hd, bs_=bs_, qp=qp: h.matmul(bs_.h[:], lhsT=kpeT.h[:, j * 128:(j + 1) * 128], rhs=qp.h[:, hd, :],
                                                                                    start=False, stop=True), reads=[kpeT, qp], writes=[bs_])
                                Sd.add("act", lambda h, bs_=bs_, pt=pt: h.activation(out=pt.h[:], in_=bs_.h[:], func=AF.Exp, scale=SCALE), reads=[bs_], writes=[pt])
                                eng, acc = ("dve", accD) if j % 2 == 0 else ("pool", accP)
                                if j < 2:
                                    Sd.add(eng, lambda h, pt=pt, acc=acc: h.tensor_copy(out=acc.h[:], in_=pt.h[:]), reads=[pt], writes=[acc])
                                else:
                                    Sd.add(eng, lambda h, pt=pt, acc=acc: h.tensor_tensor(out=acc.h[:], in0=acc.h[:], in1=pt.h[:], op=ALU.add),
                                           reads=[pt, acc], writes=[acc])
                                if j == NT - 1:
                                    Sd.add("pool", lambda h: h.tensor_tensor(out=acct.h[:], in0=accD.h[:], in1=accP.h[:], op=ALU.add),
                                           reads=[accD, accP], writes=[acct])
                            if LAG <= i < len(blocks) + LAG:
                                g2, hd2, j2 = blocks[i - LAG]
                                bo = banks[4 + hd2 % 2]
                                pt2 = pts[(i - LAG) % 6]
                                Sd.add("pe", lambda h, j2=j2, pt2=pt2, bo=bo: h.matmul(bo.h[:], lhsT=vc.h[:, j2, :], rhs=pt2.h[:], start=(j2 == 0), stop=(j2 == NT - 1)),
                                       reads=[vc, pt2], writes=[bo])
                                if j2 == NT - 1:
                                    pending.append([i + 4, hd2, 0, g2])
                            if pending and pending[0][0] <= i:
                                if pending[0][2] == 0:
                                    finish_head(pending[0][1])
                                    pending[0][2] = 1
                                    pending[0][0] = i + 12
                                else:
                                    pg = pending[0][3]
                                    while bg and bg[0][0] == "post" and bg[0][1] < pg:
                                        for _ in bg[0][2]:
                                            pass
                                        bg.pop(0)
                                    ph = pending.pop(0)[1]
                                    vup_head(ph)
                                    if ph == 3:
                                        bg.append(("post", pg, post(pg)))
                                        if pg + 2 < ng:
                                            bg.append(("prep", pg + 2, prep(pg + 2)))
                            elif bg and i % 2 == 1:
                                try:
                                    next(bg[0][2])
                                except StopIteration:
                                    bg.pop(0)
                        assert not pending
                        for kind, gg, gen in bg:
                            for _ in gen:
                                pass
                        Sd.emit(3 + 3 * sq)

        with ExitStack() as st:
            Sd = Sched(nc, sems)
            banks, ptrs = psum_banks(st)
            wg = Buf(sb(st, "wg", [128, KC, DFF], BF16))
            wu = Buf(sb(st, "wu", [128, KC, DFF], BF16))
            wd = Buf(sb(st, "wd", [128, FC, D], BF16))
            for kc in range(KC):
                Sd.dma("pool", [lambda h, kc=kc: h.dma_start(out=wg.h[:, kc, :], in_=w_gate[kc * 128:(kc + 1) * 128, :]),
                                lambda h, kc=kc: h.dma_start(out=wu.h[:, kc, :], in_=w_up[kc * 128:(kc + 1) * 128, :])], "set_wgu", writes=[wg, wu])
            for f0 in range(0, FC, 2):
                Sd.dma("pool", [lambda h, f0=f0: h.dma_start(out=wd.h[:, f0:f0 + 2, :], in_=w_down[f0 * 128:(f0 + 2) * 128, :].rearrange("(f p) n -> p f n", p=128))],
                       "set_wd", writes=[wd])
            gfb = Buf(sb(st, "gfb", [128, D], F32))
            Sd.dma("sp", [lambda h: h.dma_start(out=gfb.h[:], in_=g_final.partition_broadcast(128))], "set_gf", writes=[gfb])
            a2T = Buf(sb(st, "a2T", [128, KC], F32))
            sh2T = Buf(sb(st, "sh2T", [128, KC], F32))
            ga2b = Buf(sb(st, "ga2b", [128, D], F32))
            xts = bufs(st, "xt", [128, D], F32, 4)
            sss = bufs(st, "ss", [128, 4], F32, 2)
            tmps = bufs(st, "tmp", [128, D], F32, 2)
            hbs = bufs(st, "hb", [128, D], BF16, 4)
            hTs = bufs(st, "hT", [128, KC, 256], BF16, 2)
            sgs = bufs(st, "sg", [128, 256], F32, 2)
            actT = Buf(sb(st, "actT", [128, FC, 256], BF16))
            mts = bufs(st, "mt", [128, 512], F32, 2)
            yts = bufs(st, "yt", [128, D], F32, 2)
            ots = tmps
            ss3 = bufs(st, "ss3", [128, 4], F32, 2)
            ngrp = (S + QS) // 256

            def pd_tiles(g):
                for t in range(2):
                    xt = xts[(2 * g + t) % 4]
                    r0 = g * 256 + t * 128
                    Sd.dma("sp", [lambda h, xt=xt, r0=r0: h.dma_start(out=xt.h[:], in_=x1scr[r0:r0 + 128, :])], "xd%d" % ((2 * g + t) % 4),
                           reads=[Wrap(x1_res[r0 // 128])], writes=[xt])
                    front_tile(Sd, xt, sss[(2 * g + t) % 2], hbs[(g % 2) * 2 + t])

            for g in range(ngrp):
                r_base = g * 256
                sq = 0 if r_base < S else 1
                if g == 0 or r_base == S:
                    load_modT(Sd, sh2T, sq, 3)
                    load_modT(Sd, a2T, sq, 4)
                    load_mod(Sd, ga2b, sq, 5)
                hT = hTs[g % 2]
                gx = [xts[(2 * g + t) % 4] for t in range(2)]
                if g == 0:
                    pd_tiles(0)
                front_group(Sd, hbs[(g % 2) * 2:(g % 2) * 2 + 2], a2T, sh2T, ptrs, hT, 2)
                if g + 1 < ngrp:
                    pd_tiles(g + 1)
                for fc in range(FC):
                    bk = banks[fc % 3]
                    sg = sgs[fc % 2]
                    for kc in range(KC):
                        Sd.add("pe", lambda h, kc=kc, fc=fc, bk=bk, hT=hT: h.matmul(bk.h[:, 0:256], lhsT=wg.h[:, kc, fc * 128:(fc + 1) * 128], rhs=hT.h[:, kc, :],
                                                                            start=(kc == 0), stop=(kc == KC - 1)), reads=[wg, hT], writes=[bk])
                    for kc in range(KC):
                        Sd.add("pe", lambda h, kc=kc, fc=fc, bk=bk, hT=hT: h.matmul(bk.h[:, 256:512], lhsT=wu.h[:, kc, fc * 128:(fc + 1) * 128], rhs=hT.h[:, kc, :],
                                                                            start=(kc == 0), stop=(kc == KC - 1)), reads=[wu, hT], writes=[bk])
                    Sd.add("act", lambda h, bk=bk, sg=sg: h.activation(out=sg.h[:], in_=bk.h[:, 0:256], func=AF.Silu), reads=[bk], writes=[sg])
                    Sd.add("dve", lambda h, fc=fc, bk=bk, sg=sg: h.tensor_tensor(out=actT.h[:, fc, :], in0=bk.h[:, 256:512], in1=sg.h[:], op=ALU.mult),
                           reads=[bk, sg], writes=[actT])
                for t in range(2):
                    yt = yts[t]
                    ot = ots[t]
                    s3 = ss3[t]
                    for n2 in range(2):
                        bk = banks[3 + (2 * t + n2) % 3]
                        mt = mts[n2]
                        for fc in range(FC):
                            Sd.add("pe", lambda h, fc=fc, t=t, n2=n2, bk=bk: h.matmul(bk.h[:], lhsT=actT.h[:, fc, t * 128:(t + 1) * 128],
                                                                                    rhs=wd.h[:, fc, n2 * 512:(n2 + 1) * 512],
                                                                                    start=(fc == 0), stop=(fc == FC - 1)), reads=[actT, wd], writes=[bk])
                        Sd.add("dve", lambda h, n2=n2, bk=bk, mt=mt: h.tensor_tensor(out=mt.h[:], in0=bk.h[:], in1=ga2b.h[:, n2 * 512:(n2 + 1) * 512], op=ALU.mult),
                               reads=[bk, ga2b], writes=[mt])
                        Sd.add("pool", lambda h, n2=n2, mt=mt, xg=gx[t], yt=yt: h.tensor_tensor(out=yt.h[:, n2 * 512:(n2 + 1) * 512], in0=mt.h[:],
                                                                                       in1=xg.h[:, n2 * 512:(n2 + 1) * 512], op=ALU.add),
                               reads=[mt, gx[t]], writes=[yt])
                    Sd.add("act", lambda h, yt=yt, ot=ot, s3=s3: h.activation(out=ot.h[:], in_=yt.h[:], func=AF.Square, accum_out=s3.h[:, 0:1]),
                           reads=[yt], writes=[ot, s3])
                    Sd.add("act", lambda h, s3=s3: h.activation(out=s3.h[:, 1:2], in_=s3.h[:, 0:1], func=AF.Sqrt, scale=1.0 / D, bias=epsb.h[:, 0:1]),
                           reads=[s3, epsb], writes=[s3])
                    Sd.add("dve", lambda h, s3=s3: h.reciprocal(out=s3.h[:, 2:3], in_=s3.h[:, 1:2]), reads=[s3], writes=[s3])
                    Sd.add("act", lambda h, yt=yt, ot=ot, s3=s3: h.activation(out=ot.h[:], in_=yt.h[:], func=AF.Identity, scale=s3.h[:, 2:3]),
                           reads=[yt, s3], writes=[ot])
                    Sd.add("pool", lambda h, ot=ot: h.tensor_tensor(out=ot.h[:], in0=ot.h[:], in1=gfb.h[:], op=ALU.mult), reads=[ot, gfb], writes=[ot])
                    r0 = r_base + t * 128
                    if r0 < S:
                        dst = y_p[r0:r0 + 128, :]
                    else:
                        dst = y_s[r0 - S:r0 - S + 128, :]
                    Sd.dma("pool", [lambda h, ot=ot, dst=dst: h.dma_start(out=dst, in_=ot.h[:])], "yst%d" % t, reads=[ot])
            Sd.emit(7)
    return nc


def _tables(cfg):
    S, N2, PC = cfg.S, cfg.N2, cfg.PC
    f64 = np.float64
    pos = np.arange(S, dtype=np.float32)
    inv = (1.0 / (np.float32(10000.0) ** (np.arange(0, 64, 2, dtype=np.float32) / np.float32(64)))).astype(np.float32)
    ang = (pos[None, :] * inv[:, None]).astype(np.float32)
    rope_c = np.concatenate([np.cos(ang), np.cos(ang)], 0).astype(np.float32)
    rope_s = np.concatenate([np.sin(ang), np.sin(ang)], 0).astype(np.float32)
    norm = 1.0 / math.sqrt(S * 64)
    m = np.arange(64, dtype=f64)
    C64 = np.cos(2 * np.pi * np.outer(m, m) / 64) * norm
    S64 = np.sin(2 * np.pi * np.outer(m, m) / 64) * norm
    cb = np.zeros((128, 128), f64)
    sbm = np.zeros((128, 128), f64)
    for gl in range(2):
        cb[gl * 64:(gl + 1) * 64, gl * 64:(gl + 1) * 64] = C64
        sbm[gl * 64:(gl + 1) * 64, gl * 64:(gl + 1) * 64] = -S64
    s1 = np.arange(128, dtype=f64)
    C1 = np.cos(2 * np.pi * np.outer(s1, s1) / 128)
    S1 = np.sin(2 * np.pi * np.outer(s1, s1) / 128)
    t1u = np.concatenate([C1, -S1], 1)
    t1v = np.concatenate([S1, C1], 1)
    s2 = np.arange(N2, dtype=f64)
    Tc = np.cos(2 * np.pi * np.outer(s2, s1) / S)
    Ts = np.sin(2 * np.pi * np.outer(s2, s1) / S)
    Tcp = np.tile(Tc, (PC, 1))
    Tsp = np.tile(Ts, (PC, 1))
    ta = np.tile(Tcp, (1, 4))
    C2 = np.cos(2 * np.pi * np.outer(s2, s2) / N2)
    S2 = np.sin(2 * np.pi * np.outer(s2, s2) / N2)

    def blk(M, k2s):
        n = len(k2s)
        out = np.zeros((128, PC * n), f64)
        for c in range(PC):
            out[c * N2:(c + 1) * N2, c * n:(c + 1) * n] = M[:, k2s]
        return out

    full = np.arange(N2)
    tb = dict(rope_c=rope_c, rope_s=rope_s, cb=cb, sb=sbm, t1u=t1u, t1v=t1v, ta=ta, ts=Tsp,
              bc_p=blk(C2, full), bs_p=blk(S2, full))
    tb["ts"] = np.tile(Tsp, (1, 2))
    tb = {k: np.ascontiguousarray(v, dtype=np.float32) for k, v in tb.items()}
    bq = []
    for r in range(4):
        k2s = np.arange(cfg.NQ2 * r, cfg.NQ2 * (r + 1))
        bq.append((np.ascontiguousarray(blk(C2, k2s), dtype=np.float32), np.ascontiguousarray(blk(S2, k2s), dtype=np.float32)))
    return tb, bq


_CACHE = {}


def _run(cfg, inputs):
    S, QS = cfg.S, cfg.QS
    f = lambda a: np.ascontiguousarray(np.asarray(a), dtype=np.float32)
    x_prompt, x_sample = f(inputs["x_prompt"]), f(inputs["x_sample"])
    c_prompt, c_sample = f(inputs["c_prompt"]), f(inputs["c_sample"])
    tb, bq = _tables(cfg)
    shared = dict(
        w_ada=f(inputs["w_ada"][0]), b_ada=f(inputs["b_ada"][0]).reshape(1, -1), g_mix=f(inputs["g_mix"][0]).reshape(1, -1),
        g_ffn=f(inputs["g_ffn"][0]).reshape(1, -1), g_final=f(inputs["g_final"]).reshape(1, -1), w_in=f(inputs["w_in"][0]),
        gqT=f(f(inputs["g_q_lat"][0]).reshape(2, 128).T), w_uq=f(inputs["w_uq"][0]),
        gkvT=f(f(inputs["g_kv_lat"][0]).reshape(128, 1)), w_ukv=f(inputs["w_ukv"][0]), w_four=f(inputs["w_four"][0]),
        w_out=f(inputs["w_out"][0]), w_gate=f(inputs["w_gate"][0]), w_up=f(inputs["w_up"][0]), w_down=f(inputs["w_down"][0]),
    )
    shared.update(tb)
    in_maps = []
    for i in range(N_CORES):
        sidx, r = i // 4, i % 4
        cc = np.stack([c_prompt[i], c_sample[sidx]], 0)
        cT = f(cc.reshape(2, KC, 128).transpose(2, 1, 0))
        m = dict(shared)
        m.update(xs_p=x_prompt[i], xs_s=x_sample[sidx], xm_s=f(x_sample[sidx, r * QS:(r + 1) * QS]), cT=cT,
                 bc_s=bq[r][0], bs_s=bq[r][1],
                 rope_cm=f(tb["rope_c"][:, r * QS:(r + 1) * QS]), rope_sm=f(tb["rope_s"][:, r * QS:(r + 1) * QS]))
        in_maps.append(m)
    nc = build_program(cfg)
    res = run_bass_kernel_spmd(nc, in_maps, core_ids=list(range(N_CORES)))
    y_prompt = np.stack([np.asarray(res.results[i]["y_p"], dtype=np.float32) for i in range(N_CORES)], 0)
    y_sample = np.zeros((2, S, D), np.float32)
    for i in range(N_CORES):
        y_sample[i // 4, (i % 4) * QS:(i % 4 + 1) * QS] = np.asarray(res.results[i]["y_s"], dtype=np.float32)
    return y_prompt, y_sample


def kernel(**inputs):
    S = int(np.asarray(inputs["x_prompt"]).shape[1])
    return _run(Cfg(S), inputs)
```

```python
import math
from contextlib import ExitStack

import numpy as np
import concourse.bass as bass
import concourse.mybir as mybir
from concourse.bass_utils import run_bass_kernel_spmd

F32 = mybir.dt.float32
BF16 = mybir.dt.bfloat16
AF = mybir.ActivationFunctionType
ALU = mybir.AluOpType

D = 1024
KC = 8
DIN = 960
DFF = 2816
FC = DFF // 128
EPS = 1e-6
SCALE = 1.0 / math.sqrt(192.0)
N_CORES = 8
import os
_PH = os.environ.get("K_PHASES")
PHASES = None if not _PH else set(int(v) for v in _PH.split(","))
MAXOPS = int(os.environ.get("K_MAXOPS", "0"))


class Res:
    __slots__ = ("name", "writer", "readers")

    def __init__(self, name=""):
        self.name = name
        self.writer = None
        self.readers = []


class Buf:
    __slots__ = ("h", "r")

    def __init__(self, h, name=""):
        self.h = h
        self.r = Res(name)


class Op:
    __slots__ = ("eng", "fn", "deps", "signaled", "cnt", "sem", "is_dma", "ndma", "ph")

    def __init__(self, eng, fn):
        self.eng = eng
        self.fn = fn
        self.deps = []
        self.signaled = False
        self.cnt = 0
        self.sem = None
        self.is_dma = False
        self.ndma = 0
        self.ph = None


ENGS = ("pe", "act", "dve", "pool", "sp")
HANDLES = {"pe": "tensor", "act": "scalar", "dve": "vector", "pool": "gpsimd", "sp": "sync"}


class SemTable:
    def __init__(self, nc, stack):
        self.nc = nc
        self.stack = stack
        self.eng = {e: [stack.enter_context(nc.semaphore("s_" + e)), 0] for e in ENGS if e != "sp"}
        self.dma = {}

    def dma_sem(self, key):
        if key not in self.dma:
            self.dma[key] = [self.stack.enter_context(self.nc.semaphore("d_%d" % len(self.dma))), 0]
        return self.dma[key]


class Sched:
    def __init__(self, nc, sems):
        self.nc = nc
        self.sems = sems
        self.ops = {e: [] for e in ENGS}

    def _deps(self, op, reads, writes):
        deps = []
        for r in reads:
            if r.writer is not None:
                deps.append((0, r.writer))
        for w in writes:
            if w.writer is not None:
                deps.append((1, w.writer))
            for rd in w.readers:
                deps.append((2, rd))
        out = []
        seen = set()
        for kind, d in deps:
            if d is op or id(d) in seen or d.ph is not op.ph:
                continue
            if (not d.is_dma) and (not op.is_dma) and d.eng == op.eng:
                if op.eng == "pe" or kind != 0:
                    continue
            seen.add(id(d))
            d.signaled = True
            out.append(d)
        op.deps = out
        for r in reads:
            if not op.is_dma:
                r.readers = [x for x in r.readers if x.is_dma or x.eng != op.eng or x.ph is not op.ph]
            r.readers.append(op)
        for w in writes:
            w.writer = op
            w.readers = []

    def add(self, eng, fn, reads=(), writes=()):
        self.nrec = getattr(self, "nrec", 0) + 1
        if MAXOPS and self.nrec > MAXOPS:
            return None
        op = Op(eng, fn)
        op.ph = self
        self._deps(op, [b.r for b in reads], [b.r for b in writes])
        self.ops[eng].append(op)
        return op

    def dma(self, eng, fns, key, reads=(), writes=()):
        self.nrec = getattr(self, "nrec", 0) + 1
        if MAXOPS and self.nrec > MAXOPS:
            return None
        op = Op(eng, fns)
        op.is_dma = True
        op.ndma = len(fns)
        op.sem = key
        op.ph = self
        self._deps(op, [b.r for b in reads], [b.r for b in writes])
        self.ops[eng].append(op)
        return op

    def emit(self, phase=None):
        nc = self.nc
        if PHASES is not None and phase not in PHASES:
            return
        for e in ENGS:
            for op in self.ops[e]:
                if op.is_dma:
                    ent = self.sems.dma_sem(op.sem)
                    ent[1] += 16 * op.ndma
                    op.cnt = ent[1]
                    op.sem = ent[0]
                elif op.signaled:
                    ent = self.sems.eng[e]
                    ent[1] += 1
                    op.cnt = ent[1]
                    op.sem = ent[0]

        def run(e, h):
            waited = {}
            for op in self.ops[e]:
                for d in op.deps:
                    key = id(d.sem)
                    if waited.get(key, 0) >= d.cnt:
                        continue
                    waited[key] = d.cnt
                    if os.environ.get("K_TRACE"):
                        print("TRACE", e, "wait", d.sem.name if hasattr(d.sem, "name") else d.sem, d.cnt, "dep_eng", d.eng, "dma" if d.is_dma else "")
                    h.wait_ge(d.sem, d.cnt)
                if os.environ.get("K_TRACE"):
                    print("TRACE", e, "op", "dma" if op.is_dma else "", "sig" if op.signaled else "", op.cnt)
                if op.is_dma:
                    for f in op.fn:
                        f(h).then_inc(op.sem, 16)
                else:
                    ins = op.fn(h)
                    if op.signaled:
                        ins.then_inc(op.sem, 1)
            done = {}
            for op in self.ops[e]:
                if op.is_dma:
                    done[id(op.sem)] = (op.sem, op.cnt)
            for s, v in done.values():
                h.wait_ge(s, v)

        with nc.Block() as block:
            for e in ENGS:
                if self.ops[e]:
                    getattr(block, HANDLES[e])(lambda h, e=e: run(e, h))


class Cfg:
    def __init__(self, S):
        self.S = S
        self.N2 = S // 128
        self.PC = 128 // self.N2
        self.NT = S // 128
        self.QS = S // 4
        self.NQ2 = self.N2 // 4
        assert self.N2 * self.PC == 128 and self.QS % 512 == 0


def build_program(cfg):
    S, N2, PC, NT, QS = cfg.S, cfg.N2, cfg.PC, cfg.NT, cfg.QS
    NG = 128 // PC
    nc = bass.Bass("TRN2", target_bir_lowering=False)

    def din(name, shape, dt=F32):
        return nc.dram_tensor(name, list(shape), dt, kind="ExternalInput").ap()

    xs = [din("xs_p", [S, D]), din("xs_s", [S, D])]
    xm_s = din("xm_s", [QS, D])
    cT = din("cT", [128, KC, 2])
    w_ada = din("w_ada", [D, 6 * D])
    b_ada = din("b_ada", [1, 6 * D])
    g_mix = din("g_mix", [1, D])
    g_ffn = din("g_ffn", [1, D])
    g_final = din("g_final", [1, D])
    w_in = din("w_in", [D, DIN])
    gqT = din("gqT", [128, 2])
    w_uq = din("w_uq", [256, 768])
    gkvT = din("gkvT", [128, 1])
    w_ukv = din("w_ukv", [128, 1024])
    w_four = din("w_four", [8, 64, 64])
    w_out = din("w_out", [D, D])
    w_gate = din("w_gate", [D, DFF])
    w_up = din("w_up", [D, DFF])
    w_down = din("w_down", [DFF, D])
    rope_c = din("rope_c", [64, S])
    rope_s = din("rope_s", [64, S])
    rope_cm = din("rope_cm", [64, QS])
    rope_sm = din("rope_sm", [64, QS])
    cb_d = din("cb", [128, 128])
    sb_d = din("sb", [128, 128])
    t1u_d = din("t1u", [128, 256])
    t1v_d = din("t1v", [128, 256])
    ta_d = din("ta", [128, 512])
    ts_d = din("ts", [128, 256])
    bc_d = [din("bc_p", [128, 128]), din("bc_s", [128, 32])]
    bs_d = [din("bs_p", [128, 128]), din("bs_s", [128, 32])]
    y_p = nc.dram_tensor("y_p", [S, D], F32, kind="ExternalOutput").ap()
    y_s = nc.dram_tensor("y_s", [QS, D], F32, kind="ExternalOutput").ap()
    yf = [nc.dram_tensor("yf_p", [S, 512], BF16).ap(), nc.dram_tensor("yf_s", [QS, 512], BF16).ap()]
    x1scr = nc.dram_tensor("x1scr", [S + QS, D], F32).ap()
    modscr = nc.dram_tensor("modscr", [2, 6, D], F32).ap()
    yf_res = [[Res() for _ in range(S // 128)], [Res() for _ in range(QS // 128)]]
    x1_res = [Res() for _ in range((S + QS) // 128)]

    with ExitStack() as top:
        top.enter_context(nc.allow_low_precision("bf16 matmul operands, fp32 accumulation"))
        sems = SemTable(nc, top)

        uid = [0]

        def sb(st, name, shape, dt):
            uid[0] += 1
            return st.enter_context(nc.sbuf_tensor("%s_%d" % (name, uid[0]), list(shape), dt))

        def bufs(st, name, shape, dt, n):
            return [Buf(sb(st, "%s%d" % (name, i), shape, dt)) for i in range(n)]

        def psum_banks(st, nf=6):
            uid[0] += 1
            banks = [Buf(st.enter_context(nc.psum_tensor("pb%d_%d" % (i, uid[0]), [128, 512], F32))) for i in range(nf)]
            ptr = [Buf(st.enter_context(nc.psum_tensor("ptr%d_%d" % (i, uid[0]), [128, 1024], BF16))) for i in range(8 - nf)]
            if len(ptr) == 1:
                ptr = [ptr[0], ptr[0]]
            return banks, ptr

        class Wrap:
            def __init__(self, r):
                self.r = r

        ident = Buf(sb(top, "ident", [128, 128], BF16))
        ones_bf = Buf(sb(top, "ones_bf", [128, 128], BF16))
        epsb = Buf(sb(top, "epsb", [128, 1], F32))

        with ExitStack() as w1:
            win = Buf(sb(w1, "win", [128, KC, 1024], BF16))
            wuq = Buf(sb(w1, "wuq", [128, 2, 1024], BF16))
            wukv = Buf(sb(w1, "wukv", [128, 1024], BF16))
            wukT = Buf(sb(w1, "wukT", [128, 4, 128], BF16))
            gkv = Buf(sb(w1, "gkv", [128, 1], F32))
            abh = Buf(sb(w1, "abh", [128, 8, 128], BF16))
            t1u = Buf(sb(w1, "t1u", [128, 256], BF16))
            t1v = Buf(sb(w1, "t1v", [128, 256], BF16))
            ta = Buf(sb(w1, "ta", [128, 512], F32))
            tsn = Buf(sb(w1, "tsn", [128, 256], F32))
            bcs = [Buf(sb(w1, "bc0", [128, 128], BF16)), Buf(sb(w1, "bc1", [128, 32], BF16))]
            bss = [Buf(sb(w1, "bs0", [128, 128], BF16)), Buf(sb(w1, "bs1", [128, 32], BF16))]

            with ExitStack() as st:
                Sd = Sched(nc, sems)
                banks, ptr = psum_banks(st)
                ptr = ptr[0]
                identf = Buf(sb(st, "identf", [128, 128], F32))
                Sd.add("pool", lambda h: h.memset(identf.h[:], 0.0), writes=[identf])
                Sd.add("pool", lambda h: h.affine_select(out=identf.h[:], in_=identf.h[:], pattern=[[-1, 128]],
                                                         compare_op=ALU.not_equal, fill=1.0, base=0,
                                                         channel_multiplier=1), reads=[identf], writes=[identf])
                Sd.add("dve", lambda h: h.tensor_copy(out=ident.h[:], in_=identf.h[:]), reads=[identf], writes=[ident])
                Sd.add("pool", lambda h: h.memset(ones_bf.h[:], 1.0), writes=[ones_bf])
                Sd.add("pool", lambda h: h.memset(epsb.h[:], EPS), writes=[epsb])
                Sd.dma("pool", [lambda h: h.dma_start(out=win.h[:, :, 0:DIN], in_=w_in.rearrange("(kc p) n -> p kc n", p=128)),
                                lambda h: h.dma_start(out=win.h[:, :, 960:992], in_=w_in[:, 416:448].rearrange("(kc p) n -> p kc n", p=128)),
                                lambda h: h.dma_start(out=win.h[:, :, 992:1024], in_=w_in[:, 384:416].rearrange("(kc p) n -> p kc n", p=128))],
                       "set_win", writes=[win])
                Sd.add("pool", lambda h: h.tensor_scalar(out=win.h[:, :, 960:992], in0=win.h[:, :, 960:992], scalar1=-1.0,
                                                         scalar2=None, op0=ALU.mult), reads=[win], writes=[win])
                Sd.dma("pool", [lambda h: h.dma_start(out=wukv.h[:], in_=w_ukv),
                                lambda h: h.dma_start(out=t1u.h[:], in_=t1u_d),
                                lambda h: h.dma_start(out=t1v.h[:], in_=t1v_d),
                                lambda h: h.dma_start(out=bcs[0].h[:], in_=bc_d[0]),
                                lambda h: h.dma_start(out=bcs[1].h[:], in_=bc_d[1]),
                                lambda h: h.dma_start(out=bss[0].h[:], in_=bs_d[0]),
                                lambda h: h.dma_start(out=bss[1].h[:], in_=bs_d[1])],
                       "set_misc", writes=[wukv, t1u, t1v, bcs[0], bcs[1], bss[0], bss[1]])
                Sd.dma("sp", [lambda h: h.dma_start(out=ta.h[:], in_=ta_d),
                              lambda h: h.dma_start(out=tsn.h[:], in_=ts_d),
                              lambda h: h.dma_start(out=gkv.h[:], in_=gkvT)], "set_f32", writes=[ta, tsn, gkv])
                wuqf = Buf(sb(st, "wuqf", [128, 2, 768], F32))
                gq = Buf(sb(st, "gq", [128, 2], F32))
                Sd.dma("sp", [lambda h: h.dma_start(out=wuqf.h[:], in_=w_uq.rearrange("(kc p) n -> p kc n", p=128)),
                              lambda h: h.dma_start(out=gq.h[:], in_=gqT)], "set_wuq", writes=[wuqf, gq])
                for kc in range(2):
                    Sd.add("dve", lambda h, kc=kc: h.tensor_scalar(out=wuqf.h[:, kc, :], in0=wuqf.h[:, kc, :], scalar1=gq.h[:, kc:kc + 1],
                                                                   scalar2=None, op0=ALU.mult), reads=[wuqf, gq], writes=[wuqf])
                    Sd.add("dve", lambda h, kc=kc: h.tensor_copy(out=wuq.h[:, kc, 0:768], in_=wuqf.h[:, kc, :]), reads=[wuqf], writes=[wuq])
                    for hd in range(4):
                        b0 = hd * 192 + 128
                        Sd.add("dve", lambda h, kc=kc, hd=hd, b0=b0: h.tensor_scalar(
                            out=wuq.h[:, kc, 768 + hd * 64:768 + hd * 64 + 32], in0=wuqf.h[:, kc, b0 + 32:b0 + 64],
                            scalar1=-1.0, scalar2=None, op0=ALU.mult), reads=[wuqf], writes=[wuq])
                        Sd.add("dve", lambda h, kc=kc, hd=hd, b0=b0: h.tensor_copy(
                            out=wuq.h[:, kc, 768 + hd * 64 + 32:768 + hd * 64 + 64], in_=wuqf.h[:, kc, b0:b0 + 32]),
                            reads=[wuqf], writes=[wuq])
                for hd in range(4):
                    Sd.add("pe", lambda h, hd=hd: h.transpose(out=ptr.h[:, hd * 128:(hd + 1) * 128],
                                                              in_=wukv.h[:, hd * 256:hd * 256 + 128], identity=ident.h[:]),
                           reads=[wukv, ident], writes=[ptr])
                Sd.add("dve", lambda h: h.tensor_copy(out=wukT.h[:].rearrange("p a b -> p (a b)"), in_=ptr.h[:, 0:512]),
                       reads=[ptr], writes=[wukT])
                cbt = Buf(sb(st, "cbt", [128, 128], F32))
                sbt = Buf(sb(st, "sbt", [128, 128], F32))
                wblk = Buf(sb(st, "wblk", [128, 4, 128], F32))
                Sd.add("pool", lambda h: h.memset(wblk.h[:], 0.0), writes=[wblk])
                fl = [lambda h: h.dma_start(out=cbt.h[:], in_=cb_d), lambda h: h.dma_start(out=sbt.h[:], in_=sb_d)]
                for cc in range(4):
                    for gl in range(2):
                        fl.append(lambda h, cc=cc, gl=gl: h.dma_start(out=wblk.h[gl * 64:(gl + 1) * 64, cc, gl * 64:(gl + 1) * 64],
                                                                      in_=w_four[2 * cc + gl]))
                Sd.dma("sp", fl, "set_four", writes=[cbt, sbt, wblk])
                for cc in range(4):
                    bk = banks[cc % 2]
                    Sd.add("pe", lambda h, cc=cc, bk=bk: h.matmul(bk.h[:, 0:128], lhsT=cbt.h[:], rhs=wblk.h[:, cc, :], start=True, stop=True),
                           reads=[cbt, wblk], writes=[bk])
                    Sd.add("pe", lambda h, cc=cc, bk=bk: h.matmul(bk.h[:, 128:256], lhsT=sbt.h[:], rhs=wblk.h[:, cc, :], start=True, stop=True),
                           reads=[sbt, wblk], writes=[bk])
                    Sd.add("dve", lambda h, cc=cc, bk=bk: h.tensor_copy(
                        out=abh.h[:, 2 * cc:2 * cc + 2, :].rearrange("p f (u c) -> p u f c", u=2),
                        in_=bk.h[:, 0:256].rearrange("p (u f c) -> p u f c", u=2, f=2)), reads=[bk], writes=[abh])
                cts = Buf(sb(st, "cts", [128, KC, 2], F32))
                ctb = Buf(sb(st, "ctb", [128, KC, 2], BF16))
                modr = Buf(sb(st, "modr", [2, 6 * D], F32))
                bad = Buf(sb(st, "bad", [2, 6 * D], F32))
                gm2 = Buf(sb(st, "gm2", [2, D], F32))
                gf2 = Buf(sb(st, "gf2", [2, D], F32))
                Sd.dma("sp", [lambda h: h.dma_start(out=cts.h[:], in_=cT),
                              lambda h: h.dma_start(out=bad.h[:], in_=b_ada.partition_broadcast(2)),
                              lambda h: h.dma_start(out=gm2.h[:], in_=g_mix.partition_broadcast(2)),
                              lambda h: h.dma_start(out=gf2.h[:], in_=g_ffn.partition_broadcast(2))],
                       "set_c", writes=[cts, bad, gm2, gf2])
                Sd.add("act", lambda h: h.activation(out=ctb.h[:], in_=cts.h[:], func=AF.Silu), reads=[cts], writes=[ctb])
                wa = bufs(st, "wa", [128, KC, 512], BF16, 2)
                for nt in range(12):
                    wb = wa[nt % 2]
                    bk = banks[2 + nt % 2]
                    Sd.dma("pool", [lambda h, nt=nt, wb=wb: h.dma_start(
                        out=wb.h[:], in_=w_ada[:, nt * 512:(nt + 1) * 512].rearrange("(kc p) n -> p kc n", p=128))],
                        "wa%d" % (nt % 2), writes=[wb])
                    for kc in range(KC):
                        Sd.add("pe", lambda h, kc=kc, wb=wb, bk=bk: h.matmul(bk.h[0:2, :], lhsT=ctb.h[:, kc, :], rhs=wb.h[:, kc, :],
                                                                            start=(kc == 0), stop=(kc == KC - 1)),
                               reads=[ctb, wb], writes=[bk])
                    Sd.add("dve", lambda h, nt=nt, bk=bk: h.tensor_tensor(out=modr.h[:, nt * 512:(nt + 1) * 512], in0=bk.h[0:2, :],
                                                                         in1=bad.h[:, nt * 512:(nt + 1) * 512], op=ALU.add),
                           reads=[bk, bad], writes=[modr])
                Sd.add("dve", lambda h: h.scalar_tensor_tensor(out=modr.h[:, D:2 * D], in0=modr.h[:, D:2 * D], scalar=1.0, in1=gm2.h[:],
                                                               op0=ALU.add, op1=ALU.mult), reads=[modr, gm2], writes=[modr])
                Sd.add("dve", lambda h: h.scalar_tensor_tensor(out=modr.h[:, 4 * D:5 * D], in0=modr.h[:, 4 * D:5 * D], scalar=1.0, in1=gf2.h[:],
                                                               op0=ALU.add, op1=ALU.mult), reads=[modr, gf2], writes=[modr])
                mod_res = Wrap(Res())
                Sd.dma("pool", [lambda h: h.dma_start(out=modscr.rearrange("s j d -> s (j d)"), in_=modr.h[:])], "modst",
                       reads=[modr], writes=[mod_res])
                Sd.emit(0)

            def load_mod(Sd, dst, s, j):
                Sd.dma("sp", [lambda h: h.dma_start(out=dst.h[:], in_=modscr[s, j:j + 1, :].partition_broadcast(128))],
                       "mod_%d" % j, writes=[dst])

            def load_modT(Sd, dst, s, j):
                Sd.dma("sp", [lambda h: h.dma_start(out=dst.h[:], in_=modscr[s, j, :].rearrange("(kc p) -> p kc", p=128),
                                                    allow_slow_non_contiguous=True)], "modT_%d" % j, writes=[dst])

            def front_tile(Sd, xt, ss, xn):
                Sd.add("act", lambda h: h.activation(out=xn.h[:], in_=xt.h[:], func=AF.Square, accum_out=ss.h[:, 0:1]),
                       reads=[xt], writes=[xn, ss])
                Sd.add("act", lambda h: h.activation(out=ss.h[:, 1:2], in_=ss.h[:, 0:1], func=AF.Sqrt, scale=1.0 / D, bias=epsb.h[:, 0:1]),
                       reads=[ss, epsb], writes=[ss])
                Sd.add("dve", lambda h: h.reciprocal(out=ss.h[:, 2:3], in_=ss.h[:, 1:2]), reads=[ss], writes=[ss])
                Sd.add("act", lambda h: h.activation(out=xn.h[:], in_=xt.h[:], func=AF.Identity, scale=ss.h[:, 2:3]),
                       reads=[xt, ss], writes=[xn])

            def front_group(Sd, xns, aT, shT, ptrs, hT, G):
                for kp in range(KC // 2):
                    ptr = ptrs[kp % 2]
                    for k2 in range(2):
                        kc = 2 * kp + k2
                        for t in range(G):
                            Sd.add("pe", lambda h, kc=kc, k2=k2, t=t, ptr=ptr: h.transpose(
                                out=ptr.h[:, (k2 * G + t) * 128:(k2 * G + t + 1) * 128], in_=xns[t].h[:, kc * 128:(kc + 1) * 128],
                                identity=ident.h[:]), reads=[xns[t], ident], writes=[ptr])
                    for k2 in range(2):
                        kc = 2 * kp + k2
                        Sd.add("dve", lambda h, kc=kc, k2=k2, ptr=ptr: h.tensor_scalar(
                            out=hT.h[:, kc, 0:G * 128], in0=ptr.h[:, k2 * G * 128:(k2 + 1) * G * 128],
                            scalar1=aT.h[:, kc:kc + 1], scalar2=shT.h[:, kc:kc + 1], op0=ALU.mult, op1=ALU.add),
                            reads=[ptr, aT, shT], writes=[hT])

            for sq in range(2):
                x_seq = xs[sq]
                x_my = xs[0] if sq == 0 else xm_s
                n_my = S if sq == 0 else QS
                x1_off = 0 if sq == 0 else S
                with ExitStack() as sst:
                    kcT = Buf(sb(sst, "kcT", [128, S], BF16))
                    kpeT = Buf(sb(sst, "kpeT", [128, S], BF16))
                    vc = Buf(sb(sst, "vc", [128, NT, 128], BF16))
                    with ExitStack() as fst:
                        finT = Buf(sb(fst, "finT", [128, 4, S], BF16))
                        with ExitStack() as st:
                            Sd = Sched(nc, sems)
                            banks, ptrs = psum_banks(st)
                            ptr = ptrs[0]
                            a1T = Buf(sb(st, "a1T", [128, KC], F32))
                            sh1T = Buf(sb(st, "sh1T", [128, KC], F32))
                            load_modT(Sd, sh1T, sq, 0)
                            load_modT(Sd, a1T, sq, 1)
                            xts = bufs(st, "xt", [128, D], F32, 3)
                            sss = bufs(st, "ss", [128, 4], F32, 2)
                            xnb = bufs(st, "xn", [128, D], BF16, 8)
                            hTs = bufs(st, "hT", [128, KC, 512], BF16, 2)
                            rcs = bufs(st, "rc", [64, 512], F32, 2)
                            rss = bufs(st, "rs", [64, 512], F32, 2)
                            sqb = Buf(sb(st, "sqb", [128, 512], BF16))
                            rk = Buf(sb(st, "rk", [128, 512], F32))
                            t1 = Buf(sb(st, "t1", [64, 512], F32))
                            t2 = Buf(sb(st, "t2", [64, 512], F32))
                            def pa_tiles(g):
                                for t in range(4):
                                    ti = 4 * g + t
                                    xt = xts[ti % 3]
                                    r0 = g * 512 + t * 128
                                    Sd.dma("sp", [lambda h, xt=xt, r0=r0: h.dma_start(out=xt.h[:], in_=x_seq[r0:r0 + 128, :])],
                                           "xt%d" % (ti % 3), writes=[xt])
                                    front_tile(Sd, xt, sss[ti % 2], xnb[(g % 2) * 4 + t])

                            Sd.add("pool", lambda h: h.memset(kpeT.h[64:128, :], 0.0), writes=[kpeT])
                            pa_tiles(0)
                            for g in range(S // 512):
                                hT = hTs[g % 2]
                                rc, rs = rcs[g % 2], rss[g % 2]
                                Sd.dma("sp", [lambda h, g=g, rc=rc: h.dma_start(out=rc.h[:], in_=rope_c[:, g * 512:(g + 1) * 512]),
                                              lambda h, g=g, rs=rs: h.dma_start(out=rs.h[:], in_=rope_s[:, g * 512:(g + 1) * 512])],
                                       "rope%d" % (g % 2), writes=[rc, rs])
                                front_group(Sd, xnb[(g % 2) * 4:(g % 2) * 4 + 4], a1T, sh1T, ptrs, hT, 4)
                                if g + 1 < S // 512:
                                    pa_tiles(g + 1)
                                cols = slice(g * 512, (g + 1) * 512)

                                def zmm(bk, c0, m, hT=hT):
                                    for kc in range(KC):
                                        Sd.add("pe", lambda h, kc=kc: h.matmul(bk.h[0:m, :], lhsT=win.h[:, kc, c0:c0 + m], rhs=hT.h[:, kc, :],
                                                                              start=(kc == 0), stop=(kc == KC - 1)),
                                               reads=[win, hT], writes=[bk])
                                bkv, bsum = banks[0], banks[1]
                                bpe, brot = banks[2], banks[3]

                                def fin(cc):
                                    bk = banks[4 + cc % 2]
                                    zmm(bk, 448 + cc * 128, 128)
                                    Sd.add("act", lambda h, cc=cc, bk=bk, cols=cols: h.activation(out=finT.h[:, cc, cols], in_=bk.h[:], func=AF.Copy),
                                           reads=[bk], writes=[finT])

                                zmm(bkv, 256, 128)
                                Sd.add("act", lambda h: h.activation(out=sqb.h[:], in_=bkv.h[:], func=AF.Square), reads=[bkv], writes=[sqb])
                                zmm(bpe, 384, 64)
                                zmm(brot, 960, 64)
                                fin(0)
                                Sd.add("pe", lambda h: h.matmul(bsum.h[:], lhsT=ones_bf.h[:], rhs=sqb.h[:], start=True, stop=True),
                                       reads=[ones_bf, sqb], writes=[bsum])
                                Sd.add("act", lambda h: h.activation(out=rk.h[:], in_=bsum.h[:], func=AF.Sqrt, scale=1.0 / 128, bias=epsb.h[:, 0:1]),
                                       reads=[bsum, epsb], writes=[rk])
                                Sd.add("dve", lambda h: h.reciprocal(out=rk.h[:], in_=rk.h[:]), reads=[rk], writes=[rk])
                                Sd.add("dve", lambda h: h.tensor_scalar(out=rk.h[:], in0=rk.h[:], scalar1=gkv.h[:, 0:1], scalar2=None, op0=ALU.mult),
                                       reads=[rk, gkv], writes=[rk])
                                Sd.add("dve", lambda h, cols=cols: h.tensor_tensor(out=kcT.h[:, cols], in0=bkv.h[:], in1=rk.h[:], op=ALU.mult),
                                       reads=[bkv, rk], writes=[kcT])
                                Sd.add("dve", lambda h, rc=rc: h.tensor_tensor(out=t1.h[:], in0=bpe.h[0:64, :], in1=rc.h[:], op=ALU.mult),
                                       reads=[bpe, rc], writes=[t1])
                                Sd.add("dve", lambda h, rs=rs: h.tensor_tensor(out=t2.h[:], in0=brot.h[0:64, :], in1=rs.h[:], op=ALU.mult),
                                       reads=[brot, rs], writes=[t2])
                                Sd.add("pool", lambda h, cols=cols: h.tensor_tensor(out=kpeT.h[0:64, cols], in0=t1.h[:], in1=t2.h[:], op=ALU.add),
                                       reads=[t1, t2], writes=[kpeT])
                                fin(1)
                                fin(2)
                                for t in range(4):
                                    Sd.add("pe", lambda h, t=t, g=g: h.transpose(out=ptr.h[:, t * 128:(t + 1) * 128],
                                                                                in_=kcT.h[:, g * 512 + t * 128:g * 512 + (t + 1) * 128],
                                                                                identity=ident.h[:]), reads=[kcT, ident], writes=[ptr])
                                Sd.add("dve", lambda h, g=g: h.tensor_copy(out=vc.h[:, 4 * g:4 * g + 4, :].rearrange("p a b -> p (a b)"),
                                                                          in_=ptr.h[:, 0:512]), reads=[ptr], writes=[vc])
                                fin(3)
                            Sd.emit(1 + 3 * sq)
                        with ExitStack() as st:
                            Sd = Sched(nc, sems)
                            banks, ptrs = psum_banks(st)
                            nk2 = N2 if sq == 0 else cfg.NQ2
                            ncol2 = PC * nk2
                            xall = Buf(sb(st, "xall", [128, 128, N2], BF16))
                            ytok = Buf(sb(st, "ytok", [128, nk2, 64], BF16))
                            tms = bufs(st, "tm", [128, 512], F32, 3)
                            tus = bufs(st, "tu", [128, 512], F32, 3)
                            y2s = bufs(st, "y2", [128, 512], BF16, 3)
                            bcx, bsx = bcs[sq], bss[sq]
                            it = 0
                            for hc in range(8):
                                cc = hc // 2
                                for s2 in range(0, N2, 4):
                                    bk = banks[(s2 // 4) % 2]
                                    for j in range(4):
                                        Sd.add("pe", lambda h, s2=s2, j=j, bk=bk, cc=cc, hc=hc: h.matmul(
                                            bk.h[:, j * 128:(j + 1) * 128],
                                            lhsT=finT.h[:, cc, :].rearrange("p (a b) -> p b a", b=N2)[:, s2 + j, :],
                                            rhs=abh.h[:, hc, :], start=True, stop=True), reads=[finT, abh], writes=[bk])
                                    Sd.add("dve", lambda h, s2=s2, bk=bk: h.tensor_copy(
                                        out=xall.h[:, :, s2:s2 + 4].rearrange("p c s -> p s c"),
                                        in_=bk.h[:].rearrange("p (s c) -> p s c", s=4)),
                                        reads=[bk], writes=[xall])
                                ngh = 64 // PC
                                def s1(gp, it):
                                    b1 = banks[2 + it % 2]
                                    tm, tu, y2 = tms[it % 3], tus[it % 3], y2s[it % 3]
                                    for j in range(2):
                                        c0 = (gp + j) * PC
                                        lu = xall.h[:, c0:c0 + PC, :].rearrange("p c s -> p (c s)")
                                        lv = xall.h[:, 64 + c0:64 + c0 + PC, :].rearrange("p c s -> p (c s)")
                                        Sd.add("pe", lambda h, j=j, lu=lu, b1=b1: h.matmul(b1.h[:, j * 256:(j + 1) * 256], lhsT=lu, rhs=t1u.h[:],
                                                                                          start=True, stop=False), reads=[xall, t1u], writes=[b1])
                                        Sd.add("pe", lambda h, j=j, lv=lv, b1=b1: h.matmul(b1.h[:, j * 256:(j + 1) * 256], lhsT=lv, rhs=t1v.h[:],
                                                                                          start=False, stop=True), reads=[xall, t1v], writes=[b1])
                                    Sd.add("dve", lambda h, b1=b1, tm=tm: h.tensor_tensor(out=tm.h[:], in0=b1.h[:], in1=ta.h[:], op=ALU.mult),
                                           reads=[b1, ta], writes=[tm])
                                    y1v = b1.h[:].rearrange("p (g r k) -> p g r k", g=2, r=2)
                                    tuv = tu.h[:].rearrange("p (g r k) -> p g r k", g=2, r=2)
                                    tsb = tsn.h[:].rearrange("p (g k) -> p g k", g=2)
                                    Sd.add("dve", lambda h, y1v=y1v, tuv=tuv, tsb=tsb: h.tensor_tensor(out=tuv[:, :, 0, :], in0=y1v[:, :, 1, :], in1=tsb,
                                                                                                  op=ALU.mult), reads=[b1, tsn], writes=[tu])
                                    Sd.add("dve", lambda h, y1v=y1v, tuv=tuv, tsb=tsb: h.tensor_tensor(out=tuv[:, :, 1, :], in0=y1v[:, :, 0, :], in1=tsb,
                                                                                                  op=ALU.mult), reads=[b1, tsn], writes=[tu])
                                    tmv = tm.h[:].rearrange("p (g r k) -> p g r k", g=2, r=2)
                                    y2v = y2.h[:].rearrange("p (g r k) -> p g r k", g=2, r=2)
                                    Sd.add("pool", lambda h, tmv=tmv, tuv=tuv, y2v=y2v: h.tensor_tensor(out=y2v[:, :, 0, :], in0=tmv[:, :, 0, :],
                                                                                                    in1=tuv[:, :, 0, :], op=ALU.add),
                                           reads=[tm, tu], writes=[y2])
                                    Sd.add("pool", lambda h, tmv=tmv, tuv=tuv, y2v=y2v: h.tensor_tensor(out=y2v[:, :, 1, :], in0=tmv[:, :, 1, :],
                                                                                                    in1=tuv[:, :, 1, :], op=ALU.subtract),
                                           reads=[tm, tu], writes=[y2])

                                def s2(gp, it):
                                    b2 = banks[4 + it % 2]
                                    y2 = y2s[it % 3]
                                    for j in range(2):
                                        Sd.add("pe", lambda h, j=j, y2=y2, b2=b2: h.matmul(b2.h[:, j * ncol2:(j + 1) * ncol2], lhsT=y2.h[:, j * 256:j * 256 + 128],
                                                                                          rhs=bcx.h[:], start=True, stop=False), reads=[y2, bcx], writes=[b2])
                                        Sd.add("pe", lambda h, j=j, y2=y2, b2=b2: h.matmul(b2.h[:, j * ncol2:(j + 1) * ncol2], lhsT=y2.h[:, j * 256 + 128:j * 256 + 256],
                                                                                          rhs=bsx.h[:], start=False, stop=True), reads=[y2, bsx], writes=[b2])
                                    c0 = gp * PC
                                    Sd.add("act", lambda h, b2=b2, c0=c0: h.activation(
                                        out=ytok.h[:, :, c0:c0 + 2 * PC].rearrange("p k c -> p c k"),
                                        in_=b2.h[:, 0:2 * ncol2].rearrange("p (c k) -> p c k", k=nk2), func=AF.Copy),
                                        reads=[b2], writes=[ytok])

                                gps = list(range(0, ngh, 2))
                                for idx in range(len(gps) + 1):
                                    if idx < len(gps):
                                        s1(gps[idx], it + idx)
                                    if idx >= 1:
                                        s2(gps[idx - 1], it + idx - 1)
                                it += len(gps)
                                kstep = min(8, nk2)
                                Sd.dma("pool", [lambda h, hc=hc, k0=k0: h.dma_start(
                                    out=yf[sq].rearrange("(k2 k1) c -> k1 k2 c", k1=128)[:, k0:k0 + kstep, hc * 64:(hc + 1) * 64],
                                    in_=ytok.h[:, k0:k0 + kstep, :]) for k0 in range(0, nk2, kstep)],
                                    "yfst", reads=[ytok], writes=[Wrap(r) for r in yf_res[sq]])
                            Sd.emit(2 + 3 * sq)
                    with ExitStack() as st:
                        Sd = Sched(nc, sems)
                        banks, ptrs = psum_banks(st, 7)
                        ptr = ptrs[0]
                        wout = Buf(sb(st, "wout", [128, KC, D], BF16))
                        Sd.dma("pool", [lambda h: h.dma_start(out=wout.h[:], in_=w_out.rearrange("(kc p) n -> p kc n", p=128))],
                               "set_wout", writes=[wout])
                        a1T = Buf(sb(st, "a1T", [128, KC], F32))
                        sh1T = Buf(sb(st, "sh1T", [128, KC], F32))
                        ga1b = Buf(sb(st, "ga1b", [128, D], F32))
                        load_modT(Sd, sh1T, sq, 0)
                        load_modT(Sd, a1T, sq, 1)
                        load_mod(Sd, ga1b, sq, 2)
                        for kc in range(KC):
                            Sd.add("pool" if kc % 2 else "dve", lambda h, kc=kc: h.tensor_tensor(out=wout.h[:, kc, :], in0=wout.h[:, kc, :], in1=ga1b.h[:], op=ALU.mult),
                                   reads=[wout, ga1b], writes=[wout])
                        xts = bufs(st, "xt", [128, D], F32, 8)
                        sss = bufs(st, "ss", [128, 4], F32, 2)
                        xnb = bufs(st, "xn", [128, D], BF16, 4)
                        hTs = bufs(st, "hT", [128, KC, 512], BF16, 1)
                        rcs = bufs(st, "rc", [64, 512], F32, 1)
                        rss = bufs(st, "rs", [64, 512], F32, 1)
                        sqb2 = Buf(sb(st, "sqb2", [128, 2, 512], BF16))
                        rq = Buf(sb(st, "rq", [128, 512], F32))
                        qlT = Buf(sb(st, "qlT", [128, 2, 512], BF16))
                        qn = bufs(st, "qn", [128, 512], BF16, 2)
                        qcs = bufs(st, "qc", [128, 4, 512], BF16, 2)
                        qps = bufs(st, "qp", [128, 4, 512], BF16, 2)
                        for qpb in qps:
                            Sd.add("pool", lambda h, qpb=qpb: h.memset(qpb.h[64:128, :, :], 0.0), writes=[qpb])
                        cq = Buf(sb(st, "cq", [64, 512], F32))
                        sqr = Buf(sb(st, "sqr", [64, 512], F32))
                        t1 = Buf(sb(st, "t1", [64, 512], F32))
                        t2 = Buf(sb(st, "t2", [64, 512], F32))
                        pts = bufs(st, "pt", [128, 512], BF16, 6)
                        rec = Buf(ga1b.h[:, 0:512])
                        rec.r = ga1b.r
                        accD = Buf(sb(st, "accD", [128, 512], F32))
                        accP = Buf(sb(st, "accP", [128, 512], F32))
                        acct = Buf(ga1b.h[:, 512:1024])
                        acct.r = ga1b.r
                        ones_f = Buf(sb(st, "ones_f", [128, 128], F32))
                        Sd.add("pool", lambda h: h.memset(ones_f.h[:], 1.0), writes=[ones_f])
                        onT = Buf(sb(st, "onT", [128, 4, 512], BF16))
                        ymT = Buf(sb(st, "ymT", [128, KC, 512], BF16))
                        yfts = bufs(st, "yft", [128, 512], BF16, 2)
                        ng = n_my // 512
                        bgb = [banks[3], banks[6]]
                        bgi = [0]

                        def bgbank():
                            bgi[0] += 1
                            return bgb[bgi[0] % 2]

                        def prep(g):
                            hT = hTs[0]
                            rc, rs = rcs[0], rss[0]
                            qc, qp = qcs[g % 2], qps[g % 2]
                            pos0 = g * 512
                            rcd, rsd = (rope_c, rope_s) if sq == 0 else (rope_cm, rope_sm)
                            Sd.dma("sp", [lambda h: h.dma_start(out=rc.h[:], in_=rcd[:, pos0:pos0 + 512]),
                                          lambda h: h.dma_start(out=rs.h[:], in_=rsd[:, pos0:pos0 + 512])], "ropec", writes=[rc, rs])
                            for t in range(4):
                                xt = xts[(4 * g + t) % 8]
                                r0 = g * 512 + t * 128
                                Sd.dma("sp", [lambda h, xt=xt, r0=r0: h.dma_start(out=xt.h[:], in_=x_my[r0:r0 + 128, :])],
                                       "xc%d" % ((4 * g + t) % 8), writes=[xt])
                                front_tile(Sd, xt, sss[(4 * g + t) % 2], xnb[t])
                                yield
                            for kp in range(KC // 2):
                                ptr = ptrs[0]
                                for k2 in range(2):
                                    kc = 2 * kp + k2
                                    for t in range(4):
                                        Sd.add("pe", lambda h, kc=kc, k2=k2, t=t, ptr=ptr: h.transpose(
                                            out=ptr.h[:, (k2 * 4 + t) * 128:(k2 * 4 + t + 1) * 128], in_=xnb[t].h[:, kc * 128:(kc + 1) * 128],
                                            identity=ident.h[:]), reads=[xnb[t], ident], writes=[ptr])
                                yield
                                for k2 in range(2):
                                    kc = 2 * kp + k2
                                    Sd.add("dve", lambda h, kc=kc, k2=k2, ptr=ptr: h.tensor_scalar(
                                        out=hT.h[:, kc, 0:512], in0=ptr.h[:, k2 * 512:(k2 + 1) * 512],
                                        scalar1=a1T.h[:, kc:kc + 1], scalar2=sh1T.h[:, kc:kc + 1], op0=ALU.mult, op1=ALU.add),
                                        reads=[ptr, a1T, sh1T], writes=[hT])
                                yield
                            for c2 in range(2):
                                bk = bgbank()
                                for kc in range(KC):
                                    Sd.add("pe", lambda h, kc=kc, c2=c2, bk=bk: h.matmul(bk.h[:], lhsT=win.h[:, kc, c2 * 128:(c2 + 1) * 128], rhs=hT.h[:, kc, :],
                                                                                      start=(kc == 0), stop=(kc == KC - 1)), reads=[win, hT], writes=[bk])
                                yield
                                Sd.add("act", lambda h, c2=c2, bk=bk: h.activation(out=sqb2.h[:, c2, :], in_=bk.h[:], func=AF.Square), reads=[bk], writes=[sqb2])
                                Sd.add("act", lambda h, c2=c2, bk=bk: h.activation(out=qlT.h[:, c2, :], in_=bk.h[:], func=AF.Copy), reads=[bk], writes=[qlT])
                                yield
                            yield
                            bk = bgbank()
                            for c2 in range(2):
                                Sd.add("pe", lambda h, c2=c2, bk=bk: h.matmul(bk.h[:], lhsT=ones_bf.h[:], rhs=sqb2.h[:, c2, :], start=(c2 == 0), stop=(c2 == 1)),
                                       reads=[ones_bf, sqb2], writes=[bk])
                            yield
                            Sd.add("act", lambda h, bk=bk: h.activation(out=rq.h[:], in_=bk.h[:], func=AF.Sqrt, scale=1.0 / 256, bias=epsb.h[:, 0:1]),
                                   reads=[bk, epsb], writes=[rq])
                            yield
                            Sd.add("dve", lambda h: h.reciprocal(out=rq.h[:], in_=rq.h[:]), reads=[rq], writes=[rq])
                            Sd.add("dve", lambda h: h.tensor_tensor(out=cq.h[:], in0=rc.h[:], in1=rq.h[0:64, :], op=ALU.mult), reads=[rc, rq], writes=[cq])
                            Sd.add("dve", lambda h: h.tensor_tensor(out=sqr.h[:], in0=rs.h[:], in1=rq.h[0:64, :], op=ALU.mult), reads=[rs, rq], writes=[sqr])
                            yield
                            yield
                            for hd in range(4):
                                qnb = qn[hd % 2]
                                bn = bgbank()
                                for kc in range(2):
                                    Sd.add("pe", lambda h, kc=kc, hd=hd, bn=bn: h.matmul(bn.h[:], lhsT=wuq.h[:, kc, hd * 192:hd * 192 + 128], rhs=qlT.h[:, kc, :],
                                                                                        start=(kc == 0), stop=(kc == 1)), reads=[wuq, qlT], writes=[bn])
                                yield
                                Sd.add("dve", lambda h, qnb=qnb, bn=bn: h.tensor_tensor(out=qnb.h[:], in0=bn.h[:], in1=rq.h[:], op=ALU.mult), reads=[bn, rq], writes=[qnb])
                                yield
                                yield
                                bc_ = bgbank()
                                Sd.add("pe", lambda h, hd=hd, qnb=qnb, bc_=bc_: h.matmul(bc_.h[:], lhsT=wukT.h[:, hd, :], rhs=qnb.h[:], start=True, stop=True),
                                       reads=[wukT, qnb], writes=[bc_])
                                yield
                                Sd.add("act", lambda h, hd=hd, bc_=bc_: h.activation(out=qc.h[:, hd, :], in_=bc_.h[:], func=AF.Copy), reads=[bc_], writes=[qc])
                                yield
                                bp = bgbank()
                                for kc in range(2):
                                    Sd.add("pe", lambda h, kc=kc, hd=hd, bp=bp: h.matmul(bp.h[0:64, :], lhsT=wuq.h[:, kc, hd * 192 + 128:hd * 192 + 192], rhs=qlT.h[:, kc, :],
                                                                                        start=(kc == 0), stop=(kc == 1)), reads=[wuq, qlT], writes=[bp])
                                yield
                                Sd.add("dve", lambda h, bp=bp: h.tensor_tensor(out=t1.h[:], in0=bp.h[0:64, :], in1=cq.h[:], op=ALU.mult), reads=[bp, cq], writes=[t1])
                                yield
                                br = bgbank()
                                for kc in range(2):
                                    Sd.add("pe", lambda h, kc=kc, hd=hd, br=br: h.matmul(br.h[0:64, :], lhsT=wuq.h[:, kc, 768 + hd * 64:768 + hd * 64 + 64], rhs=qlT.h[:, kc, :],
                                                                                        start=(kc == 0), stop=(kc == 1)), reads=[wuq, qlT], writes=[br])
                                yield
                                Sd.add("dve", lambda h, br=br: h.tensor_tensor(out=t2.h[:], in0=br.h[0:64, :], in1=sqr.h[:], op=ALU.mult), reads=[br, sqr], writes=[t2])
                                yield
                                Sd.add("pool", lambda h, hd=hd, qp=qp: h.tensor_tensor(out=qp.h[0:64, hd, :], in0=t1.h[:], in1=t2.h[:], op=ALU.add), reads=[t1, t2], writes=[qp])
                                yield

                        def post(g):
                            gx = [xts[(4 * g + t) % 8] for t in range(4)]
                            for t in range(4):
                                yft = yfts[t % 2]
                                r0 = g * 512 + t * 128
                                Sd.dma("sp", [lambda h, yft=yft, r0=r0: h.dma_start(out=yft.h[:], in_=yf[sq][r0:r0 + 128, :])], "yft%d" % (t % 2),
                                       reads=[Wrap(yf_res[sq][r0 // 128])], writes=[yft])
                                yield
                                ptr = ptrs[0]
                                for cc in range(4):
                                    Sd.add("pe", lambda h, cc=cc, yft=yft, ptr=ptr: h.transpose(out=ptr.h[:, cc * 128:(cc + 1) * 128], in_=yft.h[:, cc * 128:(cc + 1) * 128],
                                                                                              identity=ident.h[:]), reads=[yft, ident], writes=[ptr])
                                yield
                                Sd.add("act", lambda h, t=t, ptr=ptr: h.activation(out=ymT.h[:, 4:8, t * 128:(t + 1) * 128],
                                                                                   in_=ptr.h[:, 0:512].rearrange("p (k c) -> p k c", k=4), func=AF.Copy),
                                       reads=[ptr], writes=[ymT])
                                yield
                            yield
                            for t in range(4):
                                for n2 in range(2):
                                    bk = bgbank()
                                    for kc in range(KC):
                                        Sd.add("pe", lambda h, kc=kc, t=t, n2=n2, bk=bk: h.matmul(bk.h[:], lhsT=ymT.h[:, kc, t * 128:(t + 1) * 128],
                                                                                                rhs=wout.h[:, kc, n2 * 512:(n2 + 1) * 512],
                                                                                                start=(kc == 0), stop=(kc == KC - 1)), reads=[ymT, wout], writes=[bk])
                                    yield
                                    Sd.add("dve", lambda h, n2=n2, bk=bk, xg=gx[t]: h.tensor_tensor(out=xg.h[:, n2 * 512:(n2 + 1) * 512], in0=bk.h[:],
                                                                                               in1=xg.h[:, n2 * 512:(n2 + 1) * 512], op=ALU.add),
                                           reads=[bk, gx[t]], writes=[gx[t]])
                                    yield
                                r0 = x1_off + g * 512 + t * 128
                                Sd.dma("pool", [lambda h, xg=gx[t], r0=r0: h.dma_start(out=x1scr[r0:r0 + 128, :], in_=xg.h[:])], "x1st%d" % ((4 * g + t) % 8),
                                       reads=[gx[t]], writes=[Wrap(x1_res[r0 // 128])])

                        def finish_head(hd):
                            bo, bm = banks[4 + hd % 2], bgbank()
                            Sd.add("pe", lambda h, bm=bm: h.matmul(bm.h[:], lhsT=ones_f.h[:], rhs=acct.h[:], start=True, stop=True),
                                   reads=[ones_f, acct], writes=[bm])
                            Sd.add("dve", lambda h, bm=bm: h.reciprocal(out=rec.h[:], in_=bm.h[:]), reads=[bm], writes=[rec])
                            Sd.add("dve", lambda h, hd=hd, bo=bo: h.tensor_tensor(out=onT.h[:, hd, :], in0=bo.h[:], in1=rec.h[:], op=ALU.mult),
                                   reads=[bo, rec], writes=[onT])

                        def vup_head(hd):
                            bk = bgbank()
                            Sd.add("pe", lambda h, hd=hd, bk=bk: h.matmul(bk.h[:], lhsT=wukv.h[:, hd * 256 + 128:hd * 256 + 256], rhs=onT.h[:, hd, :], start=True, stop=True),
                                   reads=[wukv, onT], writes=[bk])
                            Sd.add("act", lambda h, hd=hd, bk=bk: h.activation(out=ymT.h[:, hd, :], in_=bk.h[:], func=AF.Copy), reads=[bk], writes=[ymT])

                        for _ in prep(0):
                            pass
                        blocks = [(g, hd, j) for g in range(ng) for hd in range(4) for j in range(NT)]
                        LAG = 3
                        NS = 3
                        pending = []
                        bg = []
                        for i in range(len(blocks) + LAG + 24):
                            if i < len(blocks):
                                g, hd, j = blocks[i]
                                if hd == 0 and j == 0:
                                    for kind, gg, gen in bg:
                                        for _ in gen:
                                            pass
                                    bg = []
                                    if g == 0 and ng > 1:
                                        bg.append(("prep", 1, prep(1)))
                                qc, qp = qcs[g % 2], qps[g % 2]
                                bs_ = banks[i % NS]
                                pt = pts[i % 6]
                                Sd.add("pe", lambda h, j=j, hd=hd, bs_=bs_, qc=qc: h.matmul(bs_.h[:], lhsT=kcT.h[:, j * 128:(j + 1) * 128], rhs=qc.h[:, hd, :],
                                                                                    start=True, stop=False), reads=[kcT, qc], writes=[bs_])
                                Sd.add("pe", lambda h, j=j, hd=hd, bs_=bs_, qp=qp: h.matmul(bs_.h[:], lhsT=kpeT.h[:, j * 128:(j + 1) * 128], rhs=qp.h[:, hd, :],
                                                                                    start=False, stop=True), reads=[kpeT, qp], writes=[bs_])
                                Sd.add("act", lambda h, bs_=bs_, pt=pt: h.activation(out=pt.h[:], in_=bs_.h[:], func=AF.Exp, scale=SCALE), reads=[bs_], writes=[pt])
                                eng, acc = ("dve", accD) if j % 2 == 0 else ("pool", accP)
                                if j < 2:
                                    Sd.add(eng, lambda h, pt=pt, acc=acc: h.tensor_copy(out=acc.h[:], in_=pt.h[:]), reads=[pt], writes=[acc])
                                else:
                                    Sd.add(eng, lambda h, pt=pt, acc=acc: h.tensor_tensor(out=acc.h[:], in0=acc.h[:], in1=pt.h[:], op=ALU.add),
                                           reads=[pt, acc], writes=[acc])
                                if j == NT - 1:
                                    Sd.add("pool", lambda h: h.tensor_tensor(out=acct.h[:], in0=accD.h[:], in1=accP.h[:], op=ALU.add),
                                           reads=[accD, accP], writes=[acct])
                            if LAG <= i < len(blocks) + LAG:
                                g2, hd2, j2 = blocks[i - LAG]
                                bo = banks[4 + hd2 % 2]
                                pt2 = pts[(i - LAG) % 6]
                                Sd.add("pe", lambda h, j2=j2, pt2=pt2, bo=bo: h.matmul(bo.h[:], lhsT=vc.h[:, j2, :], rhs=pt2.h[:], start=(j2 == 0), stop=(j2 == NT - 1)),
                                       reads=[vc, pt2], writes=[bo])
                                if j2 == NT - 1:
                                    pending.append([i + 4, hd2, 0, g2])
                            if pending and pending[0][0] <= i:
                                if pending[0][2] == 0:
                                    finish_head(pending[0][1])
                                    pending[0][2] = 1
                                    pending[0][0] = i + 12
                                else:
                                    pg = pending[0][3]
                                    while bg and bg[0][0] == "post" and bg[0][1] < pg:
                                        for _ in bg[0][2]:
                                            pass
                                        bg.pop(0)
                                    ph = pending.pop(0)[1]
                                    vup_head(ph)
                                    if ph == 3:
                                        bg.append(("post", pg, post(pg)))
                                        if pg + 2 < ng:
                                            bg.append(("prep", pg + 2, prep(pg + 2)))
                            elif bg and i % 2 == 1:
                                try:
                                    next(bg[0][2])
                                except StopIteration:
                                    bg.pop(0)
                        assert not pending
                        for kind, gg, gen in bg:
                            for _ in gen:
                                pass
                        Sd.emit(3 + 3 * sq)

        with ExitStack() as st:
            Sd = Sched(nc, sems)
            banks, ptrs = psum_banks(st)
            wg = Buf(sb(st, "wg", [128, KC, DFF], BF16))
            wu = Buf(sb(st, "wu", [128, KC, DFF], BF16))
            wd = Buf(sb(st, "wd", [128, FC, D], BF16))
            for kc in range(KC):
                Sd.dma("pool", [lambda h, kc=kc: h.dma_start(out=wg.h[:, kc, :], in_=w_gate[kc * 128:(kc + 1) * 128, :]),
                                lambda h, kc=kc: h.dma_start(out=wu.h[:, kc, :], in_=w_up[kc * 128:(kc + 1) * 128, :])], "set_wgu", writes=[wg, wu])
            for f0 in range(0, FC, 2):
                Sd.dma("pool", [lambda h, f0=f0: h.dma_start(out=wd.h[:, f0:f0 + 2, :], in_=w_down[f0 * 128:(f0 + 2) * 128, :].rearrange("(f p) n -> p f n", p=128))],
                       "set_wd", writes=[wd])
            gfb = Buf(sb(st, "gfb", [128, D], F32))
            Sd.dma("sp", [lambda h: h.dma_start(out=gfb.h[:], in_=g_final.partition_broadcast(128))], "set_gf", writes=[gfb])
            a2T = Buf(sb(st, "a2T", [128, KC], F32))
            sh2T = Buf(sb(st, "sh2T", [128, KC], F32))
            ga2b = Buf(sb(st, "ga2b", [128, D], F32))
            xts = bufs(st, "xt", [128, D], F32, 4)
            sss = bufs(st, "ss", [128, 4], F32, 2)
            tmps = bufs(st, "tmp", [128, D], F32, 2)
            hbs = bufs(st, "hb", [128, D], BF16, 4)
            hTs = bufs(st, "hT", [128, KC, 256], BF16, 2)
            sgs = bufs(st, "sg", [128, 256], F32, 2)
            actT = Buf(sb(st, "actT", [128, FC, 256], BF16))
            mts = bufs(st, "mt", [128, 512], F32, 2)
            yts = bufs(st, "yt", [128, D], F32, 2)
            ots = tmps
            ss3 = bufs(st, "ss3", [128, 4], F32, 2)
            ngrp = (S + QS) // 256

            def pd_tiles(g):
                for t in range(2):
                    xt = xts[(2 * g + t) % 4]
                    r0 = g * 256 + t * 128
                    Sd.dma("sp", [lambda h, xt=xt, r0=r0: h.dma_start(out=xt.h[:], in_=x1scr[r0:r0 + 128, :])], "xd%d" % ((2 * g + t) % 4),
                           reads=[Wrap(x1_res[r0 // 128])], writes=[xt])
                    front_tile(Sd, xt, sss[(2 * g + t) % 2], hbs[(g % 2) * 2 + t])

            for g in range(ngrp):
                r_base = g * 256
                sq = 0 if r_base < S else 1
                if g == 0 or r_base == S:
                    load_modT(Sd, sh2T, sq, 3)
                    load_modT(Sd, a2T, sq, 4)
                    load_mod(Sd, ga2b, sq, 5)
                hT = hTs[g % 2]
                gx = [xts[(2 * g + t) % 4] for t in range(2)]
                if g == 0:
                    pd_tiles(0)
                front_group(Sd, hbs[(g % 2) * 2:(g % 2) * 2 + 2], a2T, sh2T, ptrs, hT, 2)
                if g + 1 < ngrp:
                    pd_tiles(g + 1)
                for fc in range(FC):
                    bk = banks[fc % 3]
                    sg = sgs[fc % 2]
                    for kc in range(KC):
                        Sd.add("pe", lambda h, kc=kc, fc=fc, bk=bk, hT=hT: h.matmul(bk.h[:, 0:256], lhsT=wg.h[:, kc, fc * 128:(fc + 1) * 128], rhs=hT.h[:, kc, :],
                                                                            start=(kc == 0), stop=(kc == KC - 1)), reads=[wg, hT], writes=[bk])
                    for kc in range(KC):
                        Sd.add("pe", lambda h, kc=kc, fc=fc, bk=bk, hT=hT: h.matmul(bk.h[:, 256:512], lhsT=wu.h[:, kc, fc * 128:(fc + 1) * 128], rhs=hT.h[:, kc, :],
                                                                            start=(kc == 0), stop=(kc == KC - 1)), reads=[wu, hT], writes=[bk])
                    Sd.add("act", lambda h, bk=bk, sg=sg: h.activation(out=sg.h[:], in_=bk.h[:, 0:256], func=AF.Silu), reads=[bk], writes=[sg])
                    Sd.add("dve", lambda h, fc=fc, bk=bk, sg=sg: h.tensor_tensor(out=actT.h[:, fc, :], in0=bk.h[:, 256:512], in1=sg.h[:], op=ALU.mult),
                           reads=[bk, sg], writes=[actT])
                for t in range(2):
                    yt = yts[t]
                    ot = ots[t]
                    s3 = ss3[t]
                    for n2 in range(2):
                        bk = banks[3 + (2 * t + n2) % 3]
                        mt = mts[n2]
                        for fc in range(FC):
                            Sd.add("pe", lambda h, fc=fc, t=t, n2=n2, bk=bk: h.matmul(bk.h[:], lhsT=actT.h[:, fc, t * 128:(t + 1) * 128],
                                                                                    rhs=wd.h[:, fc, n2 * 512:(n2 + 1) * 512],
                                                                                    start=(fc == 0), stop=(fc == FC - 1)), reads=[actT, wd], writes=[bk])
                        Sd.add("dve", lambda h, n2=n2, bk=bk, mt=mt: h.tensor_tensor(out=mt.h[:], in0=bk.h[:], in1=ga2b.h[:, n2 * 512:(n2 + 1) * 512], op=ALU.mult),
                               reads=[bk, ga2b], writes=[mt])
                        Sd.add("pool", lambda h, n2=n2, mt=mt, xg=gx[t], yt=yt: h.tensor_tensor(out=yt.h[:, n2 * 512:(n2 + 1) * 512], in0=mt.h[:],
                                                                                       in1=xg.h[:, n2 * 512:(n2 + 1) * 512], op=ALU.add),
                               reads=[mt, gx[t]], writes=[yt])
                    Sd.add("act", lambda h, yt=yt, ot=ot, s3=s3: h.activation(out=ot.h[:], in_=yt.h[:], func=AF.Square, accum_out=s3.h[:, 0:1]),
                           reads=[yt], writes=[ot, s3])
                    Sd.add("act", lambda h, s3=s3: h.activation(out=s3.h[:, 1:2], in_=s3.h[:, 0:1], func=AF.Sqrt, scale=1.0 / D, bias=epsb.h[:, 0:1]),
                           reads=[s3, epsb], writes=[s3])
                    Sd.add("dve", lambda h, s3=s3: h.reciprocal(out=s3.h[:, 2:3], in_=s3.h[:, 1:2]), reads=[s3], writes=[s3])
                    Sd.add("act", lambda h, yt=yt, ot=ot, s3=s3: h.activation(out=ot.h[:], in_=yt.h[:], func=AF.Identity, scale=s3.h[:, 2:3]),
                           reads=[yt, s3], writes=[ot])
                    Sd.add("pool", lambda h, ot=ot: h.tensor_tensor(out=ot.h[:], in0=ot.h[:], in1=gfb.h[:], op=ALU.mult), reads=[ot, gfb], writes=[ot])
                    r0 = r_base + t * 128
                    if r0 < S:
                        dst = y_p[r0:r0 + 128, :]
                    else:
                        dst = y_s[r0 - S:r0 - S + 128, :]
                    Sd.dma("pool", [lambda h, ot=ot, dst=dst: h.dma_start(out=dst, in_=ot.h[:])], "yst%d" % t, reads=[ot])
            Sd.emit(7)
    return nc


def _tables(cfg):
    S, N2, PC = cfg.S, cfg.N2, cfg.PC
    f64 = np.float64
    pos = np.arange(S, dtype=np.float32)
    inv = (1.0 / (np.float32(10000.0) ** (np.arange(0, 64, 2, dtype=np.float32) / np.float32(64)))).astype(np.float32)
    ang = (pos[None, :] * inv[:, None]).astype(np.float32)
    rope_c = np.concatenate([np.cos(ang), np.cos(ang)], 0).astype(np.float32)
    rope_s = np.concatenate([np.sin(ang), np.sin(ang)], 0).astype(np.float32)
    norm = 1.0 / math.sqrt(S * 64)
    m = np.arange(64, dtype=f64)
    C64 = np.cos(2 * np.pi * np.outer(m, m) / 64) * norm
    S64 = np.sin(2 * np.pi * np.outer(m, m) / 64) * norm
    cb = np.zeros((128, 128), f64)
    sbm = np.zeros((128, 128), f64)
    for gl in range(2):
        cb[gl * 64:(gl + 1) * 64, gl * 64:(gl + 1) * 64] = C64
        sbm[gl * 64:(gl + 1) * 64, gl * 64:(gl + 1) * 64] = -S64
    s1 = np.arange(128, dtype=f64)
    C1 = np.cos(2 * np.pi * np.outer(s1, s1) / 128)
    S1 = np.sin(2 * np.pi * np.outer(s1, s1) / 128)
    t1u = np.concatenate([C1, -S1], 1)
    t1v = np.concatenate([S1, C1], 1)
    s2 = np.arange(N2, dtype=f64)
    Tc = np.cos(2 * np.pi * np.outer(s2, s1) / S)
    Ts = np.sin(2 * np.pi * np.outer(s2, s1) / S)
    Tcp = np.tile(Tc, (PC, 1))
    Tsp = np.tile(Ts, (PC, 1))
    ta = np.tile(Tcp, (1, 4))
    C2 = np.cos(2 * np.pi * np.outer(s2, s2) / N2)
    S2 = np.sin(2 * np.pi * np.outer(s2, s2) / N2)

    def blk(M, k2s):
        n = len(k2s)
        out = np.zeros((128, PC * n), f64)
        for c in range(PC):
            out[c * N2:(c + 1) * N2, c * n:(c + 1) * n] = M[:, k2s]
        return out

    full = np.arange(N2)
    tb = dict(rope_c=rope_c, rope_s=rope_s, cb=cb, sb=sbm, t1u=t1u, t1v=t1v, ta=ta, ts=Tsp,
              bc_p=blk(C2, full), bs_p=blk(S2, full))
    tb["ts"] = np.tile(Tsp, (1, 2))
    tb = {k: np.ascontiguousarray(v, dtype=np.float32) for k, v in tb.items()}
    bq = []
    for r in range(4):
        k2s = np.arange(cfg.NQ2 * r, cfg.NQ2 * (r + 1))
        bq.append((np.ascontiguousarray(blk(C2, k2s), dtype=np.float32), np.ascontiguousarray(blk(S2, k2s), dtype=np.float32)))
    return tb, bq


_CACHE = {}


def _run(cfg, inputs):
    S, QS = cfg.S, cfg.QS
    f = lambda a: np.ascontiguousarray(np.asarray(a), dtype=np.float32)
    x_prompt, x_sample = f(inputs["x_prompt"]), f(inputs["x_sample"])
    c_prompt, c_sample = f(inputs["c_prompt"]), f(inputs["c_sample"])
    tb, bq = _tables(cfg)
    shared = dict(
        w_ada=f(inputs["w_ada"][0]), b_ada=f(inputs["b_ada"][0]).reshape(1, -1), g_mix=f(inputs["g_mix"][0]).reshape(1, -1),
        g_ffn=f(inputs["g_ffn"][0]).reshape(1, -1), g_final=f(inputs["g_final"]).reshape(1, -1), w_in=f(inputs["w_in"][0]),
        gqT=f(f(inputs["g_q_lat"][0]).reshape(2, 128).T), w_uq=f(inputs["w_uq"][0]),
        gkvT=f(f(inputs["g_kv_lat"][0]).reshape(128, 1)), w_ukv=f(inputs["w_ukv"][0]), w_four=f(inputs["w_four"][0]),
        w_out=f(inputs["w_out"][0]), w_gate=f(inputs["w_gate"][0]), w_up=f(inputs["w_up"][0]), w_down=f(inputs["w_down"][0]),
    )
    shared.update(tb)
    in_maps = []
    for i in range(N_CORES):
        sidx, r = i // 4, i % 4
        cc = np.stack([c_prompt[i], c_sample[sidx]], 0)
        cT = f(cc.reshape(2, KC, 128).transpose(2, 1, 0))
        m = dict(shared)
        m.update(xs_p=x_prompt[i], xs_s=x_sample[sidx], xm_s=f(x_sample[sidx, r * QS:(r + 1) * QS]), cT=cT,
                 bc_s=bq[r][0], bs_s=bq[r][1],
                 rope_cm=f(tb["rope_c"][:, r * QS:(r + 1) * QS]), rope_sm=f(tb["rope_s"][:, r * QS:(r + 1) * QS]))
        in_maps.append(m)
    nc = build_program(cfg)
    res = run_bass_kernel_spmd(nc, in_maps, core_ids=list(range(N_CORES)))
    y_prompt = np.stack([np.asarray(res.results[i]["y_p"], dtype=np.float32) for i in range(N_CORES)], 0)
    y_sample = np.zeros((2, S, D), np.float32)
    for i in range(N_CORES):
        y_sample[i // 4, (i % 4) * QS:(i % 4 + 1) * QS] = np.asarray(res.results[i]["y_s"], dtype=np.float32)
    return y_prompt, y_sample


def kernel(**inputs):
    S = int(np.asarray(inputs["x_prompt"]).shape[1])
    return _run(Cfg(S), inputs)
```
